# Optimizing a Trainium2 kernel written in Bass

```python
import jax, jax.numpy as jnp
from jax import lax
import numpy as np

D_MODEL = 1024
BATCH = 16
SEQ = 4096
DEPTH = 4
DEC_BATCH = 16
DEC_SEQ = 64
PAST_LEN = 1024

CHUNK = 64
N_HEADS = 8
QK_NOPE_DIM = 128
QK_ROPE_DIM = 64
V_HEAD_DIM = 128
Q_LORA_RANK = 384
KV_LORA_RANK = 256
ROPE_BASE = 10000.0
D_CONV = D_MODEL
CONV_WIDTH = 3
D_FF = 2816
Q_BLOCK = 128
NORM_EPS = 1e-5
ATTN_SCALE = (QK_NOPE_DIM + QK_ROPE_DIM) ** -0.5
DEEPNORM_ALPHA = (2 * DEPTH) ** 0.25
DEEPNORM_BETA = (8 * DEPTH) ** -0.25
IN_COLS = Q_LORA_RANK + KV_LORA_RANK + QK_ROPE_DIM + 3 * D_CONV + 2 * D_MODEL

kernel_name = "hybrid_shortconv_mla_streaming_step"


def layer_norm(x, g, b):
    xf = x.astype(jnp.float32)
    mu = jnp.mean(xf, axis=-1, keepdims=True)
    var = jnp.mean(jnp.square(xf - mu), axis=-1, keepdims=True)
    return ((xf - mu) * lax.rsqrt(var + NORM_EPS) * g.astype(jnp.float32) + b.astype(jnp.float32)).astype(x.dtype)


def rms_norm(x, g):
    xf = x.astype(jnp.float32)
    ms = jnp.mean(jnp.square(xf), axis=-1, keepdims=True)
    return (xf * lax.rsqrt(ms + NORM_EPS) * g.astype(jnp.float32)).astype(x.dtype)


def swiglu_ffn(x, w_gate_up, w_down):
    gate, up = jnp.split(x @ w_gate_up, 2, axis=-1)
    return (jax.nn.silu(gate) * up) @ w_down


def apply_rope(x, pos):
    half = QK_ROPE_DIM // 2
    inv = ROPE_BASE ** (-jnp.arange(half, dtype=jnp.float32) / half)
    ang = pos.astype(jnp.float32)[:, None] * inv[None, :]
    shape = (1, ang.shape[0]) + (1,) * (x.ndim - 3) + (half,)
    cos = jnp.cos(ang).reshape(shape)
    sin = jnp.sin(ang).reshape(shape)
    xf = x.astype(jnp.float32)
    x1, x2 = xf[..., :half], xf[..., half:]
    return jnp.concatenate([x1 * cos - x2 * sin, x1 * sin + x2 * cos], axis=-1).astype(x.dtype)


def mla_attend(q_nope, q_rope, q_pos, k_nope, k_rope, v, k_pos):
    s = (jnp.einsum("bqhn,bthn->bhqt", q_nope, k_nope).astype(jnp.float32)
         + jnp.einsum("bqhr,btr->bhqt", q_rope, k_rope).astype(jnp.float32)) * ATTN_SCALE
    mask = (k_pos[None, :] // CHUNK) <= (q_pos[:, None] // CHUNK)
    s = jnp.where(mask[None, None], s, -1e30)
    p = jax.nn.softmax(s, axis=-1).astype(v.dtype)
    return jnp.einsum("bhqt,bthv->bqhv", p, v)


def mixer(x, conv_buf, past_lat, past_kr, w_in, b_gate, q_norm_gain, kv_norm_gain,
          w_uq, w_ukv, w_mla_out, conv_w, w_conv_out, w_mix_out):
    bsz, t_new, _ = x.shape
    pos0 = past_lat.shape[1]
    split_at = np.cumsum([Q_LORA_RANK, KV_LORA_RANK, QK_ROPE_DIM, D_CONV, D_CONV, D_CONV, D_MODEL]).tolist()
    q_lat, kv_lat, k_r, conv_b, conv_c, conv_u, g_conv, g_mla = jnp.split(x @ w_in, split_at, axis=-1)

    u = conv_c * conv_u
    u_pad = jnp.concatenate([conv_buf.astype(u.dtype), u], axis=1)
    conv_out = conv_w[CONV_WIDTH - 1] * u_pad[:, CONV_WIDTH - 1:]
    for k in range(CONV_WIDTH - 1):
        conv_out = conv_out + conv_w[k] * u_pad[:, k:k + t_new]
    y_conv = (conv_b * conv_out) @ w_conv_out
    new_conv_buf = u_pad[:, -(CONV_WIDTH - 1):]

    q_pos = pos0 + jnp.arange(t_new)
    c_kv = rms_norm(kv_lat, kv_norm_gain)
    k_rope_new = apply_rope(k_r, q_pos)
    lat_all = jnp.concatenate([past_lat.astype(c_kv.dtype), c_kv], axis=1)
    kr_all = jnp.concatenate([past_kr.astype(k_rope_new.dtype), k_rope_new], axis=1)
    t_k = lat_all.shape[1]
    k_pos = jnp.arange(t_k)
    kv = (lat_all @ w_ukv).reshape(bsz, t_k, N_HEADS, QK_NOPE_DIM + V_HEAD_DIM)
    k_nope, v = kv[..., :QK_NOPE_DIM], kv[..., QK_NOPE_DIM:]
    q = (rms_norm(q_lat, q_norm_gain) @ w_uq).reshape(bsz, t_new, N_HEADS, QK_NOPE_DIM + QK_ROPE_DIM)
    q_nope = q[..., :QK_NOPE_DIM]
    q_rope = apply_rope(q[..., QK_NOPE_DIM:], q_pos)
    if t_new > Q_BLOCK and t_new % Q_BLOCK == 0:
        n_blk = t_new // Q_BLOCK
        qn_b = q_nope.reshape(bsz, n_blk, Q_BLOCK, N_HEADS, QK_NOPE_DIM).transpose(1, 0, 2, 3, 4)
        qr_b = q_rope.reshape(bsz, n_blk, Q_BLOCK, N_HEADS, QK_ROPE_DIM).transpose(1, 0, 2, 3, 4)
        qp_b = q_pos.reshape(n_blk, Q_BLOCK)
        o = lax.map(lambda a: mla_attend(a[0], a[1], a[2], k_nope, kr_all, v, k_pos), (qn_b, qr_b, qp_b))
        o = o.transpose(1, 0, 2, 3, 4).reshape(bsz, t_new, N_HEADS, V_HEAD_DIM)
    else:
        o = mla_attend(q_nope, q_rope, q_pos, k_nope, kr_all, v, k_pos)
    y_mla = o.reshape(bsz, t_new, N_HEADS * V_HEAD_DIM) @ w_mla_out

    merged = jax.nn.sigmoid(g_conv + b_gate[0]) * y_conv + jax.nn.sigmoid(g_mla + b_gate[1]) * y_mla
    return merged @ w_mix_out, c_kv, k_rope_new, new_conv_buf


def run_trunk(x, cache_lat, cache_kr, conv_state, ffn1_w_gate_up, ffn1_w_down, ffn2_w_gate_up, ffn2_w_down,
              ln_gain, ln_bias, w_in, b_gate, q_norm_gain, kv_norm_gain, w_uq, w_ukv, w_mla_out,
              conv_w, w_conv_out, w_mix_out):
    bsz = x.shape[0]
    new_lat, new_kr, new_conv = [], [], []
    for l in range(DEPTH):
        if cache_lat is None:
            past_lat = jnp.zeros((bsz, 0, KV_LORA_RANK), x.dtype)
            past_kr = jnp.zeros((bsz, 0, QK_ROPE_DIM), x.dtype)
            buf = jnp.zeros((bsz, CONV_WIDTH - 1, D_CONV), x.dtype)
        else:
            past_lat, past_kr, buf = cache_lat[l], cache_kr[l], conv_state[l]
        x = layer_norm(DEEPNORM_ALPHA * x + 0.5 * swiglu_ffn(x, ffn1_w_gate_up[l], ffn1_w_down[l]),
                       ln_gain[l, 0], ln_bias[l, 0])
        m, c_kv, kr, cb = mixer(x, buf, past_lat, past_kr, w_in[l], b_gate[l], q_norm_gain[l], kv_norm_gain[l],
                                w_uq[l], w_ukv[l], w_mla_out[l], conv_w[l], w_conv_out[l], w_mix_out[l])
        x = layer_norm(DEEPNORM_ALPHA * x + m, ln_gain[l, 1], ln_bias[l, 1])
        x = layer_norm(DEEPNORM_ALPHA * x + 0.5 * swiglu_ffn(x, ffn2_w_gate_up[l], ffn2_w_down[l]),
                       ln_gain[l, 2], ln_bias[l, 2])
        new_lat.append(c_kv)
        new_kr.append(kr)
        new_conv.append(cb)
    return x, jnp.stack(new_lat), jnp.stack(new_kr), jnp.stack(new_conv)


def setup_inputs(seed: int = 0) -> dict:
    key = jax.random.key(seed)
    ks = jax.random.split(key, 24)

    def nrm(k, shape, scale):
        return jax.random.normal(k, shape, jnp.float32) * scale

    hq = N_HEADS * (QK_NOPE_DIM + QK_ROPE_DIM)
    hkv = N_HEADS * (QK_NOPE_DIM + V_HEAD_DIM)
    hv = N_HEADS * V_HEAD_DIM
    return {
        "x_prompt": nrm(ks[0], (BATCH, SEQ, D_MODEL), 1.0),
        "x_sample": nrm(ks[1], (DEC_BATCH, DEC_SEQ, D_MODEL), 1.0),
        "cache_kv_latent": nrm(ks[2], (DEPTH, DEC_BATCH, PAST_LEN, KV_LORA_RANK), 1.0),
        "cache_k_rope": nrm(ks[3], (DEPTH, DEC_BATCH, PAST_LEN, QK_ROPE_DIM), 1.0),
        "state_conv": nrm(ks[4], (DEPTH, DEC_BATCH, CONV_WIDTH - 1, D_CONV), 1.0),
        "ffn1_w_gate_up": nrm(ks[5], (DEPTH, D_MODEL, 2 * D_FF), D_MODEL ** -0.5),
        "ffn1_w_down": nrm(ks[6], (DEPTH, D_FF, D_MODEL), DEEPNORM_BETA * D_FF ** -0.5),
        "ffn2_w_gate_up": nrm(ks[7], (DEPTH, D_MODEL, 2 * D_FF), D_MODEL ** -0.5),
        "ffn2_w_down": nrm(ks[8], (DEPTH, D_FF, D_MODEL), DEEPNORM_BETA * D_FF ** -0.5),
        "ln_gain": 1.0 + nrm(ks[9], (DEPTH, 3, D_MODEL), 0.01),
        "ln_bias": nrm(ks[10], (DEPTH, 3, D_MODEL), 0.01),
        "w_in": nrm(ks[11], (DEPTH, D_MODEL, IN_COLS), D_MODEL ** -0.5),
        "b_gate": nrm(ks[12], (DEPTH, 2, D_MODEL), 0.01),
        "q_norm_gain": 1.0 + nrm(ks[13], (DEPTH, Q_LORA_RANK), 0.01),
        "kv_norm_gain": 1.0 + nrm(ks[14], (DEPTH, KV_LORA_RANK), 0.01),
        "w_uq": nrm(ks[15], (DEPTH, Q_LORA_RANK, hq), Q_LORA_RANK ** -0.5),
        "w_ukv": nrm(ks[16], (DEPTH, KV_LORA_RANK, hkv), KV_LORA_RANK ** -0.5),
        "w_mla_out": nrm(ks[17], (DEPTH, hv, D_MODEL), hv ** -0.5),
        "conv_w": nrm(ks[18], (DEPTH, CONV_WIDTH, D_CONV), CONV_WIDTH ** -0.5),
        "w_conv_out": nrm(ks[19], (DEPTH, D_CONV, D_MODEL), D_CONV ** -0.5),
        "w_mix_out": nrm(ks[20], (DEPTH, D_MODEL, D_MODEL), DEEPNORM_BETA * D_MODEL ** -0.5),
    }


def reference(x_prompt, x_sample, cache_kv_latent, cache_k_rope, state_conv,
              ffn1_w_gate_up, ffn1_w_down, ffn2_w_gate_up, ffn2_w_down, ln_gain, ln_bias,
              w_in, b_gate, q_norm_gain, kv_norm_gain, w_uq, w_ukv, w_mla_out,
              conv_w, w_conv_out, w_mix_out):
    y_prompt, p_lat, p_kr, p_conv = run_trunk(
        x_prompt, None, None, None, ffn1_w_gate_up, ffn1_w_down, ffn2_w_gate_up, ffn2_w_down,
        ln_gain, ln_bias, w_in, b_gate, q_norm_gain, kv_norm_gain, w_uq, w_ukv, w_mla_out,
        conv_w, w_conv_out, w_mix_out)
    y_sample, s_lat, s_kr, s_conv = run_trunk(
        x_sample, cache_kv_latent, cache_k_rope, state_conv, ffn1_w_gate_up, ffn1_w_down,
        ffn2_w_gate_up, ffn2_w_down, ln_gain, ln_bias, w_in, b_gate, q_norm_gain, kv_norm_gain,
        w_uq, w_ukv, w_mla_out, conv_w, w_conv_out, w_mix_out)
    return (y_prompt, y_sample, p_lat, p_kr, p_conv, s_lat, s_kr, s_conv)
```

```python
import numpy as np
from contextlib import ExitStack
import concourse.bass as bass
import concourse.mybir as mybir
from concourse.bass_utils import run_bass_kernel_spmd

F32 = mybir.dt.float32
BF16 = mybir.dt.bfloat16
AF = mybir.ActivationFunctionType
ALU = mybir.AluOpType

D = 1024
DFF = 2816
NH = 8
QL = 384
KVL = 256
RD = 64
PAST = 1024
EPS = 1e-5
ATTN_SCALE = float((128 + 64) ** -0.5)
NSLOT = 8
BLK = 2048


def _f(name, *a, **kw):
    return lambda e: getattr(e, name)(*a, **kw)

class Sem:
    def __init__(self, name):
        self.name = name
        self.h = None
        self.count = 0


class Eng:
    def __init__(self, name, sem):
        self.name = name
        self.sem = sem
        self.prog = []
        self.seen = {}


class Res:
    __slots__ = ("name", "w", "r", "arena")

    def __init__(self, name, arena=False):
        self.name = name
        self.w = None
        self.r = {}
        self.arena = arena


class Emitter:
    def __init__(self):
        self.sems = []
        self.pe = Eng("pe", self.sem("e_pe"))
        self.act = Eng("act", self.sem("e_act"))
        self.dve = Eng("dve", self.sem("e_dve"))
        self.pool = Eng("pool", self.sem("e_pool"))
        self.sp = Eng("sp", None)
        self.barrier = {}
        self.nobarrier = set()
        self.arena_tok = {}

    def sem(self, name):
        s = Sem(name)
        self.sems.append(s)
        return s

    def set_phase(self):
        self.barrier = dict(self.arena_tok)

    def _deps(self, E, reads, writes):
        need = {}

        def add(sem, v):
            if need.get(sem, 0) < v:
                need[sem] = v

        arena = False
        for r in reads:
            if r.w is not None:
                add(*r.w)
            arena = arena or r.arena
        for w in writes:
            if w.w is not None:
                add(*w.w)
            for sem, v in w.r.items():
                add(sem, v)
            arena = arena or w.arena
        if arena:
            for sem, v in self.barrier.items():
                add(sem, v)
        waits = []
        for sem, v in need.items():
            if sem is E.sem:
                if E.name == "pe":
                    continue
            if E.seen.get(sem, 0) >= v:
                continue
            E.seen[sem] = v
            waits.append((sem, v))
        return waits

    def _finish(self, tok, reads, writes):
        for x in list(reads) + list(writes):
            if x.arena:
                if self.arena_tok.get(tok[0], 0) < tok[1]:
                    self.arena_tok[tok[0]] = tok[1]
                break
        for w in writes:
            w.w = tok
            w.r = {}
        for r in reads:
            if r in writes:
                continue
            if r.r.get(tok[0], 0) < tok[1]:
                r.r[tok[0]] = tok[1]

    def op(self, E, fn, reads=(), writes=()):
        waits = self._deps(E, reads, writes)
        E.sem.count += 1
        tok = (E.sem, E.sem.count)
        E.prog.append((waits, fn, (E.sem, 1)))
        self._finish(tok, reads, writes)
        return tok

    def group(self, fns, reads=(), writes=()):
        E = self.pe
        waits = self._deps(E, reads, writes)
        E.sem.count += 1
        tok = (E.sem, E.sem.count)
        n = len(fns)
        for i, f in enumerate(fns):
            E.prog.append((waits if i == 0 else (), f, (E.sem, 1) if i == n - 1 else None))
        self._finish(tok, reads, writes)
        return tok

    def dma(self, Q, dsem, items, reads=(), writes=()):
        waits = self._deps(Q, reads, writes)
        for i, (o, a, kw) in enumerate(items):
            dsem.count += 16
            Q.prog.append((waits if i == 0 else (), (_f("dma_start", out=o, in_=a, **kw)), (dsem, 16)))
        tok = (dsem, dsem.count)
        self._finish(tok, reads, writes)
        return tok

    def wait_all(self, E, sems):
        waits = [(s, s.count) for s in sems if s.count > 0]
        E.prog.append((waits, None, None))

    def replay(self, E, eng):
        for waits, fn, sig in E.prog:
            for sem, v in waits:
                eng.wait_ge(sem.h, v)
            if fn is None:
                continue
            ins = fn(eng)
            if sig is not None:
                ins.then_inc(sig[0].h, sig[1])


def layer_blocks():
    bl = []

    def ffn(w):
        gu, dn = ("f%dgu" % w, "f%dd" % w)
        for j in range(11):
            bl.append((("G", w, j), gu, 0, 8, j * 256, 256))
            bl.append((("U", w, j), gu, 0, 8, DFF + j * 256, 256))
        for half in range(2):
            for k in range(6):
                nk = 4 if k < 5 else 2
                bl.append((("D", w, half, k), dn, k * 4, nk, half * 512, 512))

    ffn(1)
    for i in range(3):
        bl.append((("T", i), "wtok", 0, 8, i * 256, 256))
    for i in range(4):
        bl.append((("Q", i), "uq", 0, 3, i * 512, 512))
    for i in range(2):
        bl.append((("KV", i), "ukv", 0, 2, i * 1024, 1024))
    for j in range(4):
        bl.append((("MO", j), "mo", 0, 8, j * 256, 256))
        bl.append((("GM", j), "win", 0, 8, 4800 + j * 256, 256))
    for j in range(4):
        bl.append((("CC", j), "win", 0, 8, 1728 + j * 256, 256))
        bl.append((("CU", j), "win", 0, 8, 2752 + j * 256, 256))
        bl.append((("CB", j), "win", 0, 8, 704 + j * 256, 256))
    for j in range(4):
        bl.append((("CO", j), "co", 0, 8, j * 256, 256))
        bl.append((("GC", j), "win", 0, 8, 3776 + j * 256, 256))
    for k in range(4):
        bl.append((("MX", k), "mx", k * 2, 2, 0, 1024))
    ffn(2)
    return bl


LBLOCKS = layer_blocks()
NB = len(LBLOCKS)
BIDX = {b[0]: i for i, b in enumerate(LBLOCKS)}


def build_program(depth=4, npseq=2, ntile=8, sample=True):
    S = ntile * 512
    ntab = ntile + 1
    nc = bass.Bass("TRN2", target_bir_lowering=False)
    em = Emitter()
    PE, ACT, DVE, POOL, SP = em.pe, em.act, em.dve, em.pool, em.sp

    def din(name, shape, dt=F32):
        return nc.dram_tensor(name, list(shape), dt, kind="ExternalInput").ap()

    def dout(name, shape, dt=F32):
        return nc.dram_tensor(name, list(shape), dt, kind="ExternalOutput").ap()

    def dscr(name, shape, dt=BF16):
        return nc.dram_tensor(name, list(shape), dt).ap()

    xp = din("xp", [npseq, S, D])
    xs = din("xs", [128, D])
    clat = din("clat", [depth, 2, PAST, KVL])
    ckr = din("ckr", [depth, 2, PAST, RD])
    sconv = din("sconv", [depth, 128, 32])
    W = {
        "f1gu": din("f1gu", [depth, D, 2 * DFF]), "f1d": din("f1d", [depth, DFF, D]),
        "f2gu": din("f2gu", [depth, D, 2 * DFF]), "f2d": din("f2d", [depth, DFF, D]),
        "win": din("win", [depth, D, 5824]), "wtok": din("wtok", [depth, D, 768]),
        "uq": din("uq", [depth, QL, 2048]), "ukv": din("ukv", [depth, KVL, 2048]),
        "mo": din("mo", [depth, D, D]), "co": din("co", [depth, D, D]), "mx": din("mx", [depth, D, D]),
    }
    lng = din("lng", [depth, 3, D])
    lnb = din("lnb", [depth, 3, D])
    qg_d = din("qg", [depth, QL])
    kvg_d = din("kvg", [depth, KVL])
    bg_d = din("bg", [depth, 128, 16])
    cw_d = din("cw", [depth, 128, 24])
    cosT_d = din("cosT", [ntab, 128, 4 * 64])
    sinT_d = din("sinT", [ntab, 128, 4 * 64])
    cosF_d = din("cosF", [ntab, 128, 512])
    sinF_d = din("sinF", [ntab, 128, 512])
    identb_d = din("identb", [128, 128], BF16)
    identf_d = din("identf", [128, 128])

    yp = dout("yp", [npseq, S, D])
    ys = dout("ys", [128, D])
    plat = dout("plat", [depth, npseq, S, KVL])
    pkr = dout("pkr", [depth, npseq, S, RD])
    pconv = dout("pconv", [depth, npseq, 2, D])
    slat = dout("slat", [depth, 128, KVL])
    skr = dout("skr", [depth, 128, RD])
    sconv_o = dout("sconv_o", [depth, 2, 2, D])

    tape = dscr("tape", [depth, NB, 128, BLK])
    Kscr = dscr("Kscr", [depth, npseq, NH, 128, S])
    Vscr = dscr("Vscr", [depth, npseq, S, NH * 128])
    Rscr = dscr("Rscr", [depth, npseq, 128, S])
    KscrS = dscr("KscrS", [depth, 2, NH, 128, PAST])
    VscrS = dscr("VscrS", [depth, 2, PAST, NH * 128])
    RscrS = dscr("RscrS", [depth, 2, 128, PAST])

    def scr(slot):
        if slot < npseq:
            return Kscr, Vscr, Rscr, slot
        return KscrS, VscrS, RscrS, slot - npseq

    es = ExitStack()

    def sb(name, shape, dt):
        return es.enter_context(nc.sbuf_tensor(name, list(shape), dt))

    ring = sb("ring", [128, NSLOT * BLK], BF16)
    x_res = sb("x_res", [128, 4, D], F32)
    xT = sb("xT", [128, 8, 512], BF16)
    OT = sb("OT", [128, 8, 512], BF16)
    lnp = sb("lnp", [128, 2 * D], F32)
    cosT = sb("cosT_s", [128, 4, 64], F32)
    sinT = sb("sinT_s", [128, 4, 64], F32)
    cosF = sb("cosF_s", [128, 512], F32)
    sinF = sb("sinF_s", [128, 512], F32)
    qg = sb("qg_s", [128, QL], F32)
    kvg = sb("kvg_s", [128, KVL], F32)
    bgt = sb("bg_s", [128, 16], F32)
    cwt = sb("cw_s", [128, 24], F32)
    identb = sb("identb_s", [128, 128], BF16)
    identf = sb("identf_s", [128, 128], F32)
    ones = sb("ones_s", [128, 128], BF16)
    carry = sb("carry", [128, depth, 2, 8], F32)
    scarry = sb("scarry", [128, depth, 2, 2, 8], F32)
    cst = sb("cst", [16, 128], F32)
    stats = sb("stats", [128, 2, 2, 6], F32)
    mv = sb("mv", [128, 2, 4], F32)
    ssq = sb("ssq", [128, 2, 4], F32)
    junk = sb("junk", [128, 512], F32)
    rtmp = sb("rtmp", [128, 2, 2, 64], F32)

    ARENA_BYTES = 96 * 1024
    arena = sb("arena", [128, ARENA_BYTES // 4], F32)

    class Carver:
        def __init__(self):
            self.off = 0

        def take(self, free_shape, dt):
            n = int(np.prod(free_shape))
            esz = 4 if dt is F32 else 2
            nbytes = (n * esz + 31) // 32 * 32
            assert self.off + nbytes <= ARENA_BYTES, (self.off, nbytes)
            ap = arena[:, self.off // 4:(self.off + n * esz) // 4]
            if dt is BF16:
                ap = ap.bitcast(BF16)
            if len(free_shape) == 2:
                ap = ap.rearrange("p (a b) -> p a b", b=free_shape[1])
            elif len(free_shape) == 3:
                ap = ap.rearrange("p (a b c) -> p a b c", b=free_shape[1], c=free_shape[2])
            self.off += nbytes
            return ap

    cv = Carver()
    NPP = 6
    pp_stage = [cv.take([BLK], F32) for _ in range(NPP)]
    pp_out = [cv.take([BLK], BF16) for _ in range(NPP)]
    cv = Carver()
    hT = cv.take([22, 512], BF16)
    silu_b = [cv.take([512], F32) for _ in range(2)]
    zbuf = [sb("zbuf%d" % i, [128, D], F32) for i in range(2)]
    xb = [sb("xb%d" % i, [128, D], BF16) for i in range(2)]
    cv = Carver()
    ckv_f = cv.take([4, KVL], F32)
    kr_f = cv.take([4, RD], F32)
    qn_b = cv.take([4, QL], BF16)
    ckv_b = cv.take([4, KVL], BF16)
    krd_b = cv.take([4, 128], BF16)
    latT = cv.take([6, 512], BF16)
    QnT = cv.take([8, 512], BF16)
    QrT = cv.take([8, 512], BF16)
    ropet = [cv.take([512], F32) for _ in range(2)]
    KT_own = cv.take([8, 512], BF16)
    V_own = cv.take([4, D], BF16)
    KpT = [cv.take([3584], BF16) for _ in range(2)]
    Vp = [cv.take([28, 128], BF16) for _ in range(2)]
    krT_past = cv.take([3584], BF16)
    PT = [cv.take([512], BF16) for _ in range(3)]
    recip = cv.take([512], F32)
    cv = Carver()
    gmy = cv.take([8, 512], F32)
    C_sb = [cv.take([512], F32) for _ in range(2)]
    upb = [cv.take([520], F32) for _ in range(2)]
    accb = [cv.take([512], F32) for _ in range(2)]
    yc_in = cv.take([8, 512], BF16)
    gate = [cv.take([512], F32) for _ in range(2)]
    ttb = [cv.take([512], F32) for _ in range(2)]
    mergedT = cv.take([8, 512], BF16)

    banks = [es.enter_context(nc.psum_tensor("bank%d" % i, [128, 512], F32)) for i in range(8)]
    bres = [Res("bank%d" % i) for i in range(8)]
    rot_state = [0]

    rot_pool = [4]

    def rot(n=1):
        P = rot_pool[0]
        if n == 2:
            if rot_state[0] % 2:
                rot_state[0] += 1
        ids = [(rot_state[0] + i) % P for i in range(n)]
        rot_state[0] = (rot_state[0] + n) % P
        return ids

    ring_sem = [em.sem("ring%d" % i) for i in range(NSLOT)]
    for s_ in ring_sem:
        em.nobarrier.add(s_)
    ring_res = [Res("ring%d" % i) for i in range(NSLOT)]
    s_xld = em.sem("xld")
    s_tbl = em.sem("tbl")
    s_lnp = em.sem("lnp")
    s_par = em.sem("par")
    s_kvp = [em.sem("kvp0"), em.sem("kvp1")]
    s_krp = em.sem("krp")
    s_cache = [em.sem("cache0"), em.sem("cache1")]
    s_scar = em.sem("scar")
    s_sty = em.sem("sty")
    s_stlat = em.sem("stlat")
    s_stkr = em.sem("stkr")
    s_stK = em.sem("stK")
    s_stV = em.sem("stV")
    s_stR = em.sem("stR")
    s_stc = em.sem("stc")
    s_ppin = [em.sem("ppin%d" % i) for i in range(NPP)]
    s_ppout = [em.sem("ppout%d" % i) for i in range(NPP)]
    s_const = em.sem("const")

    R = {}

    def res(name, arena=False):
        if name not in R:
            R[name] = Res(name, arena)
        return R[name]

    r_tape = res("tape")
    r_xres = [res("xres%d" % s) for s in range(4)]
    r_xT = [res("xT%d" % s) for s in range(4)]
    r_lnp = res("lnp")
    r_tbl = res("tbl")
    r_par = res("par")
    r_const = res("const")
    r_OT = [res("OT%d" % h) for h in range(8)]
    r_carry = [res("carry%d" % l) for l in range(depth)]
    r_scarry = [res("scarry%d" % l) for l in range(depth)]
    r_cst = res("cst")
    r_stats = [res("stats0"), res("stats1")]
    r_ssq = [res("ssq0"), res("ssq1")]
    r_junk = res("junk")
    r_rtmp = [res("rtmp0"), res("rtmp1")]
    r_scrK = {}
    r_out = res("out_dram")

    def ar(name):
        return res(name, arena=True)

    r_ppst = [ar("ppst%d" % i) for i in range(NPP)]
    r_ppo = [ar("ppo%d" % i) for i in range(NPP)]
    r_hT = [ar("hT%d" % c) for c in range(22)]
    r_silu = [ar("silu0"), ar("silu1")]
    r_z = [res("z0"), res("z1")]
    r_xb = [res("xb0"), res("xb1")]
    r_ckvf = ar("ckvf")
    r_krf = ar("krf")
    r_qnb = [ar("qnb%d" % s) for s in range(4)]
    r_ckvb = [ar("ckvb%d" % s) for s in range(4)]
    r_krdb = [ar("krdb%d" % s) for s in range(4)]
    r_latT = [ar("latT%d" % s) for s in range(4)]
    r_QnT = [ar("QnT%d" % h) for h in range(8)]
    r_QrT = [ar("QrT%d" % p) for p in range(8)]
    r_ropet = [ar("ropet0"), ar("ropet1")]
    r_KT = [ar("KT%d" % h) for h in range(8)]
    r_V = [ar("V%d" % s) for s in range(4)]
    r_KpT = [ar("KpT0"), ar("KpT1")]
    r_krp = ar("krTpast")
    r_PT = [ar("PT%d" % i) for i in range(3)]
    r_recip = ar("recip")
    r_gmy = [ar("gmy%d" % c) for c in range(8)]
    r_Csb = [ar("Csb0"), ar("Csb1")]
    r_up = [ar("up0"), ar("up1")]
    r_acc = [ar("acc0"), ar("acc1")]
    r_ycin = [ar("ycin%d" % c) for c in range(8)]
    r_gate = [ar("gate0"), ar("gate1")]
    r_tt = [ar("tt0"), ar("tt1")]
    r_mT = [ar("mT%d" % c) for c in range(8)]

    em.dma(SP, s_const, [(identb[:], identb_d[:, :], {}), (identf[:], identf_d[:, :], {})], writes=[r_const])
    em.op(POOL, _f("memset", ones[:], 1.0), writes=[r_const])

    cnt = 0
    for l in range(depth):
        for bi, (name, wk, k0, nk, c0, ncols) in enumerate(LBLOCKS):
            i = cnt % NPP
            width = nk * ncols
            src = W[wk][l, k0 * 128:(k0 + nk) * 128, c0:c0 + ncols].rearrange("(k p) c -> p k c", p=128)
            dst = pp_stage[i][:, 0:width].rearrange("p (k c) -> p k c", c=ncols)
            em.dma(SP, s_ppin[i], [(dst, src, {})], writes=[r_ppst[i]])
            if cnt % 2 == 0:
                em.op(ACT, _f("activation", out=pp_out[i][:, 0:width], in_=pp_stage[i][:, 0:width], func=AF.Copy),
                      reads=[r_ppst[i]], writes=[r_ppo[i]])
            else:
                em.op(DVE, _f("tensor_copy", out=pp_out[i][:, 0:width], in_=pp_stage[i][:, 0:width]),
                      reads=[r_ppst[i]], writes=[r_ppo[i]])
            em.dma(POOL, s_ppout[i], [(tape[l, bi, :, 0:width], pp_out[i][:, 0:width], {})], reads=[r_ppo[i]], writes=[r_tape])
            cnt += 1
    em.wait_all(SP, s_ppout)
    r_tape.w = None

    tiles = []
    for b in range(npseq):
        for t in range(ntile):
            tiles.append(dict(kind="p", b=b, t=t, NT=512, NSUB=4, tab=t))
    seq = []
    for _ in tiles:
        for l in range(depth):
            seq += [(l, bi) for bi in range(NB)]
    if sample:
        for l in range(depth):
            seq += [(l, BIDX[("KV", 0)]), (l, BIDX[("KV", 1)])]
        for l in range(depth):
            seq += [(l, bi) for bi in range(NB)]

    class Ring:
        def __init__(self):
            self.loaded = 0
            self.consumed = 0
            self.closed = 0

        def _pump(self):
            while self.loaded < min(len(seq), self.closed + NSLOT):
                k = self.loaded
                l, bi = seq[k]
                _, wk, k0, nk, c0, ncols = LBLOCKS[bi]
                width = nk * ncols
                s = k % NSLOT
                em.dma(SP, ring_sem[s], [(ring[:, s * BLK:s * BLK + width], tape[l, bi, :, 0:width], {})],
                       reads=[r_tape], writes=[ring_res[s]])
                self.loaded += 1

        def next(self, l, name):
            k = self.consumed
            assert seq[k] == (l, BIDX[name]), (seq[k], l, name)
            self._pump()
            assert k < self.loaded
            self.consumed += 1
            s = k % NSLOT
            return ring[:, s * BLK:(s + 1) * BLK], ring_res[s]

        def close(self, n=1):
            self.closed += n
            self._pump()

    rg = Ring()

    ALPHA = float((2 * 4) ** 0.25)

    def mm(out, lhsT, rhs, start, stop):
        return _f("matmul", out, lhsT, rhs, start=start, stop=stop)

    def tr(out, in_, ident):
        return _f("transpose", out, in_, ident)

    alt = [0]

    def evac_copy(out, in_, reads, writes, scale=None):
        alt[0] += 1
        if alt[0] % 2 == 0:
            if scale is None:
                em.op(ACT, _f("activation", out=out, in_=in_, func=AF.Copy), reads=reads, writes=writes)
            else:
                em.op(ACT, _f("activation", out=out, in_=in_, func=AF.Copy, scale=scale), reads=reads, writes=writes)
        else:
            if scale is None:
                em.op(DVE, _f("tensor_copy", out=out, in_=in_), reads=reads, writes=writes)
            else:
                em.op(DVE, _f("tensor_scalar", out=out, in0=in_, scalar1=scale, scalar2=None, op0=ALU.mult), reads=reads, writes=writes)

    pendT = []

    def cast_x(s):
        par = s % 2
        em.op(ACT, _f("activation", out=xb[par][:, :], in_=x_res[:, s, :], func=AF.Copy), reads=[r_xres[s]], writes=[r_xb[par]])

    def flush_T(keep=0):
        while len(pendT) > keep:
            emit_T(pendT.pop(0))

    def make_xT(s, bank=None):
        cast_x(s)
        emit_T(s, bank)

    def emit_T(s, bank=None):
        par = s % 2
        if bank is None:
            (bk,) = rot(1)
        else:
            bk = bank
        pb = banks[bk][:].bitcast(BF16)
        em.group([tr(pb[:, kc * 128:(kc + 1) * 128], xb[par][:, kc * 128:(kc + 1) * 128], identb[:]) for kc in range(8)],
                 reads=[r_xb[par], r_const], writes=[bres[bk]])
        evac_copy(xT[:, :, s * 128:(s + 1) * 128], pb.rearrange("p (a b) -> p a b", b=128), [bres[bk]], [r_xT[s]])

    def load_lnp(l, i):
        em.dma(SP, s_lnp, [(lnp[:, 0:D], lng[l, i:i + 1, :].partition_broadcast(128), {}),
                           (lnp[:, D:2 * D], lnb[l, i:i + 1, :].partition_broadcast(128), {})], writes=[r_lnp])

    lnst = sb("lnst", [128, 4, 8], F32)
    junk2 = sb("junk2", [128, D], BF16)
    r_lnst = [res("lnst%d" % i) for i in range(4)]
    r_junk2 = res("junk2")

    def ln_A(s):
        st = lnst[:, s, :]
        em.op(ACT, _f("activation", out=junk2[:, :], in_=x_res[:, s, :], func=AF.Square, accum_out=lnst[:, s, 2:3]),
              reads=[r_xres[s]], writes=[r_junk2, r_lnst[s]])
        em.op(ACT, _f("activation", out=lnst[:, s, 3:4], in_=lnst[:, s, 0:1], func=AF.Identity, bias=lnst[:, s, 1:2], scale=1.0),
              reads=[r_lnst[s]], writes=[r_lnst[s]])
        em.op(ACT, _f("activation", out=lnst[:, s, 4:5], in_=lnst[:, s, 3:4], func=AF.Copy, scale=-1.0 / (D * D)),
              reads=[r_lnst[s]], writes=[r_lnst[s]])
        em.op(ACT, _f("activation", out=lnst[:, s, 5:6], in_=lnst[:, s, 3:4], func=AF.Identity, bias=EPS_AP[:, 0:1], scale=lnst[:, s, 4:5]),
              reads=[r_lnst[s], r_const], writes=[r_lnst[s]])
        em.op(ACT, _f("activation", out=lnst[:, s, 6:7], in_=lnst[:, s, 2:3], func=AF.Sqrt, bias=lnst[:, s, 5:6], scale=1.0 / D),
              reads=[r_lnst[s]], writes=[r_lnst[s]])

    def ln_B(s):
        par = s % 2
        em.op(DVE, _f("reciprocal", out=lnst[:, s, 7:8], in_=lnst[:, s, 6:7]), reads=[r_lnst[s]], writes=[r_lnst[s]])
        em.op(DVE, _f("scalar_tensor_tensor", out=lnst[:, s, 4:5], in0=lnst[:, s, 3:4], scalar=-1.0 / D, in1=lnst[:, s, 7:8],
                      op0=ALU.mult, op1=ALU.mult), reads=[r_lnst[s]], writes=[r_lnst[s]])
        em.op(ACT, _f("activation", out=zbuf[par][:, :], in_=x_res[:, s, :], func=AF.Identity, bias=lnst[:, s, 4:5], scale=lnst[:, s, 7:8]),
              reads=[r_xres[s], r_lnst[s]], writes=[r_z[par]])

    def ln_C(s, want_cast):
        par = s % 2
        em.op(DVE, _f("tensor_tensor", out=zbuf[par][:, :], in0=zbuf[par][:, :], in1=lnp[:, 0:D], op=ALU.mult),
              reads=[r_z[par], r_lnp], writes=[r_z[par]])
        if want_cast:
            em.op(DVE, _f("tensor_tensor", out=xb[par][:, :], in0=zbuf[par][:, :], in1=lnp[:, D:2 * D], op=ALU.add),
                  reads=[r_z[par], r_lnp], writes=[r_xb[par]])
        em.op(POOL, _f("tensor_tensor", out=x_res[:, s, :], in0=zbuf[par][:, :], in1=lnp[:, D:2 * D], op=ALU.add),
              reads=[r_z[par], r_lnp], writes=[r_xres[s]])

    def ln_pipeline(NSUB, emit_group, tbank, want_T):
        for step in range(NSUB + 3):
            if step < NSUB:
                emit_group(step)
                ln_A(step)
            if 0 <= step - 1 < NSUB:
                ln_B(step - 1)
            if 0 <= step - 2 < NSUB:
                ln_C(step - 2, want_T)
            if want_T and 0 <= step - 3 < NSUB - 1:
                emit_T(step - 3, tbank(step - 3))
        if want_T:
            pendT.append(NSUB - 1)

    epsb = sb("epsb", [128, 1], F32)
    EPS_AP = epsb
    em.op(POOL, _f("memset", epsb[:], EPS), writes=[r_const])

    def residual_add(s, half, bk):
        em.op(DVE, _f("scalar_tensor_tensor", out=x_res[:, s, half * 512:(half + 1) * 512],
                      in0=x_res[:, s, half * 512:(half + 1) * 512], scalar=ALPHA,
                      in1=banks[bk][:, :], op0=ALU.mult, op1=ALU.add, accum_out=lnst[:, s, half:half + 1]),
              reads=[r_xres[s], bres[bk]], writes=[r_xres[s], r_lnst[s]])

    def ffn(l, w, tile, ln_idx, last):
        NT, NSUB = tile["NT"], tile["NSUB"]
        load_lnp(l, ln_idx)
        rot_pool[0] = 8
        split = (NSUB == 4 and len(pendT) > 0)
        if not split:
            flush_T()
        deferred = []

        def gu_mms(G, U, i, bg_, bu_, t0, t1):
            fns = [mm(banks[bg_][:, t0:t1], G[:, kc * 256 + i * 128:kc * 256 + (i + 1) * 128], xT[:, kc, t0:t1], kc == 0, kc == 7)
                   for kc in range(8)]
            fns += [mm(banks[bu_][:, t0:t1], U[:, kc * 256 + i * 128:kc * 256 + (i + 1) * 128], xT[:, kc, t0:t1], kc == 0, kc == 7)
                    for kc in range(8)]
            return fns

        def gu_evac(c, bg_, bu_):
            par = c % 2
            em.op(ACT, _f("activation", out=silu_b[par][:, 0:NT], in_=banks[bg_][:, 0:NT], func=AF.Silu),
                  reads=[bres[bg_]], writes=[r_silu[par]])
            em.op(DVE, _f("scalar_tensor_tensor", out=hT[:, c, 0:NT], in0=silu_b[par][:, 0:NT], scalar=0.5,
                          in1=banks[bu_][:, 0:NT], op0=ALU.mult, op1=ALU.mult),
                  reads=[r_silu[par], bres[bu_]], writes=[r_hT[c]])

        for j in range(11):
            G, rG = rg.next(l, ("G", w, j))
            U, rU = rg.next(l, ("U", w, j))
            for i in range(2):
                c = 2 * j + i
                bg_, bu_ = rot(2)
                if split and c < 3:
                    em.group(gu_mms(G, U, i, bg_, bu_, 0, 384), reads=[rG, rU] + r_xT[0:3], writes=[bres[bg_], bres[bu_]])
                    deferred.append((c, G, U, rG, rU, i, bg_, bu_))
                    if c == 2:
                        flush_T()
                        for (c2, G2, U2, rG2, rU2, i2, bg2, bu2) in deferred:
                            em.group(gu_mms(G2, U2, i2, bg2, bu2, 384, 512), reads=[rG2, rU2, r_xT[3]], writes=[bres[bg2], bres[bu2]])
                            gu_evac(c2, bg2, bu2)
                else:
                    em.group(gu_mms(G, U, i, bg_, bu_, 0, NT), reads=[rG, rU] + r_xT[:NSUB], writes=[bres[bg_], bres[bu_]])
                    gu_evac(c, bg_, bu_)
            if split and j == 0:
                pass
            elif split and j == 1:
                rg.close(4)
            else:
                rg.close(2)
        rot_pool[0] = 4
        for k in range(6):
            Dk, rD = rg.next(l, ("D", w, 0, k))
            nk = 4 if k < 5 else 2
            fns = []
            for kk in range(nk):
                kc = 4 * k + kk
                for s in range(NSUB):
                    fns.append(mm(banks[4 + s][:, :], hT[:, kc, s * 128:(s + 1) * 128], Dk[:, kk * 512:(kk + 1) * 512], kc == 0, kc == 21))
            em.group(fns, reads=[rD] + r_hT[4 * k:4 * k + nk], writes=[bres[4 + s] for s in range(NSUB)])
            rg.close(1)
        Dh = [rg.next(l, ("D", w, 1, k)) for k in range(6)]
        def grp(s):
            fns = []
            for kc in range(22):
                Dk = Dh[kc // 4][0]
                kk = kc % 4
                fns.append(mm(banks[s][:, :], hT[:, kc, s * 128:(s + 1) * 128], Dk[:, kk * 512:(kk + 1) * 512], kc == 0, kc == 21))
            em.group(fns, reads=[d[1] for d in Dh] + r_hT, writes=[bres[s]])
            if s == NSUB - 1:
                rg.close(6)
            residual_add(s, 0, 4 + s)
            residual_add(s, 1, s)

        ln_pipeline(NSUB, grp, lambda s2: 4 + s2, want_T=not last)

    def lat_transposes(s, with_q):
        (bk,) = rot(1)
        pb = banks[bk][:].bitcast(BF16)
        fns = []
        reads = [r_ckvb[s], r_krdb[s], r_const]
        if with_q:
            reads.append(r_qnb[s])
            for c in range(3):
                fns.append(tr(pb[:, c * 128:(c + 1) * 128], qn_b[:, s, c * 128:(c + 1) * 128], identb[:]))
        for c in range(2):
            fns.append(tr(pb[:, 384 + c * 128:384 + (c + 1) * 128], ckv_b[:, s, c * 128:(c + 1) * 128], identb[:]))
        fns.append(tr(pb[:, 640:768], krd_b[:, s, :], identb[:]))
        em.group(fns, reads=reads, writes=[bres[bk]])
        if with_q:
            evac_copy(latT[:, :, s * 128:(s + 1) * 128], pb[:, 0:768].rearrange("p (a b) -> p a b", b=128), [bres[bk]], [r_latT[s]])
        else:
            evac_copy(latT[:, 3:6, s * 128:(s + 1) * 128], pb[:, 384:768].rearrange("p (a b) -> p a b", b=128), [bres[bk]], [r_latT[s]])

    def cast_lat(s):
        em.op(POOL, _f("tensor_copy", out=ckv_b[:, s, :], in_=ckv_f[:, s, :]), reads=[r_ckvf], writes=[r_ckvb[s]])
        em.op(POOL, _f("tensor_copy", out=krd_b[:, s, 0:64], in_=kr_f[:, s, :]), reads=[r_krf], writes=[r_krdb[s]])
        em.op(POOL, _f("tensor_copy", out=krd_b[:, s, 64:128], in_=kr_f[:, s, :]), reads=[r_krf], writes=[r_krdb[s]])

    def kv_expand(l, NT, NSUB, vblocks, store):
        KV0, rK0 = rg.next(l, ("KV", 0))
        KV1, rK1 = rg.next(l, ("KV", 1))
        for h in range(NH):
            (bk,) = rot(1)
            em.group([mm(banks[bk][:, 0:NT], KV0[:, kc * 1024 + h * 128:kc * 1024 + (h + 1) * 128], latT[:, 3 + kc, 0:NT], kc == 0, kc == 1)
                      for kc in range(2)], reads=[rK0] + r_latT[:NSUB], writes=[bres[bk]])
            evac_copy(KT_own[:, h, 0:NT], banks[bk][:, 0:NT], [bres[bk]], [r_KT[h]])
        for vb, (c0, nk) in enumerate(vblocks):
            b0, b1 = rot(2)
            fns = []
            for half, bk in enumerate((b0, b1)):
                for kc in range(2):
                    fns.append(mm(banks[bk][0:nk, :], latT[:, 3 + kc, c0:c0 + nk], KV1[:, kc * 1024 + half * 512:kc * 1024 + (half + 1) * 512], kc == 0, kc == 1))
            em.group(fns, reads=[rK1] + r_latT[:NSUB], writes=[bres[b0], bres[b1]])
            for half, bk in enumerate((b0, b1)):
                evac_copy(V_own[0:nk, vb, half * 512:(half + 1) * 512], banks[bk][0:nk, :], [bres[bk]], [r_V[vb]])
        if store is not None:
            slot, t0 = store
            Ks, Vs, Rs, si = scr(slot)
            key = (l, slot)
            rk = r_scrK.setdefault(key, [Res("scrK"), Res("scrV"), Res("scrR")])
            em.dma(POOL, s_stK, [(Ks[l, si, :, :, t0:t0 + NT].rearrange("h d t -> d h t"), KT_own[:, :, 0:NT], {})],
                   reads=r_KT, writes=[rk[0]])
            em.dma(POOL, s_stV, [(Vs[l, si, t0:t0 + NT, :].rearrange("(s p) c -> p s c", p=128), V_own[:, 0:NSUB, :], {})],
                   reads=r_V[:NSUB], writes=[rk[1]])
            em.dma(POOL, s_stR, [(Rs[l, si, :, t0:t0 + NT], latT[:, 5, 0:NT], {})], reads=r_latT[:NSUB], writes=[rk[2]])

    hcount = [0]

    def load_past(l, slot, h, n_past, buf):
        Ks, Vs, Rs, si = scr(slot)
        rk = r_scrK[(l, slot)]
        nkb = n_past // 128
        em.dma(SP, s_kvp[buf],
               [(KpT[buf][:, 0:n_past], Ks[l, si, h, :, 0:n_past], {}),
                (Vp[buf][:, 0:nkb, :], Vs[l, si, 0:n_past, h * 128:(h + 1) * 128].rearrange("(k p) v -> p k v", p=128), {})],
               reads=[rk[0], rk[1]], writes=[r_KpT[buf]])

    def attention_prefetch(l, jobs):
        pairs = [(ji, h) for ji in range(len(jobs)) for h in range(NH)]

        def prefetch(idx):
            if idx < len(pairs):
                ji, h = pairs[idx]
                jb = jobs[ji]
                if jb["n_past"] > 0:
                    if idx == 0:
                        load_krp(jb)
                    load_past(l, jb["slot"], h, jb["n_past"], idx % 2)

        def load_krp(jb):
            Ks, Vs, Rs, si = scr(jb["slot"])
            em.dma(SP, s_krp, [(krT_past[:, 0:jb["n_past"]], Rs[l, si, :, 0:jb["n_past"]], {})],
                   reads=[r_scrK[(l, jb["slot"])][2]], writes=[r_krp])

        prefetch(0)
        prefetch(1)
        return pairs, prefetch, load_krp

    def attention(l, jobs, pre):
        pairs, prefetch, load_krp = pre
        for idx, (ji, h) in enumerate(pairs):
            jb = jobs[ji]
            q0, N, n_past = jb["q0"], jb["N"], jb["n_past"]
            if h == 0 and ji > 0 and n_past > 0:
                load_krp(jb)
            buf = idx % 2
            hp = h % 2
            rows = slice(hp * 64, hp * 64 + 64)
            blocks = []
            for kb in range(n_past // 128):
                blocks.append(dict(K=KpT[buf][:, kb * 128:(kb + 1) * 128], R=krT_past[:, kb * 128:(kb + 1) * 128],
                                   V=Vp[buf][:, kb, :], nk=128, q0=q0, N=N, masked=False,
                                   reads=[r_KpT[buf], r_krp]))
            for (kc0, nk, vb, qb0, masked) in jb["own"]:
                blocks.append(dict(K=KT_own[:, h, kc0:kc0 + nk], R=latT[:, 5, kc0:kc0 + nk],
                                   V=V_own[0:nk, vb, h * 128:(h + 1) * 128], nk=nk, q0=qb0, N=q0 + N - qb0, masked=masked,
                                   reads=[r_KT[h], r_V[vb]] + r_latT))
            nb = len(blocks)
            Ob = 4 + (hcount[0] % 2)
            Lb = 6 + (hcount[0] % 2)
            hcount[0] += 1
            sbank = {}

            def QK(i):
                bl = blocks[i]
                (bk,) = rot(1)
                sbank[i] = bk
                nk, bq0, bN = bl["nk"], bl["q0"], bl["N"]
                em.group([mm(banks[bk][0:nk, 0:bN], bl["K"], QnT[:, h, bq0:bq0 + bN], True, False),
                          mm(banks[bk][0:nk, 0:bN], bl["R"], QrT[:, h, bq0:bq0 + bN], False, True)],
                         reads=bl["reads"] + [r_QnT[h], r_QrT[h]], writes=[bres[bk]])

            def EXP(i):
                bl = blocks[i]
                bk = sbank[i]
                p = i % 3
                nk, bN = bl["nk"], bl["N"]
                if not bl["masked"]:
                    em.op(ACT, _f("activation", out=PT[p][0:nk, 0:bN], in_=banks[bk][0:nk, 0:bN], func=AF.Exp),
                          reads=[bres[bk]], writes=[r_PT[p]])
                else:
                    em.op(ACT, _f("activation", out=PT[p][0:128, 64:bN], in_=banks[bk][0:128, 64:bN], func=AF.Exp),
                          reads=[bres[bk]], writes=[r_PT[p]])
                    em.op(ACT, _f("activation", out=PT[p][0:64, 0:64], in_=banks[bk][0:64, 0:64], func=AF.Exp),
                          reads=[bres[bk]], writes=[r_PT[p]])
                    em.op(POOL, _f("memset", PT[p][64:128, 0:64], 0.0), reads=[r_PT[p]], writes=[r_PT[p]])

            def PV(i):
                bl = blocks[i]
                p = i % 3
                nk, bq0, bN = bl["nk"], bl["q0"], bl["N"]
                em.group([mm(banks[Ob][:, bq0:bq0 + bN], bl["V"], PT[p][0:nk, 0:bN], i == 0, i == nb - 1),
                          mm(banks[Lb][:, bq0:bq0 + bN], ones[0:nk, :], PT[p][0:nk, 0:bN], i == 0, i == nb - 1)],
                         reads=bl["reads"] + [r_PT[p], r_const], writes=[bres[Ob], bres[Lb]])

            QK(0)
            if nb > 1:
                QK(1)
            for i in range(nb):
                if i + 2 < nb:
                    QK(i + 2)
                EXP(i)
                PV(i)
            prefetch(idx + 2)
            em.op(DVE, _f("reciprocal", out=recip[:, q0:q0 + N], in_=banks[Lb][:, q0:q0 + N]), reads=[bres[Lb]], writes=[r_recip])
            em.op(DVE, _f("tensor_tensor", out=OT[:, h, q0:q0 + N], in0=banks[Ob][:, q0:q0 + N], in1=recip[:, q0:q0 + N], op=ALU.mult),
                  reads=[bres[Ob], r_recip], writes=[r_OT[h]])

    def mixer(l, tile):
        NT, NSUB, kind = tile["NT"], tile["NSUB"], tile["kind"]
        em.set_phase()
        rot_pool[0] = 8
        load_lnp(l, 1)
        if kind == "p":
            jobs = [dict(q0=0, N=512, slot=tile["b"], n_past=tile["t"] * 512,
                         own=[(j * 128, 128, j, j * 128, True) for j in range(4)])]
        else:
            jobs = [dict(q0=i * 64, N=64, slot=npseq + i, n_past=PAST, own=[(i * 64, 64, i, i * 64, False)]) for i in range(2)]
        pre = attention_prefetch(l, jobs)
        Tb = [rg.next(l, ("T", i)) for i in range(3)]
        latb = {}

        def lat_group(s):
            if s in pendT:
                flush_T()
            bA, bB = rot(2)
            latb[s] = (bA, bB)
            fns = []
            for reg, (bk, c0) in enumerate(((bA, 0), (bA, 256), (bB, 0))):
                for kc in range(8):
                    fns.append(mm(banks[bk][:, c0:c0 + 256], xT[:, kc, s * 128:(s + 1) * 128], Tb[reg][0][:, kc * 256:(kc + 1) * 256], kc == 0, kc == 7))
            em.group(fns, reads=[t[1] for t in Tb] + [r_xT[s]], writes=[bres[bA], bres[bB]])

        for s in range(min(3, NSUB)):
            lat_group(s)
        for s in range(NSUB):
            if s + 3 < NSUB:
                pass
            bA, bB = latb[s]
            par = s % 2
            em.op(ACT, _f("activation", out=junk[:, 0:QL], in_=banks[bA][:, 0:QL], func=AF.Square, scale=float(QL ** -0.5),
                                                              accum_out=ssq[:, par, 0:1]), reads=[bres[bA]], writes=[r_junk, r_ssq[par]])
            em.op(ACT, _f("activation", out=junk[:, 0:KVL], in_=banks[bB][:, 0:KVL], func=AF.Square, scale=float(KVL ** -0.5),
                                                              accum_out=ssq[:, par, 1:2]), reads=[bres[bB]], writes=[r_junk, r_ssq[par]])
            em.op(ACT, _f("activation", out=ssq[:, par, 2:4], in_=ssq[:, par, 0:2], func=AF.Sqrt, bias=EPS_AP[:, 0:1], scale=1.0),
                  reads=[r_ssq[par], r_const], writes=[r_ssq[par]])
            em.op(DVE, _f("reciprocal", out=ssq[:, par, 0:2], in_=ssq[:, par, 2:4]), reads=[r_ssq[par]], writes=[r_ssq[par]])
            em.op(DVE, _f("scalar_tensor_tensor", out=qn_b[:, s, :], in0=banks[bA][:, 0:QL], scalar=ssq[:, par, 0:1],
                                                                             in1=qg[:, :], op0=ALU.mult, op1=ALU.mult),
                  reads=[bres[bA], r_ssq[par], r_par], writes=[r_qnb[s]])
            em.op(DVE, _f("scalar_tensor_tensor", out=ckv_f[:, s, :], in0=banks[bB][:, 0:KVL], scalar=ssq[:, par, 1:2],
                                                                             in1=kvg[:, :], op0=ALU.mult, op1=ALU.mult),
                  reads=[bres[bB], r_ssq[par], r_par], writes=[r_ckvf])
            em.op(DVE, _f("tensor_tensor", out=rtmp[:, par, 0, :], in0=banks[bA][:, 384:448], in1=cosT[:, s, :], op=ALU.mult),
                  reads=[bres[bA], r_tbl], writes=[r_rtmp[par]])
            em.op(DVE, _f("tensor_tensor", out=rtmp[:, par, 1, :], in0=banks[bA][:, 448:512], in1=sinT[:, s, :], op=ALU.mult),
                  reads=[bres[bA], r_tbl, r_rtmp[par]], writes=[r_rtmp[par]])
            em.op(POOL, _f("tensor_tensor", out=kr_f[:, s, :], in0=rtmp[:, par, 0, :], in1=rtmp[:, par, 1, :], op=ALU.add),
                  reads=[r_rtmp[par]], writes=[r_krf])
            cast_lat(s)
            lat_transposes(s, with_q=True)
            if s + 3 < NSUB:
                lat_group(s + 3)
        flush_T()
        rg.close(3)
        if kind == "p":
            b, t0 = tile["b"], tile["t"] * 512
            em.dma(POOL, s_stlat, [(plat[l, b, t0:t0 + NT, :].rearrange("(s p) c -> p s c", p=128), ckv_f[:, 0:NSUB, :], {})],
                   reads=[r_ckvf])
            em.dma(POOL, s_stkr, [(pkr[l, b, t0:t0 + NT, :].rearrange("(s p) c -> p s c", p=128), kr_f[:, 0:NSUB, :], {})],
                   reads=[r_krf])
        else:
            em.dma(POOL, s_stlat, [(slat[l, :, :], ckv_f[:, 0, :], {})], reads=[r_ckvf])
            em.dma(POOL, s_stkr, [(skr[l, :, :], kr_f[:, 0, :], {})], reads=[r_krf])
        Qb = [rg.next(l, ("Q", i)) for i in range(4)]
        for h in range(NH):
            (bk,) = rot(1)
            Qw, rQ = Qb[h // 4]
            em.group([mm(banks[bk][:, 0:NT], Qw[:, kc * 512 + (h % 4) * 128:kc * 512 + (h % 4 + 1) * 128], latT[:, kc, 0:NT], kc == 0, kc == 2)
                      for kc in range(3)], reads=[rQ] + r_latT[:NSUB], writes=[bres[bk]])
            evac_copy(QnT[:, h, 0:NT], banks[bk][:, 0:NT], [bres[bk]], [r_QnT[h]], scale=ATTN_SCALE)
        for pr in range(4):
            bA, bB = rot(2)
            em.group([mm(banks[bA][:, 0:NT], Qb[2][0][:, kc * 512 + pr * 128:kc * 512 + (pr + 1) * 128], latT[:, kc, 0:NT], kc == 0, kc == 2) for kc in range(3)]
                     + [mm(banks[bB][:, 0:NT], Qb[3][0][:, kc * 512 + pr * 128:kc * 512 + (pr + 1) * 128], latT[:, kc, 0:NT], kc == 0, kc == 2) for kc in range(3)],
                     reads=[Qb[2][1], Qb[3][1]] + r_latT[:NSUB], writes=[bres[bA], bres[bB]])
            em.op(DVE, _f("tensor_tensor", out=ropet[0][:, 0:NT], in0=banks[bA][:, 0:NT], in1=cosF[:, 0:NT], op=ALU.mult),
                  reads=[bres[bA], r_tbl], writes=[r_ropet[0]])
            em.op(DVE, _f("tensor_tensor", out=ropet[1][:, 0:NT], in0=banks[bB][:, 0:NT], in1=sinF[:, 0:NT], op=ALU.mult),
                  reads=[bres[bB], r_tbl], writes=[r_ropet[1]])
            for hh in range(2):
                hq = 2 * pr + hh
                lo, zo = hh * 64, (1 - hh) * 64
                em.op(POOL, _f("tensor_tensor", out=QrT[lo:lo + 64, hq, 0:NT], in0=ropet[0][lo:lo + 64, 0:NT], in1=ropet[1][lo:lo + 64, 0:NT], op=ALU.add),
                      reads=r_ropet, writes=[r_QrT[hq]])
                em.op(POOL, _f("memset", QrT[zo:zo + 64, hq, 0:NT], 0.0), writes=[r_QrT[hq]])
        rg.close(4)
        if kind == "p":
            vblocks = [(j * 128, 128) for j in range(4)]
            kv_expand(l, NT, NSUB, vblocks, store=(tile["b"], tile["t"] * 512))
        else:
            vblocks = [(0, 64), (64, 64)]
            kv_expand(l, NT, NSUB, vblocks, store=None)
        rg.close(2)
        rot_pool[0] = 4
        attention(l, jobs, pre)

        em.set_phase()
        rot_pool[0] = 8
        for j in range(4):
            MO, rMO = rg.next(l, ("MO", j))
            GM, rGM = rg.next(l, ("GM", j))
            for i in range(2):
                c = 2 * j + i
                by, bg_ = rot(2)
                em.group([mm(banks[by][:, 0:NT], MO[:, kc * 256 + i * 128:kc * 256 + (i + 1) * 128], OT[:, kc, 0:NT], kc == 0, kc == 7) for kc in range(8)]
                         + [mm(banks[bg_][:, 0:NT], GM[:, kc * 256 + i * 128:kc * 256 + (i + 1) * 128], xT[:, kc, 0:NT], kc == 0, kc == 7) for kc in range(8)],
                         reads=[rMO, rGM] + r_OT + r_xT[:NSUB], writes=[bres[by], bres[bg_]])
                par = c % 2
                em.op(ACT, _f("activation", out=gate[par][:, 0:NT], in_=banks[bg_][:, 0:NT], func=AF.Sigmoid,
                                                                         bias=bgt[:, 8 + c:9 + c], scale=1.0),
                      reads=[bres[bg_], r_par], writes=[r_gate[par]])
                em.op(DVE, _f("tensor_tensor", out=gmy[:, c, 0:NT], in0=banks[by][:, 0:NT], in1=gate[par][:, 0:NT], op=ALU.mult),
                      reads=[bres[by], r_gate[par]], writes=[r_gmy[c]])
            rg.close(2)
        if kind == "p":
            segs = [(0, NT)]
        else:
            segs = [(0, 64), (64, 64)]
        for j in range(4):
            CC, rCC = rg.next(l, ("CC", j))
            CU, rCU = rg.next(l, ("CU", j))
            CB, rCB = rg.next(l, ("CB", j))
            for i in range(2):
                c = 2 * j + i
                bC, bU = rot(2)
                (bB,) = rot(1)
                em.group([mm(banks[bC][:, 0:NT], CC[:, kc * 256 + i * 128:kc * 256 + (i + 1) * 128], xT[:, kc, 0:NT], kc == 0, kc == 7) for kc in range(8)]
                         + [mm(banks[bU][:, 0:NT], CU[:, kc * 256 + i * 128:kc * 256 + (i + 1) * 128], xT[:, kc, 0:NT], kc == 0, kc == 7) for kc in range(8)]
                         + [mm(banks[bB][:, 0:NT], CB[:, kc * 256 + i * 128:kc * 256 + (i + 1) * 128], xT[:, kc, 0:NT], kc == 0, kc == 7) for kc in range(8)],
                         reads=[rCC, rCU, rCB] + r_xT[:NSUB], writes=[bres[bC], bres[bU], bres[bB]])
                par = c % 2
                em.op(ACT, _f("activation", out=C_sb[par][:, 0:NT], in_=banks[bC][:, 0:NT], func=AF.Copy),
                      reads=[bres[bC]], writes=[r_Csb[par]])
                up = upb[par]
                for si, (c0, L) in enumerate(segs):
                    o = c0 + 2 * si
                    em.op(DVE, _f("tensor_tensor", out=up[:, o + 2:o + 2 + L], in0=banks[bU][:, c0:c0 + L],
                                                                                                in1=C_sb[par][:, c0:c0 + L], op=ALU.mult),
                          reads=[bres[bU], r_Csb[par]], writes=[r_up[par]])
                    if kind == "p":
                        em.op(POOL, _f("tensor_copy", out=up[:, o:o + 2], in_=carry[:, l, :, c]),
                              reads=[r_carry[l], r_up[par]], writes=[r_up[par]])
                    else:
                        em.op(POOL, _f("tensor_copy", out=up[:, o:o + 2], in_=scarry[:, l, si, :, c]),
                              reads=[r_scarry[l], r_up[par]], writes=[r_up[par]])
                    acc = accb[par]
                    em.op(ACT, _f("activation", out=acc[:, c0:c0 + L], in_=up[:, o + 2:o + 2 + L], func=AF.Copy, scale=cwt[:, 16 + c:17 + c]),
                          reads=[r_up[par], r_par], writes=[r_acc[par]])
                    for k in (1, 0):
                        em.op(DVE, _f("scalar_tensor_tensor",
                            out=acc[:, c0:c0 + L], in0=up[:, o + k:o + k + L], scalar=cwt[:, 8 * k + c:8 * k + c + 1], in1=acc[:, c0:c0 + L],
                            op0=ALU.mult, op1=ALU.add), reads=[r_up[par], r_par, r_acc[par]], writes=[r_acc[par]])
                    if kind == "p":
                        em.op(POOL, _f("tensor_copy", out=carry[:, l, :, c], in_=up[:, o + L:o + L + 2]),
                              reads=[r_up[par], r_carry[l]], writes=[r_carry[l]])
                    else:
                        em.op(POOL, _f("tensor_copy", out=scarry[:, l, si, :, c], in_=up[:, o + L:o + L + 2]),
                              reads=[r_up[par], r_scarry[l]], writes=[r_scarry[l]])
                em.op(DVE, _f("tensor_tensor", out=yc_in[:, c, 0:NT], in0=banks[bB][:, 0:NT], in1=accb[par][:, 0:NT], op=ALU.mult),
                      reads=[bres[bB], r_acc[par]], writes=[r_ycin[c]])
            rg.close(3)
        conv_out = []
        if kind == "p" and tile["t"] == ntile - 1:
            conv_out.append((carry[:, l, :, :].rearrange("p r c -> p (r c)"), pconv[l, tile["b"]], r_carry[l]))
        if kind == "s":
            for i in range(2):
                conv_out.append((scarry[:, l, i, :, :].rearrange("p r c -> p (r c)"), sconv_o[l, i], r_scarry[l]))
        for src, dst, rr in conv_out:
            (bk,) = rot(1)
            em.group([tr(banks[bk][0:16, 0:128], src, identf[:])], reads=[rr, r_const], writes=[bres[bk]])
            em.op(ACT, _f("activation", out=cst[:, :], in_=banks[bk][0:16, 0:128], func=AF.Copy), reads=[bres[bk]], writes=[r_cst])
            em.dma(POOL, s_stc, [(dst.rearrange("r (c p) -> (r c) p", p=128), cst[:, :], {})], reads=[r_cst])
        for j in range(4):
            CO, rCO = rg.next(l, ("CO", j))
            GC, rGC = rg.next(l, ("GC", j))
            for i in range(2):
                c = 2 * j + i
                by, bg_ = rot(2)
                em.group([mm(banks[by][:, 0:NT], CO[:, kc * 256 + i * 128:kc * 256 + (i + 1) * 128], yc_in[:, kc, 0:NT], kc == 0, kc == 7) for kc in range(8)]
                         + [mm(banks[bg_][:, 0:NT], GC[:, kc * 256 + i * 128:kc * 256 + (i + 1) * 128], xT[:, kc, 0:NT], kc == 0, kc == 7) for kc in range(8)],
                         reads=[rCO, rGC] + r_ycin + r_xT[:NSUB], writes=[bres[by], bres[bg_]])
                par = c % 2
                em.op(ACT, _f("activation", out=gate[par][:, 0:NT], in_=banks[bg_][:, 0:NT], func=AF.Sigmoid,
                                                                         bias=bgt[:, c:c + 1], scale=1.0),
                      reads=[bres[bg_], r_par], writes=[r_gate[par]])
                em.op(DVE, _f("tensor_tensor", out=ttb[par][:, 0:NT], in0=banks[by][:, 0:NT], in1=gate[par][:, 0:NT], op=ALU.mult),
                      reads=[bres[by], r_gate[par]], writes=[r_tt[par]])
                em.op(POOL, _f("tensor_tensor", out=mergedT[:, c, 0:NT], in0=ttb[par][:, 0:NT], in1=gmy[:, c, 0:NT], op=ALU.add),
                      reads=[r_tt[par], r_gmy[c]], writes=[r_mT[c]])
            rg.close(2)
        MX = [rg.next(l, ("MX", k)) for k in range(4)]
        mxb = {}

        def grp(s):
            b0, b1 = rot(2)
            mxb[s] = b0
            fns = []
            for half, bk in enumerate((b0, b1)):
                for kc in range(8):
                    fns.append(mm(banks[bk][:, :], mergedT[:, kc, s * 128:(s + 1) * 128], MX[kc // 2][0][:, (kc % 2) * 1024 + half * 512:(kc % 2) * 1024 + (half + 1) * 512],
                                  kc == 0, kc == 7))
            em.group(fns, reads=[m[1] for m in MX] + r_mT, writes=[bres[b0], bres[b1]])
            if s == NSUB - 1:
                rg.close(4)
            residual_add(s, 0, b0)
            residual_add(s, 1, b1)

        ln_pipeline(NSUB, grp, lambda s2: mxb[s2], want_T=True)
        em.set_phase()

    def load_params(l):
        em.dma(SP, s_par, [(qg[:, :], qg_d[l:l + 1, :].partition_broadcast(128), {}),
                           (kvg[:, :], kvg_d[l:l + 1, :].partition_broadcast(128), {}),
                           (bgt[:, :], bg_d[l, :, :], {}),
                           (cwt[:, :], cw_d[l, :, :], {})], writes=[r_par])

    def run_tile(tile):
        NT, NSUB = tile["NT"], tile["NSUB"]
        tab = tile["tab"]
        if tile["kind"] == "p":
            src = xp[tile["b"], tile["t"] * 512:(tile["t"] + 1) * 512, :].rearrange("(s p) c -> p s c", p=128)
            em.dma(SP, s_xld, [(x_res[:, :, :], src, {})], writes=r_xres)
        else:
            em.dma(SP, s_xld, [(x_res[:, 0, :], xs[:, :], {})], writes=[r_xres[0]])
        em.dma(SP, s_tbl, [(cosT[:, :, :], cosT_d[tab].rearrange("p (s c) -> p s c", c=64), {}),
                           (sinT[:, :, :], sinT_d[tab].rearrange("p (s c) -> p s c", c=64), {}),
                           (cosF[:, :], cosF_d[tab], {}), (sinF[:, :], sinF_d[tab], {})], writes=[r_tbl])
        if tile["kind"] == "p" and tile["t"] == 0:
            em.op(POOL, _f("memset", carry[:], 0.0), writes=r_carry)
        em.set_phase()
        for s in range(NSUB):
            make_xT(s)
        for l in range(depth):
            load_params(l)
            ffn(l, 1, tile, 0, last=False)
            mixer(l, tile)
            ffn(l, 2, tile, 2, last=(l == depth - 1))
        if tile["kind"] == "p":
            dst = yp[tile["b"], tile["t"] * 512:(tile["t"] + 1) * 512, :].rearrange("(s p) c -> p s c", p=128)
            em.dma(POOL, s_sty, [(dst, x_res[:, :, :], {})], reads=r_xres)
        else:
            em.dma(POOL, s_sty, [(ys[:, :], x_res[:, 0, :], {})], reads=[r_xres[0]])

    em.set_phase()
    for tile in tiles:
        run_tile(tile)

    if sample:
        em.set_phase()
        em.dma(SP, s_scar, [(scarry[:].rearrange("p l i r c -> p l (i r c)"), sconv.rearrange("l p f -> p l f"), {})], writes=r_scarry)
        for l in range(depth):
            first = True
            for i in range(2):
                for pt in range(2):
                    t0 = pt * 512
                    em.dma(SP, s_cache[0], [(ckv_f[:, :, :], clat[l, i, t0:t0 + 512, :].rearrange("(s p) c -> p s c", p=128), {})], writes=[r_ckvf])
                    em.dma(SP, s_cache[1], [(kr_f[:, :, :], ckr[l, i, t0:t0 + 512, :].rearrange("(s p) c -> p s c", p=128), {})], writes=[r_krf])
                    for s in range(4):
                        cast_lat(s)
                        lat_transposes(s, with_q=False)
                    if not first:
                        rg.consumed -= 2
                    kv_expand(l, 512, 4, [(j * 128, 128) for j in range(4)], store=(npseq + i, t0))
                    first = False
            rg.close(2)
        run_tile(dict(kind="s", NT=128, NSUB=1, tab=ntile))

    em.wait_all(SP, [s_sty, s_stlat, s_stkr, s_stc, s_stK, s_stV, s_stR])

    for s_ in em.sems:
        s_.h = es.enter_context(nc.semaphore(s_.name))
    with nc.Block() as block:
        @block.sync
        def _(e):
            em.replay(SP, e)

        @block.tensor
        def _(e):
            em.replay(PE, e)

        @block.scalar
        def _(e):
            em.replay(ACT, e)

        @block.vector
        def _(e):
            em.replay(DVE, e)

        @block.gpsimd
        def _(e):
            em.replay(POOL, e)
    es.close()
    return nc


def _rope_tables(ntile, with_sample=True):
    half = RD // 2
    inv = (np.float32(10000.0) ** (-np.arange(half, dtype=np.float32) / np.float32(half))).astype(np.float32)
    ntab = ntile + 1
    cosT = np.zeros((ntab, 128, 4, 64), np.float32)
    sinT = np.zeros((ntab, 128, 4, 64), np.float32)
    cosF = np.zeros((ntab, 128, 512), np.float32)
    sinF = np.zeros((ntab, 128, 512), np.float32)
    k = np.arange(64)
    sign = np.where(k < 32, -1.0, 1.0)
    for t in range(ntab):
        if t < ntile:
            pos = t * 512 + np.arange(512)
        else:
            pos = np.concatenate([PAST + np.arange(64), PAST + np.arange(64), np.zeros(384)])
        ang = (pos.astype(np.float32)[:, None] * inv[None, :]).astype(np.float32).astype(np.float64)
        c = np.cos(ang)[:, k % 32]
        s = np.sin(ang)[:, k % 32] * sign[None, :]
        cosT[t] = c.reshape(4, 128, 64).transpose(1, 0, 2)
        sinT[t] = s.reshape(4, 128, 64).transpose(1, 0, 2)
        cosF[t] = np.concatenate([c.T, c.T], axis=0) * ATTN_SCALE
        sinF[t] = np.concatenate([s.T, s.T], axis=0) * ATTN_SCALE
    return cosT.reshape(ntab, 128, 256), sinT.reshape(ntab, 128, 256), cosF, sinF


def _prep_weights(inp, depth):
    w_in = np.ascontiguousarray(inp["w_in"][:depth])
    kr = w_in[:, :, 640:704]
    kr_sw = np.concatenate([kr[:, :, 32:64], kr[:, :, 0:32]], axis=2)
    wtok = np.ascontiguousarray(np.concatenate([w_in[:, :, 0:384], kr, kr_sw, w_in[:, :, 384:640]], axis=2))
    w_uq = inp["w_uq"][:depth]
    uq = np.zeros((depth, QL, 2048), np.float32)
    swap = (np.arange(64) + 32) % 64
    for h in range(NH):
        uq[:, :, h * 128:(h + 1) * 128] = w_uq[:, :, h * 192:h * 192 + 128]
        pr, hh = h // 2, h % 2
        r = w_uq[:, :, h * 192 + 128:h * 192 + 192]
        uq[:, :, 1024 + pr * 128 + hh * 64:1024 + pr * 128 + hh * 64 + 64] = r
        uq[:, :, 1536 + pr * 128 + hh * 64:1536 + pr * 128 + hh * 64 + 64] = r[:, :, swap]
    w_ukv = inp["w_ukv"][:depth]
    ukv = np.zeros((depth, KVL, 2048), np.float32)
    for h in range(NH):
        ukv[:, :, h * 128:(h + 1) * 128] = w_ukv[:, :, h * 256:h * 256 + 128]
        ukv[:, :, 1024 + h * 128:1024 + (h + 1) * 128] = w_ukv[:, :, h * 256 + 128:h * 256 + 256]
    bg = np.ascontiguousarray(inp["b_gate"][:depth].reshape(depth, 2, 8, 128).transpose(0, 3, 1, 2).reshape(depth, 128, 16))
    cw = np.ascontiguousarray(inp["conv_w"][:depth].reshape(depth, 3, 8, 128).transpose(0, 3, 1, 2).reshape(depth, 128, 24))
    c = np.ascontiguousarray
    return {
        "f1gu": c(inp["ffn1_w_gate_up"][:depth]), "f1d": c(inp["ffn1_w_down"][:depth]),
        "f2gu": c(inp["ffn2_w_gate_up"][:depth]), "f2d": c(inp["ffn2_w_down"][:depth]),
        "win": w_in, "wtok": wtok, "uq": uq, "ukv": ukv,
        "mo": c(inp["w_mla_out"][:depth]), "co": c(inp["w_conv_out"][:depth]), "mx": c(inp["w_mix_out"][:depth]),
        "lng": c(inp["ln_gain"][:depth]), "lnb": c(inp["ln_bias"][:depth]),
        "qg": c(inp["q_norm_gain"][:depth]), "kvg": c(inp["kv_norm_gain"][:depth]), "bg": bg, "cw": cw,
    }


def run(inp, depth=4, npseq=2, ntile=8, sample=True, n_cores=8, trace=False):
    import ml_dtypes
    S = ntile * 512
    nc = build_program(depth, npseq, ntile, sample)
    shared = _prep_weights(inp, depth)
    cT, sT, cF, sF = _rope_tables(ntile)
    shared.update(cosT=cT, sinT=sT, cosF=cF, sinF=sF,
                  identb=np.eye(128, dtype=np.float32).astype(ml_dtypes.bfloat16), identf=np.eye(128, dtype=np.float32))
    in_maps = []
    for c in range(n_cores):
        m = dict(shared)
        m["xp"] = np.ascontiguousarray(inp["x_prompt"][c * npseq:(c + 1) * npseq, :S])
        m["xs"] = np.ascontiguousarray(inp["x_sample"][2 * c:2 * c + 2]).reshape(128, D)
        m["clat"] = np.ascontiguousarray(inp["cache_kv_latent"][:depth, 2 * c:2 * c + 2])
        m["ckr"] = np.ascontiguousarray(inp["cache_k_rope"][:depth, 2 * c:2 * c + 2])
        sc = inp["state_conv"][:depth, 2 * c:2 * c + 2]
        m["sconv"] = np.ascontiguousarray(sc.reshape(depth, 2, 2, 8, 128).transpose(0, 4, 1, 2, 3).reshape(depth, 128, 32))
        in_maps.append(m)
    res = run_bass_kernel_spmd(nc, in_maps, core_ids=list(range(n_cores)), **({"trace": True} if trace else {}))
    R = res.results
    y_p = np.concatenate([r["yp"] for r in R], axis=0)
    y_s = np.concatenate([r["ys"].reshape(2, 64, D) for r in R], axis=0)
    p_lat = np.concatenate([r["plat"] for r in R], axis=1)
    p_kr = np.concatenate([r["pkr"] for r in R], axis=1)
    p_conv = np.concatenate([r["pconv"] for r in R], axis=1)
    s_lat = np.concatenate([r["slat"].reshape(depth, 2, 64, KVL) for r in R], axis=1)
    s_kr = np.concatenate([r["skr"].reshape(depth, 2, 64, RD) for r in R], axis=1)
    s_conv = np.concatenate([r["sconv_o"] for r in R], axis=1)
    return (y_p, y_s, p_lat, p_kr, p_conv, s_lat, s_kr, s_conv), res


def kernel(**inputs):
    inp = {k: np.asarray(v) for k, v in inputs.items()}
    outs, _ = run(inp)
    return tuple(np.ascontiguousarray(o, dtype=np.float32) for o in outs)
```

```python
import numpy as np
from contextlib import ExitStack
import concourse.bass as bass
import concourse.mybir as mybir
from concourse.bass_utils import run_bass_kernel_spmd

F32 = mybir.dt.float32
BF16 = mybir.dt.bfloat16
AF = mybir.ActivationFunctionType
ALU = mybir.AluOpType

D = 1024
DFF = 2816
NH = 8
QL = 384
KVL = 256
RD = 64
PAST = 1024
EPS = 1e-5
ATTN_SCALE = float((128 + 64) ** -0.5)
NSLOT = 8
BLK = 2048


def _f(name, *a, **kw):
    return lambda e: getattr(e, name)(*a, **kw)

class Sem:
    def __init__(self, name):
        self.name = name
        self.h = None
        self.count = 0


class Eng:
    def __init__(self, name, sem):
        self.name = name
        self.sem = sem
        self.prog = []
        self.seen = {}


class Res:
    __slots__ = ("name", "w", "r", "arena")

    def __init__(self, name, arena=False):
        self.name = name
        self.w = None
        self.r = {}
        self.arena = arena


class Emitter:
    def __init__(self):
        self.sems = []
        self.pe = Eng("pe", self.sem("e_pe"))
        self.act = Eng("act", self.sem("e_act"))
        self.dve = Eng("dve", self.sem("e_dve"))
        self.pool = Eng("pool", self.sem("e_pool"))
        self.sp = Eng("sp", None)
        self.barrier = {}
        self.nobarrier = set()
        self.arena_tok = {}

    def sem(self, name):
        s = Sem(name)
        self.sems.append(s)
        return s

    def set_phase(self):
        self.barrier = dict(self.arena_tok)

    def _deps(self, E, reads, writes):
        need = {}

        def add(sem, v):
            if need.get(sem, 0) < v:
                need[sem] = v

        arena = False
        for r in reads:
            if r.w is not None:
                add(*r.w)
            arena = arena or r.arena
        for w in writes:
            if w.w is not None:
                add(*w.w)
            for sem, v in w.r.items():
                add(sem, v)
            arena = arena or w.arena
        if arena:
            for sem, v in self.barrier.items():
                add(sem, v)
        waits = []
        for sem, v in need.items():
            if sem is E.sem:
                if E.name == "pe":
                    continue
            if E.seen.get(sem, 0) >= v:
                continue
            E.seen[sem] = v
            waits.append((sem, v))
        return waits

    def _finish(self, tok, reads, writes):
        for x in list(reads) + list(writes):
            if x.arena:
                if self.arena_tok.get(tok[0], 0) < tok[1]:
                    self.arena_tok[tok[0]] = tok[1]
                break
        for w in writes:
            w.w = tok
            w.r = {}
        for r in reads:
            if r in writes:
                continue
            if r.r.get(tok[0], 0) < tok[1]:
                r.r[tok[0]] = tok[1]

    def op(self, E, fn, reads=(), writes=()):
        waits = self._deps(E, reads, writes)
        E.sem.count += 1
        tok = (E.sem, E.sem.count)
        E.prog.append((waits, fn, (E.sem, 1)))
        self._finish(tok, reads, writes)
        return tok

    def group(self, fns, reads=(), writes=()):
        E = self.pe
        waits = self._deps(E, reads, writes)
        E.sem.count += 1
        tok = (E.sem, E.sem.count)
        n = len(fns)
        for i, f in enumerate(fns):
            E.prog.append((waits if i == 0 else (), f, (E.sem, 1) if i == n - 1 else None))
        self._finish(tok, reads, writes)
        return tok

    def dma(self, Q, dsem, items, reads=(), writes=()):
        waits = self._deps(Q, reads, writes)
        for i, (o, a, kw) in enumerate(items):
            dsem.count += 16
            Q.prog.append((waits if i == 0 else (), (_f("dma_start", out=o, in_=a, **kw)), (dsem, 16)))
        tok = (dsem, dsem.count)
        self._finish(tok, reads, writes)
        return tok

    def wait_all(self, E, sems):
        waits = [(s, s.count) for s in sems if s.count > 0]
        E.prog.append((waits, None, None))

    def replay(self, E, eng):
        for waits, fn, sig in E.prog:
            for sem, v in waits:
                eng.wait_ge(sem.h, v)
            if fn is None:
                continue
            ins = fn(eng)
            if sig is not None:
                ins.then_inc(sig[0].h, sig[1])


def layer_blocks():
    bl = []

    def ffn(w):
        gu, dn = ("f%dgu" % w, "f%dd" % w)
        for j in range(11):
            bl.append((("G", w, j), gu, 0, 8, j * 256, 256))
            bl.append((("U", w, j), gu, 0, 8, DFF + j * 256, 256))
        for half in range(2):
            for k in range(6):
                nk = 4 if k < 5 else 2
                bl.append((("D", w, half, k), dn, k * 4, nk, half * 512, 512))

    ffn(1)
    for i in range(3):
        bl.append((("T", i), "wtok", 0, 8, i * 256, 256))
    for i in range(4):
        bl.append((("Q", i), "uq", 0, 3, i * 512, 512))
    for i in range(2):
        bl.append((("KV", i), "ukv", 0, 2, i * 1024, 1024))
    for j in range(4):
        bl.append((("CC", j), "win", 0, 8, 1728 + j * 256, 256))
        bl.append((("CU", j), "win", 0, 8, 2752 + j * 256, 256))
        bl.append((("CB", j), "win", 0, 8, 704 + j * 256, 256))
    for j in range(4):
        bl.append((("MO", j), "mo", 0, 8, j * 256, 256))
        bl.append((("GM", j), "win", 0, 8, 4800 + j * 256, 256))
    for j in range(4):
        bl.append((("CO", j), "co", 0, 8, j * 256, 256))
        bl.append((("GC", j), "win", 0, 8, 3776 + j * 256, 256))
    for k in range(4):
        bl.append((("MX", k), "mx", k * 2, 2, 0, 1024))
    ffn(2)
    return bl


LBLOCKS = layer_blocks()
NB = len(LBLOCKS)
BIDX = {b[0]: i for i, b in enumerate(LBLOCKS)}


def build_program(depth=4, npseq=2, ntile=8, sample=True):
    S = ntile * 512
    ntab = ntile + 1
    nc = bass.Bass("TRN2", target_bir_lowering=False)
    em = Emitter()
    PE, ACT, DVE, POOL, SP = em.pe, em.act, em.dve, em.pool, em.sp

    def din(name, shape, dt=F32):
        return nc.dram_tensor(name, list(shape), dt, kind="ExternalInput").ap()

    def dout(name, shape, dt=F32):
        return nc.dram_tensor(name, list(shape), dt, kind="ExternalOutput").ap()

    def dscr(name, shape, dt=BF16):
        return nc.dram_tensor(name, list(shape), dt).ap()

    xp = din("xp", [npseq, S, D])
    xs = din("xs", [128, D])
    clat = din("clat", [depth, 2, PAST, KVL])
    ckr = din("ckr", [depth, 2, PAST, RD])
    sconv = din("sconv", [depth, 128, 32])
    W = {
        "f1gu": din("f1gu", [depth, D, 2 * DFF]), "f1d": din("f1d", [depth, DFF, D]),
        "f2gu": din("f2gu", [depth, D, 2 * DFF]), "f2d": din("f2d", [depth, DFF, D]),
        "win": din("win", [depth, D, 5824]), "wtok": din("wtok", [depth, D, 768]),
        "uq": din("uq", [depth, QL, 2048]), "ukv": din("ukv", [depth, KVL, 2048]),
        "mo": din("mo", [depth, D, D]), "co": din("co", [depth, D, D]), "mx": din("mx", [depth, D, D]),
    }
    lng = din("lng", [depth, 3, D])
    lnb = din("lnb", [depth, 3, D])
    qg_d = din("qg", [depth, QL])
    kvg_d = din("kvg", [depth, KVL])
    bg_d = din("bg", [depth, 128, 16])
    cw_d = din("cw", [depth, 128, 24])
    cosT_d = din("cosT", [ntab, 128, 4 * 64])
    sinT_d = din("sinT", [ntab, 128, 4 * 64])
    cosF_d = din("cosF", [ntab, 128, 512])
    sinF_d = din("sinF", [ntab, 128, 512])
    identb_d = din("identb", [128, 128], BF16)
    identf_d = din("identf", [128, 128])

    yp = dout("yp", [npseq, S, D])
    ys = dout("ys", [128, D])
    plat = dout("plat", [depth, npseq, S, KVL])
    pkr = dout("pkr", [depth, npseq, S, RD])
    pconv = dout("pconv", [depth, npseq, 2, D])
    slat = dout("slat", [depth, 128, KVL])
    skr = dout("skr", [depth, 128, RD])
    sconv_o = dout("sconv_o", [depth, 2, 2, D])

    tape = dscr("tape", [depth, NB, 128, BLK])
    Kscr = dscr("Kscr", [depth, npseq, NH, 128, S])
    Vscr = dscr("Vscr", [depth, npseq, S, NH * 128])
    Rscr = dscr("Rscr", [depth, npseq, 128, S])
    KscrS = dscr("KscrS", [depth, 2, NH, 128, PAST])
    VscrS = dscr("VscrS", [depth, 2, PAST, NH * 128])
    RscrS = dscr("RscrS", [depth, 2, 128, PAST])

    def scr(slot):
        if slot < npseq:
            return Kscr, Vscr, Rscr, slot
        return KscrS, VscrS, RscrS, slot - npseq

    es = ExitStack()

    def sb(name, shape, dt):
        return es.enter_context(nc.sbuf_tensor(name, list(shape), dt))

    ring = sb("ring", [128, NSLOT * BLK], BF16)
    x_res = sb("x_res", [128, 4, D], F32)
    xT = sb("xT", [128, 8, 512], BF16)
    OT = sb("OT", [128, 8, 512], BF16)
    lnp = sb("lnp", [128, 2 * D], F32)
    cosT = sb("cosT_s", [128, 4, 64], F32)
    sinT = sb("sinT_s", [128, 4, 64], F32)
    cosF = sb("cosF_s", [128, 512], F32)
    sinF = sb("sinF_s", [128, 512], F32)
    qg = sb("qg_s", [128, QL], F32)
    kvg = sb("kvg_s", [128, KVL], F32)
    bgt = sb("bg_s", [128, 16], F32)
    cwt = sb("cw_s", [128, 24], F32)
    identb = sb("identb_s", [128, 128], BF16)
    identf = sb("identf_s", [128, 128], F32)
    ones = sb("ones_s", [128, 128], BF16)
    onesf = sb("onesf_s", [128, 128], F32)
    carry = sb("carry", [128, depth, 2, 8], F32)
    scarry = sb("scarry", [128, depth, 2, 2, 8], F32)
    cst = sb("cst", [16, 128], F32)
    stats = sb("stats", [128, 2, 2, 6], F32)
    mv = sb("mv", [128, 2, 4], F32)
    ssq = sb("ssq", [128, 2, 4], F32)
    junk = sb("junk", [128, 512], F32)
    rtmp = sb("rtmp", [128, 2, 2, 64], F32)

    ARENA_BYTES = 100 * 1024
    arena = sb("arena", [128, ARENA_BYTES // 4], F32)

    class Carver:
        def __init__(self):
            self.off = 0

        def take(self, free_shape, dt):
            n = int(np.prod(free_shape))
            esz = 4 if dt is F32 else 2
            nbytes = (n * esz + 31) // 32 * 32
            assert self.off + nbytes <= ARENA_BYTES, (self.off, nbytes)
            ap = arena[:, self.off // 4:(self.off + n * esz) // 4]
            if dt is BF16:
                ap = ap.bitcast(BF16)
            if len(free_shape) == 2:
                ap = ap.rearrange("p (a b) -> p a b", b=free_shape[1])
            elif len(free_shape) == 3:
                ap = ap.rearrange("p (a b c) -> p a b c", b=free_shape[1], c=free_shape[2])
            self.off += nbytes
            return ap

    cv = Carver()
    NPP = 3
    pp_stage = [cv.take([BLK], F32) for _ in range(NPP)]
    pp_out = [cv.take([BLK], BF16) for _ in range(NPP)]
    cv = Carver()
    hT = cv.take([22, 512], BF16)
    silu_b = [cv.take([512], F32) for _ in range(2)]
    zbuf = [sb("zbuf%d" % i, [128, D], F32) for i in range(2)]
    xb = [sb("xb%d" % i, [128, D], BF16) for i in range(2)]
    cv = Carver()
    ckv_f = cv.take([4, KVL], F32)
    kr_f = cv.take([4, RD], F32)
    qn_b = cv.take([4, QL], BF16)
    ckv_b = cv.take([4, KVL], BF16)
    krd_b = cv.take([4, 128], BF16)
    latT = cv.take([6, 512], BF16)
    QnT = cv.take([8, 512], BF16)
    QrT = cv.take([8, 512], BF16)
    ropet = [cv.take([512], F32) for _ in range(2)]
    KT_own = cv.take([8, 512], BF16)
    V_own = cv.take([4, D], BF16)
    KpT = [cv.take([3584], BF16) for _ in range(2)]
    Vp = [cv.take([28, 128], BF16) for _ in range(2)]
    krT_past = cv.take([3584], BF16)
    PT = [cv.take([512], BF16) for _ in range(3)]
    recip = cv.take([512], F32)
    Lacc = [cv.take([512], F32) for _ in range(2)]
    cv = Carver()
    gmy = cv.take([8, 512], F32)
    C_sb = [cv.take([512], F32) for _ in range(2)]
    upb = [cv.take([520], F32) for _ in range(2)]
    accb = [cv.take([512], F32) for _ in range(2)]
    yc_in = cv.take([8, 512], BF16)
    gate = [cv.take([512], F32) for _ in range(2)]
    ttb = [cv.take([512], F32) for _ in range(2)]
    mergedT = cv.take([8, 512], BF16)

    banks = [es.enter_context(nc.psum_tensor("bank%d" % i, [128, 512], F32)) for i in range(8)]
    bres = [Res("bank%d" % i) for i in range(8)]
    rot_state = [0]

    rot_pool = [4]

    def rot(n=1):
        P = rot_pool[0]
        if n == 2:
            if rot_state[0] % 2:
                rot_state[0] += 1
        ids = [(rot_state[0] + i) % P for i in range(n)]
        rot_state[0] = (rot_state[0] + n) % P
        return ids

    ring_sem = [em.sem("ring%d" % i) for i in range(NSLOT)]
    for s_ in ring_sem:
        em.nobarrier.add(s_)
    ring_res = [Res("ring%d" % i) for i in range(NSLOT)]
    s_xld = em.sem("xld")
    s_tbl = em.sem("tbl")
    s_lnp = em.sem("lnp")
    s_par = em.sem("par")
    s_kvp = [em.sem("kvp0"), em.sem("kvp1")]
    s_krp = em.sem("krp")
    s_cache = [em.sem("cache0"), em.sem("cache1")]
    s_scar = em.sem("scar")
    s_sty = em.sem("sty")
    s_stlat = em.sem("stlat")
    s_stkr = em.sem("stkr")
    s_stK = em.sem("stK")
    s_stV = em.sem("stV")
    s_stR = em.sem("stR")
    s_stc = em.sem("stc")
    s_ppin = [em.sem("ppin%d" % i) for i in range(NPP)]
    s_ppout = [em.sem("ppout%d" % i) for i in range(NPP)]
    s_const = em.sem("const")

    R = {}

    def res(name, arena=False):
        if name not in R:
            R[name] = Res(name, arena)
        return R[name]

    r_tape = res("tape")
    r_xres = [res("xres%d" % s) for s in range(4)]
    r_xT = [res("xT%d" % s) for s in range(4)]
    r_lnp = res("lnp")
    r_tbl = res("tbl")
    r_par = res("par")
    r_const = res("const")
    r_OT = [res("OT%d" % h) for h in range(8)]
    r_carry = [res("carry%d" % l) for l in range(depth)]
    r_scarry = [res("scarry%d" % l) for l in range(depth)]
    r_cst = res("cst")
    r_stats = [res("stats0"), res("stats1")]
    r_ssq = [res("ssq0"), res("ssq1")]
    r_junk = res("junk")
    r_rtmp = [res("rtmp0"), res("rtmp1")]
    r_scrK = {}
    r_out = res("out_dram")

    def ar(name):
        return res(name, arena=True)

    r_ppst = [ar("ppst%d" % i) for i in range(NPP)]
    r_ppo = [ar("ppo%d" % i) for i in range(NPP)]
    r_hT = [ar("hT%d" % c) for c in range(22)]
    r_silu = [ar("silu0"), ar("silu1")]
    r_z = [res("z0"), res("z1")]
    r_xb = [res("xb0"), res("xb1")]
    r_ckvf = ar("ckvf")
    r_krf = ar("krf")
    r_qnb = [ar("qnb%d" % s) for s in range(4)]
    r_ckvb = [ar("ckvb%d" % s) for s in range(4)]
    r_krdb = [ar("krdb%d" % s) for s in range(4)]
    r_latT = [ar("latT%d" % s) for s in range(4)]
    r_QnT = [ar("QnT%d" % h) for h in range(8)]
    r_QrT = [ar("QrT%d" % p) for p in range(8)]
    r_ropet = [ar("ropet0"), ar("ropet1")]
    r_KT = [ar("KT%d" % h) for h in range(8)]
    r_V = [ar("V%d" % s) for s in range(4)]
    r_KpT = [ar("KpT0"), ar("KpT1")]
    r_krp = ar("krTpast")
    r_PT = [ar("PT%d" % i) for i in range(3)]
    r_recip = ar("recip")
    r_Lacc = [ar("Lacc0"), ar("Lacc1")]
    r_gmy = [ar("gmy%d" % c) for c in range(8)]
    r_Csb = [ar("Csb0"), ar("Csb1")]
    r_up = [ar("up0"), ar("up1")]
    r_acc = [ar("acc0"), ar("acc1")]
    r_ycin = [ar("ycin%d" % c) for c in range(8)]
    r_gate = [ar("gate0"), ar("gate1")]
    r_tt = [ar("tt0"), ar("tt1")]
    r_mT = [ar("mT%d" % c) for c in range(8)]

    em.dma(SP, s_const, [(identb[:], identb_d[:, :], {}), (identf[:], identf_d[:, :], {})], writes=[r_const])
    em.op(POOL, _f("memset", ones[:], 1.0), writes=[r_const])
    em.op(POOL, _f("memset", onesf[:], 1.0), writes=[r_const])

    cnt = 0
    for l in range(depth):
        for bi, (name, wk, k0, nk, c0, ncols) in enumerate(LBLOCKS):
            i = cnt % NPP
            width = nk * ncols
            src = W[wk][l, k0 * 128:(k0 + nk) * 128, c0:c0 + ncols].rearrange("(k p) c -> p k c", p=128)
            dst = pp_stage[i][:, 0:width].rearrange("p (k c) -> p k c", c=ncols)
            em.dma(SP, s_ppin[i], [(dst, src, {})], writes=[r_ppst[i]])
            if cnt % 2 == 0:
                em.op(ACT, _f("activation", out=pp_out[i][:, 0:width], in_=pp_stage[i][:, 0:width], func=AF.Copy),
                      reads=[r_ppst[i]], writes=[r_ppo[i]])
            else:
                em.op(DVE, _f("tensor_copy", out=pp_out[i][:, 0:width], in_=pp_stage[i][:, 0:width]),
                      reads=[r_ppst[i]], writes=[r_ppo[i]])
            em.dma(POOL, s_ppout[i], [(tape[l, bi, :, 0:width], pp_out[i][:, 0:width], {})], reads=[r_ppo[i]], writes=[r_tape])
            cnt += 1
    em.wait_all(SP, s_ppout)
    r_tape.w = None

    tiles = []
    for b in range(npseq):
        for t in range(ntile):
            tiles.append(dict(kind="p", b=b, t=t, NT=512, NSUB=4, tab=t))
    seq = []
    for _ in tiles:
        for l in range(depth):
            seq += [(l, bi) for bi in range(NB)]
    if sample:
        for l in range(depth):
            seq += [(l, BIDX[("KV", 0)]), (l, BIDX[("KV", 1)])]
        for l in range(depth):
            seq += [(l, bi) for bi in range(NB)]

    class Ring:
        def __init__(self):
            self.loaded = 0
            self.consumed = 0
            self.closed = 0

        def _pump(self):
            while self.loaded < min(len(seq), self.closed + NSLOT):
                k = self.loaded
                l, bi = seq[k]
                _, wk, k0, nk, c0, ncols = LBLOCKS[bi]
                width = nk * ncols
                s = k % NSLOT
                em.dma(SP, ring_sem[s], [(ring[:, s * BLK:s * BLK + width], tape[l, bi, :, 0:width], {})],
                       reads=[r_tape], writes=[ring_res[s]])
                self.loaded += 1

        def next(self, l, name):
            k = self.consumed
            assert seq[k] == (l, BIDX[name]), (seq[k], l, name)
            self._pump()
            assert k < self.loaded
            self.consumed += 1
            s = k % NSLOT
            return ring[:, s * BLK:(s + 1) * BLK], ring_res[s]

        def close(self, n=1):
            self.closed += n
            self._pump()

    rg = Ring()

    ALPHA = float((2 * 4) ** 0.25)

    def mm(out, lhsT, rhs, start, stop):
        return _f("matmul", out, lhsT, rhs, start=start, stop=stop)

    def tr(out, in_, ident):
        return _f("transpose", out, in_, ident)

    alt = [0]

    def evac_copy(out, in_, reads, writes, scale=None):
        alt[0] += 1
        if alt[0] % 2 == 0:
            if scale is None:
                em.op(ACT, _f("activation", out=out, in_=in_, func=AF.Copy), reads=reads, writes=writes)
            else:
                em.op(ACT, _f("activation", out=out, in_=in_, func=AF.Copy, scale=scale), reads=reads, writes=writes)
        else:
            if scale is None:
                em.op(DVE, _f("tensor_copy", out=out, in_=in_), reads=reads, writes=writes)
            else:
                em.op(DVE, _f("tensor_scalar", out=out, in0=in_, scalar1=scale, scalar2=None, op0=ALU.mult), reads=reads, writes=writes)

    pendT = []

    def cast_x(s):
        par = s % 2
        em.op(ACT, _f("activation", out=xb[par][:, :], in_=x_res[:, s, :], func=AF.Copy), reads=[r_xres[s]], writes=[r_xb[par]])

    def flush_T(keep=0):
        while len(pendT) > keep:
            emit_T(pendT.pop(0))

    def make_xT(s, bank=None):
        cast_x(s)
        emit_T(s, bank)

    def emit_T(s, bank=None):
        par = s % 2
        if bank is None:
            (bk,) = rot(1)
        else:
            bk = bank
        pb = banks[bk][:].bitcast(BF16)
        em.group([tr(pb[:, kc * 128:(kc + 1) * 128], xb[par][:, kc * 128:(kc + 1) * 128], identb[:]) for kc in range(8)],
                 reads=[r_xb[par], r_const], writes=[bres[bk]])
        evac_copy(xT[:, :, s * 128:(s + 1) * 128], pb.rearrange("p (a b) -> p a b", b=128), [bres[bk]], [r_xT[s]])

    def load_lnp(l, i):
        em.dma(SP, s_lnp, [(lnp[:, 0:D], lng[l, i:i + 1, :].partition_broadcast(128), {}),
                           (lnp[:, D:2 * D], lnb[l, i:i + 1, :].partition_broadcast(128), {})], writes=[r_lnp])

    lnst = sb("lnst", [128, 4, 8], F32)
    junk2 = sb("junk2", [128, D], BF16)
    r_lnst = [res("lnst%d" % i) for i in range(4)]
    r_junk2 = res("junk2")

    def ln_A(s):
        st = lnst[:, s, :]
        em.op(ACT, _f("activation", out=junk2[:, :], in_=x_res[:, s, :], func=AF.Square, accum_out=lnst[:, s, 2:3]),
              reads=[r_xres[s]], writes=[r_junk2, r_lnst[s]])
        em.op(ACT, _f("activation", out=lnst[:, s, 3:4], in_=lnst[:, s, 0:1], func=AF.Identity, bias=lnst[:, s, 1:2], scale=1.0),
              reads=[r_lnst[s]], writes=[r_lnst[s]])
        em.op(ACT, _f("activation", out=lnst[:, s, 4:5], in_=lnst[:, s, 3:4], func=AF.Copy, scale=-1.0 / (D * D)),
              reads=[r_lnst[s]], writes=[r_lnst[s]])
        em.op(ACT, _f("activation", out=lnst[:, s, 5:6], in_=lnst[:, s, 3:4], func=AF.Identity, bias=EPS_AP[:, 0:1], scale=lnst[:, s, 4:5]),
              reads=[r_lnst[s], r_const], writes=[r_lnst[s]])
        em.op(ACT, _f("activation", out=lnst[:, s, 6:7], in_=lnst[:, s, 2:3], func=AF.Sqrt, bias=lnst[:, s, 5:6], scale=1.0 / D),
              reads=[r_lnst[s]], writes=[r_lnst[s]])

    def ln_B(s):
        par = s % 2
        em.op(DVE, _f("reciprocal", out=lnst[:, s, 7:8], in_=lnst[:, s, 6:7]), reads=[r_lnst[s]], writes=[r_lnst[s]])
        em.op(DVE, _f("scalar_tensor_tensor", out=lnst[:, s, 4:5], in0=lnst[:, s, 3:4], scalar=-1.0 / D, in1=lnst[:, s, 7:8],
                      op0=ALU.mult, op1=ALU.mult), reads=[r_lnst[s]], writes=[r_lnst[s]])
        em.op(ACT, _f("activation", out=zbuf[par][:, :], in_=x_res[:, s, :], func=AF.Identity, bias=lnst[:, s, 4:5], scale=lnst[:, s, 7:8]),
              reads=[r_xres[s], r_lnst[s]], writes=[r_z[par]])

    def ln_C(s, want_cast):
        par = s % 2
        em.op(DVE, _f("tensor_tensor", out=zbuf[par][:, :], in0=zbuf[par][:, :], in1=lnp[:, 0:D], op=ALU.mult),
              reads=[r_z[par], r_lnp], writes=[r_z[par]])
        if want_cast:
            em.op(DVE, _f("tensor_tensor", out=xb[par][:, :], in0=zbuf[par][:, :], in1=lnp[:, D:2 * D], op=ALU.add),
                  reads=[r_z[par], r_lnp], writes=[r_xb[par]])
        em.op(POOL, _f("tensor_tensor", out=x_res[:, s, :], in0=zbuf[par][:, :], in1=lnp[:, D:2 * D], op=ALU.add),
              reads=[r_z[par], r_lnp], writes=[r_xres[s]])

    def ln_pipeline(NSUB, emit_group, tbank, want_T):
        for step in range(NSUB + 3):
            if step < NSUB:
                emit_group(step)
                ln_A(step)
            if 0 <= step - 1 < NSUB:
                ln_B(step - 1)
            if 0 <= step - 2 < NSUB:
                ln_C(step - 2, want_T)
            if want_T and 0 <= step - 3 < NSUB - 1:
                emit_T(step - 3, tbank(step - 3))
        if want_T:
            pendT.append(NSUB - 1)

    epsb = sb("epsb", [128, 1], F32)
    EPS_AP = epsb
    em.op(POOL, _f("memset", epsb[:], EPS), writes=[r_const])

    def residual_add(s, half, bk):
        em.op(DVE, _f("scalar_tensor_tensor", out=x_res[:, s, half * 512:(half + 1) * 512],
                      in0=x_res[:, s, half * 512:(half + 1) * 512], scalar=ALPHA,
                      in1=banks[bk][:, :], op0=ALU.mult, op1=ALU.add, accum_out=lnst[:, s, half:half + 1]),
              reads=[r_xres[s], bres[bk]], writes=[r_xres[s], r_lnst[s]])

    def ffn(l, w, tile, ln_idx, last):
        NT, NSUB = tile["NT"], tile["NSUB"]
        load_lnp(l, ln_idx)
        rot_pool[0] = 8
        split = (NSUB == 4 and len(pendT) > 0)
        if not split:
            flush_T()
        deferred = []

        def gu_mms(G, U, i, bg_, bu_, t0, t1):
            fns = [mm(banks[bg_][:, t0:t1], G[:, kc * 256 + i * 128:kc * 256 + (i + 1) * 128], xT[:, kc, t0:t1], kc == 0, kc == 7)
                   for kc in range(8)]
            fns += [mm(banks[bu_][:, t0:t1], U[:, kc * 256 + i * 128:kc * 256 + (i + 1) * 128], xT[:, kc, t0:t1], kc == 0, kc == 7)
                    for kc in range(8)]
            return fns

        def gu_evac(c, bg_, bu_):
            par = c % 2
            em.op(ACT, _f("activation", out=silu_b[par][:, 0:NT], in_=banks[bg_][:, 0:NT], func=AF.Silu),
                  reads=[bres[bg_]], writes=[r_silu[par]])
            em.op(DVE, _f("scalar_tensor_tensor", out=hT[:, c, 0:NT], in0=silu_b[par][:, 0:NT], scalar=0.5,
                          in1=banks[bu_][:, 0:NT], op0=ALU.mult, op1=ALU.mult),
                  reads=[r_silu[par], bres[bu_]], writes=[r_hT[c]])

        for j in range(11):
            G, rG = rg.next(l, ("G", w, j))
            U, rU = rg.next(l, ("U", w, j))
            for i in range(2):
                c = 2 * j + i
                bg_, bu_ = rot(2)
                if split and c < 3:
                    em.group(gu_mms(G, U, i, bg_, bu_, 0, 384), reads=[rG, rU] + r_xT[0:3], writes=[bres[bg_], bres[bu_]])
                    deferred.append((c, G, U, rG, rU, i, bg_, bu_))
                    if c == 2:
                        flush_T()
                        for (c2, G2, U2, rG2, rU2, i2, bg2, bu2) in deferred:
                            em.group(gu_mms(G2, U2, i2, bg2, bu2, 384, 512), reads=[rG2, rU2, r_xT[3]], writes=[bres[bg2], bres[bu2]])
                            gu_evac(c2, bg2, bu2)
                else:
                    em.group(gu_mms(G, U, i, bg_, bu_, 0, NT), reads=[rG, rU] + r_xT[:NSUB], writes=[bres[bg_], bres[bu_]])
                    gu_evac(c, bg_, bu_)
            if split and j == 0:
                pass
            elif split and j == 1:
                rg.close(4)
            else:
                rg.close(2)
        rot_pool[0] = 4
        for k in range(6):
            Dk, rD = rg.next(l, ("D", w, 0, k))
            nk = 4 if k < 5 else 2
            fns = []
            for kk in range(nk):
                kc = 4 * k + kk
                for s in range(NSUB):
                    fns.append(mm(banks[4 + s][:, :], hT[:, kc, s * 128:(s + 1) * 128], Dk[:, kk * 512:(kk + 1) * 512], kc == 0, kc == 21))
            em.group(fns, reads=[rD] + r_hT[4 * k:4 * k + nk], writes=[bres[4 + s] for s in range(NSUB)])
            rg.close(1)
        Dh = [rg.next(l, ("D", w, 1, k)) for k in range(6)]
        def grp(s):
            fns = []
            for kc in range(22):
                Dk = Dh[kc // 4][0]
                kk = kc % 4
                fns.append(mm(banks[s][:, :], hT[:, kc, s * 128:(s + 1) * 128], Dk[:, kk * 512:(kk + 1) * 512], kc == 0, kc == 21))
            em.group(fns, reads=[d[1] for d in Dh] + r_hT, writes=[bres[s]])
            if s == NSUB - 1:
                rg.close(6)
            residual_add(s, 0, 4 + s)
            residual_add(s, 1, s)

        ln_pipeline(NSUB, grp, lambda s2: 4 + s2, want_T=not last)

    def lat_transposes(s, with_q):
        (bk,) = rot(1)
        pb = banks[bk][:].bitcast(BF16)
        fns = []
        reads = [r_ckvb[s], r_krdb[s], r_const]
        if with_q:
            reads.append(r_qnb[s])
            for c in range(3):
                fns.append(tr(pb[:, c * 128:(c + 1) * 128], qn_b[:, s, c * 128:(c + 1) * 128], identb[:]))
        for c in range(2):
            fns.append(tr(pb[:, 384 + c * 128:384 + (c + 1) * 128], ckv_b[:, s, c * 128:(c + 1) * 128], identb[:]))
        fns.append(tr(pb[:, 640:768], krd_b[:, s, :], identb[:]))
        em.group(fns, reads=reads, writes=[bres[bk]])
        if with_q:
            evac_copy(latT[:, :, s * 128:(s + 1) * 128], pb[:, 0:768].rearrange("p (a b) -> p a b", b=128), [bres[bk]], [r_latT[s]])
        else:
            evac_copy(latT[:, 3:6, s * 128:(s + 1) * 128], pb[:, 384:768].rearrange("p (a b) -> p a b", b=128), [bres[bk]], [r_latT[s]])

    def cast_lat(s):
        em.op(POOL, _f("tensor_copy", out=ckv_b[:, s, :], in_=ckv_f[:, s, :]), reads=[r_ckvf], writes=[r_ckvb[s]])
        em.op(POOL, _f("tensor_copy", out=krd_b[:, s, 0:64], in_=kr_f[:, s, :]), reads=[r_krf], writes=[r_krdb[s]])
        em.op(POOL, _f("tensor_copy", out=krd_b[:, s, 64:128], in_=kr_f[:, s, :]), reads=[r_krf], writes=[r_krdb[s]])

    def kv_expand(l, NT, NSUB, vblocks, store):
        KV0, rK0 = rg.next(l, ("KV", 0))
        KV1, rK1 = rg.next(l, ("KV", 1))
        for h in range(NH):
            (bk,) = rot(1)
            em.group([mm(banks[bk][:, 0:NT], KV0[:, kc * 1024 + h * 128:kc * 1024 + (h + 1) * 128], latT[:, 3 + kc, 0:NT], kc == 0, kc == 1)
                      for kc in range(2)], reads=[rK0] + r_latT[:NSUB], writes=[bres[bk]])
            evac_copy(KT_own[:, h, 0:NT], banks[bk][:, 0:NT], [bres[bk]], [r_KT[h]])
        for vb, (c0, nk) in enumerate(vblocks):
            b0, b1 = rot(2)
            fns = []
            for half, bk in enumerate((b0, b1)):
                for kc in range(2):
                    fns.append(mm(banks[bk][0:nk, :], latT[:, 3 + kc, c0:c0 + nk], KV1[:, kc * 1024 + half * 512:kc * 1024 + (half + 1) * 512], kc == 0, kc == 1))
            em.group(fns, reads=[rK1] + r_latT[:NSUB], writes=[bres[b0], bres[b1]])
            for half, bk in enumerate((b0, b1)):
                evac_copy(V_own[0:nk, vb, half * 512:(half + 1) * 512], banks[bk][0:nk, :], [bres[bk]], [r_V[vb]])
        if store is not None:
            slot, t0 = store
            Ks, Vs, Rs, si = scr(slot)
            key = (l, slot)
            rk = r_scrK.setdefault(key, [Res("scrK"), Res("scrV"), Res("scrR")])
            em.dma(POOL, s_stK, [(Ks[l, si, :, :, t0:t0 + NT].rearrange("h d t -> d h t"), KT_own[:, :, 0:NT], {})],
                   reads=r_KT, writes=[rk[0]])
            em.dma(POOL, s_stV, [(Vs[l, si, t0:t0 + NT, :].rearrange("(s p) c -> p s c", p=128), V_own[:, 0:NSUB, :], {})],
                   reads=r_V[:NSUB], writes=[rk[1]])
            em.dma(POOL, s_stR, [(Rs[l, si, :, t0:t0 + NT], latT[:, 5, 0:NT], {})], reads=r_latT[:NSUB], writes=[rk[2]])

    hcount = [0]

    def load_past(l, slot, h, n_past, buf):
        Ks, Vs, Rs, si = scr(slot)
        rk = r_scrK[(l, slot)]
        nkb = n_past // 128
        em.dma(SP, s_kvp[buf],
               [(KpT[buf][:, 0:n_past], Ks[l, si, h, :, 0:n_past], {}),
                (Vp[buf][:, 0:nkb, :], Vs[l, si, 0:n_past, h * 128:(h + 1) * 128].rearrange("(k p) v -> p k v", p=128), {})],
               reads=[rk[0], rk[1]], writes=[r_KpT[buf]])

    def attention_prefetch(l, jobs):
        pairs = [(ji, h) for ji in range(len(jobs)) for h in range(NH)]

        def prefetch(idx):
            if idx < len(pairs):
                ji, h = pairs[idx]
                jb = jobs[ji]
                if jb["n_past"] > 0:
                    if idx == 0:
                        load_krp(jb)
                    load_past(l, jb["slot"], h, jb["n_past"], idx % 2)

        def load_krp(jb):
            Ks, Vs, Rs, si = scr(jb["slot"])
            em.dma(SP, s_krp, [(krT_past[:, 0:jb["n_past"]], Rs[l, si, :, 0:jb["n_past"]], {})],
                   reads=[r_scrK[(l, jb["slot"])][2]], writes=[r_krp])

        prefetch(0)
        prefetch(1)
        return pairs, prefetch, load_krp

    def attention(l, jobs, pre):
        pairs, prefetch, load_krp = pre
        for idx, (ji, h) in enumerate(pairs):
            jb = jobs[ji]
            q0, N, n_past = jb["q0"], jb["N"], jb["n_past"]
            if h == 0 and ji > 0 and n_past > 0:
                load_krp(jb)
            buf = idx % 2
            hp = h % 2
            rows = slice(hp * 64, hp * 64 + 64)
            blocks = []
            for kb in range(n_past // 128):
                blocks.append(dict(K=KpT[buf][:, kb * 128:(kb + 1) * 128], R=krT_past[:, kb * 128:(kb + 1) * 128],
                                   V=Vp[buf][:, kb, :], nk=128, q0=q0, N=N, masked=False,
                                   reads=[r_KpT[buf], r_krp]))
            for (kc0, nk, vb, qb0, masked) in jb["own"]:
                blocks.append(dict(K=KT_own[:, h, kc0:kc0 + nk], R=latT[:, 5, kc0:kc0 + nk],
                                   V=V_own[0:nk, vb, h * 128:(h + 1) * 128], nk=nk, q0=qb0, N=q0 + N - qb0, masked=masked,
                                   reads=[r_KT[h], r_V[vb]] + r_latT))
            nb = len(blocks)
            Ob = 4 + (hcount[0] % 2)
            Lb = 6 + (hcount[0] % 2)
            hcount[0] += 1
            sbank = {}

            def QK(i):
                bl = blocks[i]
                (bk,) = rot(1)
                sbank[i] = bk
                nk, bq0, bN = bl["nk"], bl["q0"], bl["N"]
                em.group([mm(banks[bk][0:nk, 0:bN], bl["K"], QnT[:, h, bq0:bq0 + bN], True, False),
                          mm(banks[bk][0:nk, 0:bN], bl["R"], QrT[:, h, bq0:bq0 + bN], False, True)],
                         reads=bl["reads"] + [r_QnT[h], r_QrT[h]], writes=[bres[bk]])

            def EXP(i):
                bl = blocks[i]
                bk = sbank[i]
                p = i % 3
                nk, bN = bl["nk"], bl["N"]
                if not bl["masked"]:
                    em.op(ACT, _f("activation", out=PT[p][0:nk, 0:bN], in_=banks[bk][0:nk, 0:bN], func=AF.Exp),
                          reads=[bres[bk]], writes=[r_PT[p]])
                else:
                    em.op(ACT, _f("activation", out=PT[p][0:128, 64:bN], in_=banks[bk][0:128, 64:bN], func=AF.Exp),
                          reads=[bres[bk]], writes=[r_PT[p]])
                    em.op(ACT, _f("activation", out=PT[p][0:64, 0:64], in_=banks[bk][0:64, 0:64], func=AF.Exp),
                          reads=[bres[bk]], writes=[r_PT[p]])
                    em.op(POOL, _f("memset", PT[p][64:128, 0:64], 0.0), reads=[r_PT[p]], writes=[r_PT[p]])

            def PV(i):
                bl = blocks[i]
                p = i % 3
                nk, bq0, bN = bl["nk"], bl["q0"], bl["N"]
                em.group([mm(banks[Ob][:, bq0:bq0 + bN], bl["V"], PT[p][0:nk, 0:bN], i == 0, i == nb - 1)],
                         reads=bl["reads"] + [r_PT[p]], writes=[bres[Ob]])
                a = 1 if i % 3 == 2 else 0
                em.op(POOL if a else DVE, _f("tensor_tensor", out=Lacc[a][0:nk, bq0:bq0 + bN], in0=Lacc[a][0:nk, bq0:bq0 + bN],
                                             in1=PT[p][0:nk, 0:bN], op=ALU.add), reads=[r_PT[p], r_Lacc[a]], writes=[r_Lacc[a]])

            for a in range(2):
                em.op(POOL, _f("memset", Lacc[a][:, :], 0.0), writes=[r_Lacc[a]])
            QK(0)
            if nb > 1:
                QK(1)
            for i in range(nb):
                if i + 2 < nb:
                    QK(i + 2)
                EXP(i)
                PV(i)
            prefetch(idx + 2)
            em.group([mm(banks[Lb][:, q0:q0 + N], onesf[:, :], Lacc[0][:, q0:q0 + N], True, False),
                      mm(banks[Lb][:, q0:q0 + N], onesf[:, :], Lacc[1][:, q0:q0 + N], False, True)],
                     reads=r_Lacc + [r_const], writes=[bres[Lb]])
            em.op(DVE, _f("reciprocal", out=recip[:, q0:q0 + N], in_=banks[Lb][:, q0:q0 + N]), reads=[bres[Lb]], writes=[r_recip])
            em.op(DVE, _f("tensor_tensor", out=OT[:, h, q0:q0 + N], in0=banks[Ob][:, q0:q0 + N], in1=recip[:, q0:q0 + N], op=ALU.mult),
                  reads=[bres[Ob], r_recip], writes=[r_OT[h]])

    def mixer(l, tile):
        NT, NSUB, kind = tile["NT"], tile["NSUB"], tile["kind"]
        em.set_phase()
        rot_pool[0] = 8
        load_lnp(l, 1)
        if kind == "p":
            jobs = [dict(q0=0, N=512, slot=tile["b"], n_past=tile["t"] * 512,
                         own=[(j * 128, 128, j, j * 128, True) for j in range(4)])]
        else:
            jobs = [dict(q0=i * 64, N=64, slot=npseq + i, n_past=PAST, own=[(i * 64, 64, i, i * 64, False)]) for i in range(2)]
        pre = attention_prefetch(l, jobs)
        Tb = [rg.next(l, ("T", i)) for i in range(3)]
        latb = {}

        def lat_group(s):
            if s in pendT:
                flush_T()
            bA, bB = rot(2)
            latb[s] = (bA, bB)
            fns = []
            for reg, (bk, c0) in enumerate(((bA, 0), (bA, 256), (bB, 0))):
                for kc in range(8):
                    fns.append(mm(banks[bk][:, c0:c0 + 256], xT[:, kc, s * 128:(s + 1) * 128], Tb[reg][0][:, kc * 256:(kc + 1) * 256], kc == 0, kc == 7))
            em.group(fns, reads=[t[1] for t in Tb] + [r_xT[s]], writes=[bres[bA], bres[bB]])

        for s in range(min(3, NSUB)):
            lat_group(s)
        for s in range(NSUB):
            if s + 3 < NSUB:
                pass
            bA, bB = latb[s]
            par = s % 2
            em.op(ACT, _f("activation", out=junk[:, 0:QL], in_=banks[bA][:, 0:QL], func=AF.Square, scale=float(QL ** -0.5),
                                                              accum_out=ssq[:, par, 0:1]), reads=[bres[bA]], writes=[r_junk, r_ssq[par]])
            em.op(ACT, _f("activation", out=junk[:, 0:KVL], in_=banks[bB][:, 0:KVL], func=AF.Square, scale=float(KVL ** -0.5),
                                                              accum_out=ssq[:, par, 1:2]), reads=[bres[bB]], writes=[r_junk, r_ssq[par]])
            em.op(ACT, _f("activation", out=ssq[:, par, 2:4], in_=ssq[:, par, 0:2], func=AF.Sqrt, bias=EPS_AP[:, 0:1], scale=1.0),
                  reads=[r_ssq[par], r_const], writes=[r_ssq[par]])
            em.op(DVE, _f("reciprocal", out=ssq[:, par, 0:2], in_=ssq[:, par, 2:4]), reads=[r_ssq[par]], writes=[r_ssq[par]])
            em.op(DVE, _f("scalar_tensor_tensor", out=qn_b[:, s, :], in0=banks[bA][:, 0:QL], scalar=ssq[:, par, 0:1],
                                                                             in1=qg[:, :], op0=ALU.mult, op1=ALU.mult),
                  reads=[bres[bA], r_ssq[par], r_par], writes=[r_qnb[s]])
            em.op(DVE, _f("scalar_tensor_tensor", out=ckv_f[:, s, :], in0=banks[bB][:, 0:KVL], scalar=ssq[:, par, 1:2],
                                                                             in1=kvg[:, :], op0=ALU.mult, op1=ALU.mult),
                  reads=[bres[bB], r_ssq[par], r_par], writes=[r_ckvf])
            em.op(DVE, _f("tensor_tensor", out=rtmp[:, par, 0, :], in0=banks[bA][:, 384:448], in1=cosT[:, s, :], op=ALU.mult),
                  reads=[bres[bA], r_tbl], writes=[r_rtmp[par]])
            em.op(DVE, _f("tensor_tensor", out=rtmp[:, par, 1, :], in0=banks[bA][:, 448:512], in1=sinT[:, s, :], op=ALU.mult),
                  reads=[bres[bA], r_tbl, r_rtmp[par]], writes=[r_rtmp[par]])
            em.op(POOL, _f("tensor_tensor", out=kr_f[:, s, :], in0=rtmp[:, par, 0, :], in1=rtmp[:, par, 1, :], op=ALU.add),
                  reads=[r_rtmp[par]], writes=[r_krf])
            cast_lat(s)
            lat_transposes(s, with_q=True)
            if s + 3 < NSUB:
                lat_group(s + 3)
        flush_T()
        rg.close(3)
        if kind == "p":
            b, t0 = tile["b"], tile["t"] * 512
            em.dma(POOL, s_stlat, [(plat[l, b, t0:t0 + NT, :].rearrange("(s p) c -> p s c", p=128), ckv_f[:, 0:NSUB, :], {})],
                   reads=[r_ckvf])
            em.dma(POOL, s_stkr, [(pkr[l, b, t0:t0 + NT, :].rearrange("(s p) c -> p s c", p=128), kr_f[:, 0:NSUB, :], {})],
                   reads=[r_krf])
        else:
            em.dma(POOL, s_stlat, [(slat[l, :, :], ckv_f[:, 0, :], {})], reads=[r_ckvf])
            em.dma(POOL, s_stkr, [(skr[l, :, :], kr_f[:, 0, :], {})], reads=[r_krf])
        Qb = [rg.next(l, ("Q", i)) for i in range(4)]
        for h in range(NH):
            (bk,) = rot(1)
            Qw, rQ = Qb[h // 4]
            em.group([mm(banks[bk][:, 0:NT], Qw[:, kc * 512 + (h % 4) * 128:kc * 512 + (h % 4 + 1) * 128], latT[:, kc, 0:NT], kc == 0, kc == 2)
                      for kc in range(3)], reads=[rQ] + r_latT[:NSUB], writes=[bres[bk]])
            evac_copy(QnT[:, h, 0:NT], banks[bk][:, 0:NT], [bres[bk]], [r_QnT[h]], scale=ATTN_SCALE)
        for pr in range(4):
            bA, bB = rot(2)
            em.group([mm(banks[bA][:, 0:NT], Qb[2][0][:, kc * 512 + pr * 128:kc * 512 + (pr + 1) * 128], latT[:, kc, 0:NT], kc == 0, kc == 2) for kc in range(3)]
                     + [mm(banks[bB][:, 0:NT], Qb[3][0][:, kc * 512 + pr * 128:kc * 512 + (pr + 1) * 128], latT[:, kc, 0:NT], kc == 0, kc == 2) for kc in range(3)],
                     reads=[Qb[2][1], Qb[3][1]] + r_latT[:NSUB], writes=[bres[bA], bres[bB]])
            em.op(DVE, _f("tensor_tensor", out=ropet[0][:, 0:NT], in0=banks[bA][:, 0:NT], in1=cosF[:, 0:NT], op=ALU.mult),
                  reads=[bres[bA], r_tbl], writes=[r_ropet[0]])
            em.op(DVE, _f("tensor_tensor", out=ropet[1][:, 0:NT], in0=banks[bB][:, 0:NT], in1=sinF[:, 0:NT], op=ALU.mult),
                  reads=[bres[bB], r_tbl], writes=[r_ropet[1]])
            for hh in range(2):
                hq = 2 * pr + hh
                lo, zo = hh * 64, (1 - hh) * 64
                em.op(POOL, _f("tensor_tensor", out=QrT[lo:lo + 64, hq, 0:NT], in0=ropet[0][lo:lo + 64, 0:NT], in1=ropet[1][lo:lo + 64, 0:NT], op=ALU.add),
                      reads=r_ropet, writes=[r_QrT[hq]])
                em.op(POOL, _f("memset", QrT[zo:zo + 64, hq, 0:NT], 0.0), writes=[r_QrT[hq]])
        rg.close(4)
        if kind == "p":
            vblocks = [(j * 128, 128) for j in range(4)]
            kv_expand(l, NT, NSUB, vblocks, store=(tile["b"], tile["t"] * 512))
        else:
            vblocks = [(0, 64), (64, 64)]
            kv_expand(l, NT, NSUB, vblocks, store=None)
        rg.close(2)
        rot_pool[0] = 4
        attention(l, jobs, pre)

        em.set_phase()
        rot_pool[0] = 8
        if kind == "p":
            segs = [(0, NT)]
        else:
            segs = [(0, 64), (64, 64)]
        for j in range(4):
            CC, rCC = rg.next(l, ("CC", j))
            CU, rCU = rg.next(l, ("CU", j))
            CB, rCB = rg.next(l, ("CB", j))
            for i in range(2):
                c = 2 * j + i
                bC, bU = rot(2)
                (bB,) = rot(1)
                em.group([mm(banks[bC][:, 0:NT], CC[:, kc * 256 + i * 128:kc * 256 + (i + 1) * 128], xT[:, kc, 0:NT], kc == 0, kc == 7) for kc in range(8)]
                         + [mm(banks[bU][:, 0:NT], CU[:, kc * 256 + i * 128:kc * 256 + (i + 1) * 128], xT[:, kc, 0:NT], kc == 0, kc == 7) for kc in range(8)]
                         + [mm(banks[bB][:, 0:NT], CB[:, kc * 256 + i * 128:kc * 256 + (i + 1) * 128], xT[:, kc, 0:NT], kc == 0, kc == 7) for kc in range(8)],
                         reads=[rCC, rCU, rCB] + r_xT[:NSUB], writes=[bres[bC], bres[bU], bres[bB]])
                par = c % 2
                em.op(ACT, _f("activation", out=C_sb[par][:, 0:NT], in_=banks[bC][:, 0:NT], func=AF.Copy),
                      reads=[bres[bC]], writes=[r_Csb[par]])
                up = upb[par]
                for si, (c0, L) in enumerate(segs):
                    o = c0 + 2 * si
                    em.op(DVE, _f("tensor_tensor", out=up[:, o + 2:o + 2 + L], in0=banks[bU][:, c0:c0 + L],
                                                                                                in1=C_sb[par][:, c0:c0 + L], op=ALU.mult),
                          reads=[bres[bU], r_Csb[par]], writes=[r_up[par]])
                    if kind == "p":
                        em.op(POOL, _f("tensor_copy", out=up[:, o:o + 2], in_=carry[:, l, :, c]),
                              reads=[r_carry[l], r_up[par]], writes=[r_up[par]])
                    else:
                        em.op(POOL, _f("tensor_copy", out=up[:, o:o + 2], in_=scarry[:, l, si, :, c]),
                              reads=[r_scarry[l], r_up[par]], writes=[r_up[par]])
                    acc = accb[par]
                    em.op(ACT, _f("activation", out=acc[:, c0:c0 + L], in_=up[:, o + 2:o + 2 + L], func=AF.Copy, scale=cwt[:, 16 + c:17 + c]),
                          reads=[r_up[par], r_par], writes=[r_acc[par]])
                    for k in (1, 0):
                        em.op(DVE, _f("scalar_tensor_tensor",
                            out=acc[:, c0:c0 + L], in0=up[:, o + k:o + k + L], scalar=cwt[:, 8 * k + c:8 * k + c + 1], in1=acc[:, c0:c0 + L],
                            op0=ALU.mult, op1=ALU.add), reads=[r_up[par], r_par, r_acc[par]], writes=[r_acc[par]])
                    if kind == "p":
                        em.op(POOL, _f("tensor_copy", out=carry[:, l, :, c], in_=up[:, o + L:o + L + 2]),
                              reads=[r_up[par], r_carry[l]], writes=[r_carry[l]])
                    else:
                        em.op(POOL, _f("tensor_copy", out=scarry[:, l, si, :, c], in_=up[:, o + L:o + L + 2]),
                              reads=[r_up[par], r_scarry[l]], writes=[r_scarry[l]])
                em.op(DVE, _f("tensor_tensor", out=yc_in[:, c, 0:NT], in0=banks[bB][:, 0:NT], in1=accb[par][:, 0:NT], op=ALU.mult),
                      reads=[bres[bB], r_acc[par]], writes=[r_ycin[c]])
            rg.close(3)
        for j in range(4):
            MO, rMO = rg.next(l, ("MO", j))
            GM, rGM = rg.next(l, ("GM", j))
            for i in range(2):
                c = 2 * j + i
                by, bg_ = rot(2)
                em.group([mm(banks[by][:, 0:NT], MO[:, kc * 256 + i * 128:kc * 256 + (i + 1) * 128], OT[:, kc, 0:NT], kc == 0, kc == 7) for kc in range(8)]
                         + [mm(banks[bg_][:, 0:NT], GM[:, kc * 256 + i * 128:kc * 256 + (i + 1) * 128], xT[:, kc, 0:NT], kc == 0, kc == 7) for kc in range(8)],
                         reads=[rMO, rGM] + r_OT + r_xT[:NSUB], writes=[bres[by], bres[bg_]])
                par = c % 2
                em.op(ACT, _f("activation", out=gate[par][:, 0:NT], in_=banks[bg_][:, 0:NT], func=AF.Sigmoid,
                                                                         bias=bgt[:, 8 + c:9 + c], scale=1.0),
                      reads=[bres[bg_], r_par], writes=[r_gate[par]])
                em.op(DVE, _f("tensor_tensor", out=gmy[:, c, 0:NT], in0=banks[by][:, 0:NT], in1=gate[par][:, 0:NT], op=ALU.mult),
                      reads=[bres[by], r_gate[par]], writes=[r_gmy[c]])
            rg.close(2)
        conv_out = []
        if kind == "p" and tile["t"] == ntile - 1:
            conv_out.append((carry[:, l, :, :].rearrange("p r c -> p (r c)"), pconv[l, tile["b"]], r_carry[l]))
        if kind == "s":
            for i in range(2):
                conv_out.append((scarry[:, l, i, :, :].rearrange("p r c -> p (r c)"), sconv_o[l, i], r_scarry[l]))
        for src, dst, rr in conv_out:
            (bk,) = rot(1)
            em.group([tr(banks[bk][0:16, 0:128], src, identf[:])], reads=[rr, r_const], writes=[bres[bk]])
            em.op(ACT, _f("activation", out=cst[:, :], in_=banks[bk][0:16, 0:128], func=AF.Copy), reads=[bres[bk]], writes=[r_cst])
            em.dma(POOL, s_stc, [(dst.rearrange("r (c p) -> (r c) p", p=128), cst[:, :], {})], reads=[r_cst])
        for j in range(4):
            CO, rCO = rg.next(l, ("CO", j))
            GC, rGC = rg.next(l, ("GC", j))
            for i in range(2):
                c = 2 * j + i
                by, bg_ = rot(2)
                em.group([mm(banks[by][:, 0:NT], CO[:, kc * 256 + i * 128:kc * 256 + (i + 1) * 128], yc_in[:, kc, 0:NT], kc == 0, kc == 7) for kc in range(8)]
                         + [mm(banks[bg_][:, 0:NT], GC[:, kc * 256 + i * 128:kc * 256 + (i + 1) * 128], xT[:, kc, 0:NT], kc == 0, kc == 7) for kc in range(8)],
                         reads=[rCO, rGC] + r_ycin + r_xT[:NSUB], writes=[bres[by], bres[bg_]])
                par = c % 2
                em.op(ACT, _f("activation", out=gate[par][:, 0:NT], in_=banks[bg_][:, 0:NT], func=AF.Sigmoid,
                                                                         bias=bgt[:, c:c + 1], scale=1.0),
                      reads=[bres[bg_], r_par], writes=[r_gate[par]])
                em.op(DVE, _f("tensor_tensor", out=ttb[par][:, 0:NT], in0=banks[by][:, 0:NT], in1=gate[par][:, 0:NT], op=ALU.mult),
                      reads=[bres[by], r_gate[par]], writes=[r_tt[par]])
                em.op(POOL, _f("tensor_tensor", out=mergedT[:, c, 0:NT], in0=ttb[par][:, 0:NT], in1=gmy[:, c, 0:NT], op=ALU.add),
                      reads=[r_tt[par], r_gmy[c]], writes=[r_mT[c]])
            rg.close(2)
        MX = [rg.next(l, ("MX", k)) for k in range(4)]
        mxb = {}

        def grp(s):
            b0, b1 = rot(2)
            mxb[s] = b0
            fns = []
            for half, bk in enumerate((b0, b1)):
                for kc in range(8):
                    fns.append(mm(banks[bk][:, :], mergedT[:, kc, s * 128:(s + 1) * 128], MX[kc // 2][0][:, (kc % 2) * 1024 + half * 512:(kc % 2) * 1024 + (half + 1) * 512],
                                  kc == 0, kc == 7))
            em.group(fns, reads=[m[1] for m in MX] + r_mT, writes=[bres[b0], bres[b1]])
            if s == NSUB - 1:
                rg.close(4)
            residual_add(s, 0, b0)
            residual_add(s, 1, b1)

        ln_pipeline(NSUB, grp, lambda s2: mxb[s2], want_T=True)
        em.set_phase()

    def load_params(l):
        em.dma(SP, s_par, [(qg[:, :], qg_d[l:l + 1, :].partition_broadcast(128), {}),
                           (kvg[:, :], kvg_d[l:l + 1, :].partition_broadcast(128), {}),
                           (bgt[:, :], bg_d[l, :, :], {}),
                           (cwt[:, :], cw_d[l, :, :], {})], writes=[r_par])

    def run_tile(tile):
        NT, NSUB = tile["NT"], tile["NSUB"]
        tab = tile["tab"]
        if tile["kind"] == "p":
            src = xp[tile["b"], tile["t"] * 512:(tile["t"] + 1) * 512, :].rearrange("(s p) c -> p s c", p=128)
            em.dma(SP, s_xld, [(x_res[:, :, :], src, {})], writes=r_xres)
        else:
            em.dma(SP, s_xld, [(x_res[:, 0, :], xs[:, :], {})], writes=[r_xres[0]])
        em.dma(SP, s_tbl, [(cosT[:, :, :], cosT_d[tab].rearrange("p (s c) -> p s c", c=64), {}),
                           (sinT[:, :, :], sinT_d[tab].rearrange("p (s c) -> p s c", c=64), {}),
                           (cosF[:, :], cosF_d[tab], {}), (sinF[:, :], sinF_d[tab], {})], writes=[r_tbl])
        if tile["kind"] == "p" and tile["t"] == 0:
            em.op(POOL, _f("memset", carry[:], 0.0), writes=r_carry)
        em.set_phase()
        for s in range(NSUB):
            make_xT(s)
        for l in range(depth):
            load_params(l)
            ffn(l, 1, tile, 0, last=False)
            mixer(l, tile)
            ffn(l, 2, tile, 2, last=(l == depth - 1))
        if tile["kind"] == "p":
            dst = yp[tile["b"], tile["t"] * 512:(tile["t"] + 1) * 512, :].rearrange("(s p) c -> p s c", p=128)
            em.dma(POOL, s_sty, [(dst, x_res[:, :, :], {})], reads=r_xres)
        else:
            em.dma(POOL, s_sty, [(ys[:, :], x_res[:, 0, :], {})], reads=[r_xres[0]])

    em.set_phase()
    for tile in tiles:
        run_tile(tile)

    if sample:
        em.set_phase()
        em.dma(SP, s_scar, [(scarry[:].rearrange("p l i r c -> p l (i r c)"), sconv.rearrange("l p f -> p l f"), {})], writes=r_scarry)
        for l in range(depth):
            first = True
            for i in range(2):
                for pt in range(2):
                    t0 = pt * 512
                    em.dma(SP, s_cache[0], [(ckv_f[:, :, :], clat[l, i, t0:t0 + 512, :].rearrange("(s p) c -> p s c", p=128), {})], writes=[r_ckvf])
                    em.dma(SP, s_cache[1], [(kr_f[:, :, :], ckr[l, i, t0:t0 + 512, :].rearrange("(s p) c -> p s c", p=128), {})], writes=[r_krf])
                    for s in range(4):
                        cast_lat(s)
                        lat_transposes(s, with_q=False)
                    if not first:
                        rg.consumed -= 2
                    kv_expand(l, 512, 4, [(j * 128, 128) for j in range(4)], store=(npseq + i, t0))
                    first = False
            rg.close(2)
        run_tile(dict(kind="s", NT=128, NSUB=1, tab=ntile))

    em.wait_all(SP, [s_sty, s_stlat, s_stkr, s_stc, s_stK, s_stV, s_stR])

    for s_ in em.sems:
        s_.h = es.enter_context(nc.semaphore(s_.name))
    with nc.Block() as block:
        @block.sync
        def _(e):
            em.replay(SP, e)

        @block.tensor
        def _(e):
            em.replay(PE, e)

        @block.scalar
        def _(e):
            em.replay(ACT, e)

        @block.vector
        def _(e):
            em.replay(DVE, e)

        @block.gpsimd
        def _(e):
            em.replay(POOL, e)
    es.close()
    return nc


def _rope_tables(ntile, with_sample=True):
    half = RD // 2
    inv = (np.float32(10000.0) ** (-np.arange(half, dtype=np.float32) / np.float32(half))).astype(np.float32)
    ntab = ntile + 1
    cosT = np.zeros((ntab, 128, 4, 64), np.float32)
    sinT = np.zeros((ntab, 128, 4, 64), np.float32)
    cosF = np.zeros((ntab, 128, 512), np.float32)
    sinF = np.zeros((ntab, 128, 512), np.float32)
    k = np.arange(64)
    sign = np.where(k < 32, -1.0, 1.0)
    for t in range(ntab):
        if t < ntile:
            pos = t * 512 + np.arange(512)
        else:
            pos = np.concatenate([PAST + np.arange(64), PAST + np.arange(64), np.zeros(384)])
        ang = (pos.astype(np.float32)[:, None] * inv[None, :]).astype(np.float32).astype(np.float64)
        c = np.cos(ang)[:, k % 32]
        s = np.sin(ang)[:, k % 32] * sign[None, :]
        cosT[t] = c.reshape(4, 128, 64).transpose(1, 0, 2)
        sinT[t] = s.reshape(4, 128, 64).transpose(1, 0, 2)
        cosF[t] = np.concatenate([c.T, c.T], axis=0) * ATTN_SCALE
        sinF[t] = np.concatenate([s.T, s.T], axis=0) * ATTN_SCALE
    return cosT.reshape(ntab, 128, 256), sinT.reshape(ntab, 128, 256), cosF, sinF


def _prep_weights(inp, depth):
    w_in = np.ascontiguousarray(inp["w_in"][:depth])
    kr = w_in[:, :, 640:704]
    kr_sw = np.concatenate([kr[:, :, 32:64], kr[:, :, 0:32]], axis=2)
    wtok = np.ascontiguousarray(np.concatenate([w_in[:, :, 0:384], kr, kr_sw, w_in[:, :, 384:640]], axis=2))
    w_uq = inp["w_uq"][:depth]
    uq = np.zeros((depth, QL, 2048), np.float32)
    swap = (np.arange(64) + 32) % 64
    for h in range(NH):
        uq[:, :, h * 128:(h + 1) * 128] = w_uq[:, :, h * 192:h * 192 + 128]
        pr, hh = h // 2, h % 2
        r = w_uq[:, :, h * 192 + 128:h * 192 + 192]
        uq[:, :, 1024 + pr * 128 + hh * 64:1024 + pr * 128 + hh * 64 + 64] = r
        uq[:, :, 1536 + pr * 128 + hh * 64:1536 + pr * 128 + hh * 64 + 64] = r[:, :, swap]
    w_ukv = inp["w_ukv"][:depth]
    ukv = np.zeros((depth, KVL, 2048), np.float32)
    for h in range(NH):
        ukv[:, :, h * 128:(h + 1) * 128] = w_ukv[:, :, h * 256:h * 256 + 128]
        ukv[:, :, 1024 + h * 128:1024 + (h + 1) * 128] = w_ukv[:, :, h * 256 + 128:h * 256 + 256]
    bg = np.ascontiguousarray(inp["b_gate"][:depth].reshape(depth, 2, 8, 128).transpose(0, 3, 1, 2).reshape(depth, 128, 16))
    cw = np.ascontiguousarray(inp["conv_w"][:depth].reshape(depth, 3, 8, 128).transpose(0, 3, 1, 2).reshape(depth, 128, 24))
    c = np.ascontiguousarray
    return {
        "f1gu": c(inp["ffn1_w_gate_up"][:depth]), "f1d": c(inp["ffn1_w_down"][:depth]),
        "f2gu": c(inp["ffn2_w_gate_up"][:depth]), "f2d": c(inp["ffn2_w_down"][:depth]),
        "win": w_in, "wtok": wtok, "uq": uq, "ukv": ukv,
        "mo": c(inp["w_mla_out"][:depth]), "co": c(inp["w_conv_out"][:depth]), "mx": c(inp["w_mix_out"][:depth]),
        "lng": c(inp["ln_gain"][:depth]), "lnb": c(inp["ln_bias"][:depth]),
        "qg": c(inp["q_norm_gain"][:depth]), "kvg": c(inp["kv_norm_gain"][:depth]), "bg": bg, "cw": cw,
    }


def run(inp, depth=4, npseq=2, ntile=8, sample=True, n_cores=8, trace=False):
    import ml_dtypes
    S = ntile * 512
    nc = build_program(depth, npseq, ntile, sample)
    shared = _prep_weights(inp, depth)
    cT, sT, cF, sF = _rope_tables(ntile)
    shared.update(cosT=cT, sinT=sT, cosF=cF, sinF=sF,
                  identb=np.eye(128, dtype=np.float32).astype(ml_dtypes.bfloat16), identf=np.eye(128, dtype=np.float32))
    in_maps = []
    for c in range(n_cores):
        m = dict(shared)
        m["xp"] = np.ascontiguousarray(inp["x_prompt"][c * npseq:(c + 1) * npseq, :S])
        m["xs"] = np.ascontiguousarray(inp["x_sample"][2 * c:2 * c + 2]).reshape(128, D)
        m["clat"] = np.ascontiguousarray(inp["cache_kv_latent"][:depth, 2 * c:2 * c + 2])
        m["ckr"] = np.ascontiguousarray(inp["cache_k_rope"][:depth, 2 * c:2 * c + 2])
        sc = inp["state_conv"][:depth, 2 * c:2 * c + 2]
        m["sconv"] = np.ascontiguousarray(sc.reshape(depth, 2, 2, 8, 128).transpose(0, 4, 1, 2, 3).reshape(depth, 128, 32))
        in_maps.append(m)
    res = run_bass_kernel_spmd(nc, in_maps, core_ids=list(range(n_cores)), **({"trace": True} if trace else {}))
    R = res.results
    y_p = np.concatenate([r["yp"] for r in R], axis=0)
    y_s = np.concatenate([r["ys"].reshape(2, 64, D) for r in R], axis=0)
    p_lat = np.concatenate([r["plat"] for r in R], axis=1)
    p_kr = np.concatenate([r["pkr"] for r in R], axis=1)
    p_conv = np.concatenate([r["pconv"] for r in R], axis=1)
    s_lat = np.concatenate([r["slat"].reshape(depth, 2, 64, KVL) for r in R], axis=1)
    s_kr = np.concatenate([r["skr"].reshape(depth, 2, 64, RD) for r in R], axis=1)
    s_conv = np.concatenate([r["sconv_o"] for r in R], axis=1)
    return (y_p, y_s, p_lat, p_kr, p_conv, s_lat, s_kr, s_conv), res


def kernel(**inputs):
    inp = {k: np.asarray(v) for k, v in inputs.items()}
    outs, _ = run(inp)
    return tuple(np.ascontiguousarray(o, dtype=np.float32) for o in outs)
```

```python
import numpy as np
from contextlib import ExitStack
import concourse.bass as bass
import concourse.mybir as mybir
from concourse.bass_utils import run_bass_kernel_spmd

F32 = mybir.dt.float32
BF16 = mybir.dt.bfloat16
AF = mybir.ActivationFunctionType
ALU = mybir.AluOpType

D = 1024
DFF = 2816
NH = 8
QL = 384
KVL = 256
RD = 64
PAST = 1024
EPS = 1e-5
ATTN_SCALE = float((128 + 64) ** -0.5)
NSLOT = 8
BLK = 2048


def _f(name, *a, **kw):
    return lambda e: getattr(e, name)(*a, **kw)

class Sem:
    def __init__(self, name):
        self.name = name
        self.h = None
        self.count = 0


class Eng:
    def __init__(self, name, sem):
        self.name = name
        self.sem = sem
        self.prog = []
        self.seen = {}


class Res:
    __slots__ = ("name", "w", "r", "arena")

    def __init__(self, name, arena=False):
        self.name = name
        self.w = None
        self.r = {}
        self.arena = arena


class Emitter:
    def __init__(self):
        self.sems = []
        self.pe = Eng("pe", self.sem("e_pe"))
        self.act = Eng("act", self.sem("e_act"))
        self.dve = Eng("dve", self.sem("e_dve"))
        self.pool = Eng("pool", self.sem("e_pool"))
        self.sp = Eng("sp", None)
        self.barrier = {}
        self.nobarrier = set()
        self.arena_tok = {}

    def sem(self, name):
        s = Sem(name)
        self.sems.append(s)
        return s

    def set_phase(self):
        self.barrier = dict(self.arena_tok)

    def _deps(self, E, reads, writes):
        need = {}

        def add(sem, v):
            if need.get(sem, 0) < v:
                need[sem] = v

        arena = False
        for r in reads:
            if r.w is not None:
                add(*r.w)
            arena = arena or r.arena
        for w in writes:
            if w.w is not None:
                add(*w.w)
            for sem, v in w.r.items():
                add(sem, v)
            arena = arena or w.arena
        if arena:
            for sem, v in self.barrier.items():
                add(sem, v)
        waits = []
        for sem, v in need.items():
            if sem is E.sem:
                if E.name == "pe":
                    continue
            if E.seen.get(sem, 0) >= v:
                continue
            E.seen[sem] = v
            waits.append((sem, v))
        return waits

    def _finish(self, tok, reads, writes):
        for x in list(reads) + list(writes):
            if x.arena:
                if self.arena_tok.get(tok[0], 0) < tok[1]:
                    self.arena_tok[tok[0]] = tok[1]
                break
        for w in writes:
            w.w = tok
            w.r = {}
        for r in reads:
            if r in writes:
                continue
            if r.r.get(tok[0], 0) < tok[1]:
                r.r[tok[0]] = tok[1]

    def op(self, E, fn, reads=(), writes=()):
        waits = self._deps(E, reads, writes)
        E.sem.count += 1
        tok = (E.sem, E.sem.count)
        E.prog.append((waits, fn, (E.sem, 1)))
        self._finish(tok, reads, writes)
        return tok

    def group(self, fns, reads=(), writes=()):
        E = self.pe
        waits = self._deps(E, reads, writes)
        E.sem.count += 1
        tok = (E.sem, E.sem.count)
        n = len(fns)
        for i, f in enumerate(fns):
            E.prog.append((waits if i == 0 else (), f, (E.sem, 1) if i == n - 1 else None))
        self._finish(tok, reads, writes)
        return tok

    def dma(self, Q, dsem, items, reads=(), writes=()):
        waits = self._deps(Q, reads, writes)
        for i, (o, a, kw) in enumerate(items):
            dsem.count += 16
            Q.prog.append((waits if i == 0 else (), (_f("dma_start", out=o, in_=a, **kw)), (dsem, 16)))
        tok = (dsem, dsem.count)
        self._finish(tok, reads, writes)
        return tok

    def wait_all(self, E, sems):
        waits = [(s, s.count) for s in sems if s.count > 0]
        E.prog.append((waits, None, None))

    def replay(self, E, eng):
        for waits, fn, sig in E.prog:
            for sem, v in waits:
                eng.wait_ge(sem.h, v)
            if fn is None:
                continue
            ins = fn(eng)
            if sig is not None:
                ins.then_inc(sig[0].h, sig[1])


def layer_blocks():
    bl = []

    def ffn(w):
        gu, dn = ("f%dgu" % w, "f%dd" % w)
        for j in range(11):
            bl.append((("G", w, j), gu, 0, 8, j * 256, 256))
            bl.append((("U", w, j), gu, 0, 8, DFF + j * 256, 256))
        for half in range(2):
            for k in range(6):
                nk = 4 if k < 5 else 2
                bl.append((("D", w, half, k), dn, k * 4, nk, half * 512, 512))

    ffn(1)
    for i in range(3):
        bl.append((("T", i), "wtok", 0, 8, i * 256, 256))
    for i in range(4):
        bl.append((("Q", i), "uq", 0, 3, i * 512, 512))
    for i in range(2):
        bl.append((("KV", i), "ukv", 0, 2, i * 1024, 1024))
    for j in range(4):
        bl.append((("CC", j), "win", 0, 8, 1728 + j * 256, 256))
        bl.append((("CU", j), "win", 0, 8, 2752 + j * 256, 256))
        bl.append((("CB", j), "win", 0, 8, 704 + j * 256, 256))
    for j in range(4):
        bl.append((("MO", j), "mo", 0, 8, j * 256, 256))
        bl.append((("GM", j), "win", 0, 8, 4800 + j * 256, 256))
    for j in range(4):
        bl.append((("CO", j), "co", 0, 8, j * 256, 256))
        bl.append((("GC", j), "win", 0, 8, 3776 + j * 256, 256))
    for k in range(4):
        bl.append((("MX", k), "mx", k * 2, 2, 0, 1024))
    ffn(2)
    return bl


LBLOCKS = layer_blocks()
NB = len(LBLOCKS)
BIDX = {b[0]: i for i, b in enumerate(LBLOCKS)}


def build_program(depth=4, npseq=2, ntile=8, sample=True):
    S = ntile * 512
    ntab = ntile + 1
    nc = bass.Bass("TRN2", target_bir_lowering=False)
    em = Emitter()
    PE, ACT, DVE, POOL, SP = em.pe, em.act, em.dve, em.pool, em.sp

    def din(name, shape, dt=F32):
        return nc.dram_tensor(name, list(shape), dt, kind="ExternalInput").ap()

    def dout(name, shape, dt=F32):
        return nc.dram_tensor(name, list(shape), dt, kind="ExternalOutput").ap()

    def dscr(name, shape, dt=BF16):
        return nc.dram_tensor(name, list(shape), dt).ap()

    xp = din("xp", [npseq, S, D])
    xs = din("xs", [128, D])
    clat = din("clat", [depth, 2, PAST, KVL])
    ckr = din("ckr", [depth, 2, PAST, RD])
    sconv = din("sconv", [depth, 128, 32])
    W = {
        "f1gu": din("f1gu", [depth, D, 2 * DFF]), "f1d": din("f1d", [depth, DFF, D]),
        "f2gu": din("f2gu", [depth, D, 2 * DFF]), "f2d": din("f2d", [depth, DFF, D]),
        "win": din("win", [depth, D, 5824]), "wtok": din("wtok", [depth, D, 768]),
        "uq": din("uq", [depth, QL, 2048]), "ukv": din("ukv", [depth, KVL, 2048]),
        "mo": din("mo", [depth, D, D]), "co": din("co", [depth, D, D]), "mx": din("mx", [depth, D, D]),
    }
    lng = din("lng", [depth, 3, D])
    lnb = din("lnb", [depth, 3, D])
    qg_d = din("qg", [depth, QL])
    kvg_d = din("kvg", [depth, KVL])
    bg_d = din("bg", [depth, 128, 16])
    cw_d = din("cw", [depth, 128, 24])
    cosT_d = din("cosT", [ntab, 128, 4 * 64])
    sinT_d = din("sinT", [ntab, 128, 4 * 64])
    cosF_d = din("cosF", [ntab, 128, 512])
    sinF_d = din("sinF", [ntab, 128, 512])
    identb_d = din("identb", [128, 128], BF16)
    identf_d = din("identf", [128, 128])

    yp = dout("yp", [npseq, S, D])
    ys = dout("ys", [128, D])
    plat = dout("plat", [depth, npseq, S, KVL])
    pkr = dout("pkr", [depth, npseq, S, RD])
    pconv = dout("pconv", [depth, npseq, 2, D])
    slat = dout("slat", [depth, 128, KVL])
    skr = dout("skr", [depth, 128, RD])
    sconv_o = dout("sconv_o", [depth, 2, 2, D])

    tape = dscr("tape", [depth, NB, 128, BLK])
    Kscr = dscr("Kscr", [depth, npseq, NH, 128, S])
    Vscr = dscr("Vscr", [depth, npseq, S, NH * 128])
    Rscr = dscr("Rscr", [depth, npseq, 128, S])
    KscrS = dscr("KscrS", [depth, 2, NH, 128, PAST])
    VscrS = dscr("VscrS", [depth, 2, PAST, NH * 128])
    RscrS = dscr("RscrS", [depth, 2, 128, PAST])

    def scr(slot):
        if slot < npseq:
            return Kscr, Vscr, Rscr, slot
        return KscrS, VscrS, RscrS, slot - npseq

    es = ExitStack()

    def sb(name, shape, dt):
        return es.enter_context(nc.sbuf_tensor(name, list(shape), dt))

    ring = sb("ring", [128, NSLOT * BLK], BF16)
    x_res = sb("x_res", [128, 4, D], F32)
    xT = sb("xT", [128, 8, 512], BF16)
    OT = sb("OT", [128, 8, 512], BF16)
    lnp = sb("lnp", [128, 2 * D], F32)
    cosT = sb("cosT_s", [128, 4, 64], F32)
    sinT = sb("sinT_s", [128, 4, 64], F32)
    cosF = sb("cosF_s", [128, 512], F32)
    sinF = sb("sinF_s", [128, 512], F32)
    qg = sb("qg_s", [128, QL], F32)
    kvg = sb("kvg_s", [128, KVL], F32)
    bgt = sb("bg_s", [128, 16], F32)
    cwt = sb("cw_s", [128, 24], F32)
    identb = sb("identb_s", [128, 128], BF16)
    identf = sb("identf_s", [128, 128], F32)
    ones = sb("ones_s", [128, 128], BF16)
    onesf = sb("onesf_s", [128, 128], F32)
    carry = sb("carry", [128, depth, 2, 8], F32)
    scarry = sb("scarry", [128, depth, 2, 2, 8], F32)
    cst = sb("cst", [16, 128], F32)
    stats = sb("stats", [128, 2, 2, 6], F32)
    mv = sb("mv", [128, 2, 4], F32)
    ssq = sb("ssq", [128, 2, 4], F32)
    junk = sb("junk", [128, 512], F32)
    rtmp = sb("rtmp", [128, 2, 2, 64], F32)

    ARENA_BYTES = 100 * 1024
    arena = sb("arena", [128, ARENA_BYTES // 4], F32)

    class Carver:
        def __init__(self):
            self.off = 0

        def take(self, free_shape, dt):
            n = int(np.prod(free_shape))
            esz = 4 if dt is F32 else 2
            nbytes = (n * esz + 31) // 32 * 32
            assert self.off + nbytes <= ARENA_BYTES, (self.off, nbytes)
            ap = arena[:, self.off // 4:(self.off + n * esz) // 4]
            if dt is BF16:
                ap = ap.bitcast(BF16)
            if len(free_shape) == 2:
                ap = ap.rearrange("p (a b) -> p a b", b=free_shape[1])
            elif len(free_shape) == 3:
                ap = ap.rearrange("p (a b c) -> p a b c", b=free_shape[1], c=free_shape[2])
            self.off += nbytes
            return ap

    cv = Carver()
    NPP = 3
    pp_stage = [cv.take([BLK], F32) for _ in range(NPP)]
    pp_out = [cv.take([BLK], BF16) for _ in range(NPP)]
    cv = Carver()
    hT = cv.take([22, 512], BF16)
    silu_b = [cv.take([512], F32) for _ in range(2)]
    zbuf = [sb("zbuf%d" % i, [128, D], F32) for i in range(2)]
    xb = [sb("xb%d" % i, [128, D], BF16) for i in range(2)]
    cv = Carver()
    ckv_f = cv.take([4, KVL], F32)
    kr_f = cv.take([4, RD], F32)
    qn_b = cv.take([4, QL], BF16)
    ckv_b = cv.take([4, KVL], BF16)
    krd_b = cv.take([4, 128], BF16)
    latT = cv.take([6, 512], BF16)
    QnT = cv.take([8, 512], BF16)
    QrT = cv.take([8, 512], BF16)
    ropet = [cv.take([512], F32) for _ in range(2)]
    KT_own = cv.take([8, 512], BF16)
    V_own = cv.take([4, D], BF16)
    KpT = [cv.take([3584], BF16) for _ in range(2)]
    Vp = [cv.take([28, 128], BF16) for _ in range(2)]
    krT_past = cv.take([3584], BF16)
    PT = [cv.take([512], BF16) for _ in range(4)]
    recip = cv.take([512], F32)
    Lacc = [cv.take([512], F32) for _ in range(2)]
    cv = Carver()
    gmy = cv.take([8, 512], F32)
    C_sb = [cv.take([512], F32) for _ in range(2)]
    upb = [cv.take([520], F32) for _ in range(2)]
    accb = [cv.take([512], F32) for _ in range(2)]
    yc_in = cv.take([8, 512], BF16)
    gate = [cv.take([512], F32) for _ in range(2)]
    ttb = [cv.take([512], F32) for _ in range(2)]
    mergedT = cv.take([8, 512], BF16)

    banks = [es.enter_context(nc.psum_tensor("bank%d" % i, [128, 512], F32)) for i in range(8)]
    bres = [Res("bank%d" % i) for i in range(8)]
    rot_state = [0]

    rot_pool = [4]

    def rot(n=1):
        P = rot_pool[0]
        if n == 2:
            if rot_state[0] % 2:
                rot_state[0] += 1
        ids = [(rot_state[0] + i) % P for i in range(n)]
        rot_state[0] = (rot_state[0] + n) % P
        return ids

    ring_sem = [em.sem("ring%d" % i) for i in range(NSLOT)]
    for s_ in ring_sem:
        em.nobarrier.add(s_)
    ring_res = [Res("ring%d" % i) for i in range(NSLOT)]
    s_xld = em.sem("xld")
    s_tbl = em.sem("tbl")
    s_lnp = em.sem("lnp")
    s_par = em.sem("par")
    s_kvp = [em.sem("kvp0"), em.sem("kvp1")]
    s_krp = em.sem("krp")
    s_cache = [em.sem("cache0"), em.sem("cache1")]
    s_scar = em.sem("scar")
    s_sty = em.sem("sty")
    s_stlat = em.sem("stlat")
    s_stkr = em.sem("stkr")
    s_stK = em.sem("stK")
    s_stV = em.sem("stV")
    s_stR = em.sem("stR")
    s_stc = em.sem("stc")
    s_ppin = [em.sem("ppin%d" % i) for i in range(NPP)]
    s_ppout = [em.sem("ppout%d" % i) for i in range(NPP)]
    s_const = em.sem("const")

    R = {}

    def res(name, arena=False):
        if name not in R:
            R[name] = Res(name, arena)
        return R[name]

    r_tape = res("tape")
    r_xres = [res("xres%d" % s) for s in range(4)]
    r_xT = [res("xT%d" % s) for s in range(4)]
    r_lnp = res("lnp")
    r_tbl = res("tbl")
    r_par = res("par")
    r_const = res("const")
    r_OT = [res("OT%d" % h) for h in range(8)]
    r_carry = [res("carry%d" % l) for l in range(depth)]
    r_scarry = [res("scarry%d" % l) for l in range(depth)]
    r_cst = res("cst")
    r_stats = [res("stats0"), res("stats1")]
    r_ssq = [res("ssq0"), res("ssq1")]
    r_junk = res("junk")
    r_rtmp = [res("rtmp0"), res("rtmp1")]
    r_scrK = {}
    r_out = res("out_dram")

    def ar(name):
        return res(name, arena=True)

    r_ppst = [ar("ppst%d" % i) for i in range(NPP)]
    r_ppo = [ar("ppo%d" % i) for i in range(NPP)]
    r_hT = [ar("hT%d" % c) for c in range(22)]
    r_silu = [ar("silu0"), ar("silu1")]
    r_z = [res("z0"), res("z1")]
    r_xb = [res("xb0"), res("xb1")]
    r_ckvf = ar("ckvf")
    r_krf = ar("krf")
    r_qnb = [ar("qnb%d" % s) for s in range(4)]
    r_ckvb = [ar("ckvb%d" % s) for s in range(4)]
    r_krdb = [ar("krdb%d" % s) for s in range(4)]
    r_latT = [ar("latT%d" % s) for s in range(4)]
    r_QnT = [ar("QnT%d" % h) for h in range(8)]
    r_QrT = [ar("QrT%d" % p) for p in range(8)]
    r_ropet = [ar("ropet0"), ar("ropet1")]
    r_KT = [ar("KT%d" % h) for h in range(8)]
    r_V = [ar("V%d" % s) for s in range(4)]
    r_KpT = [ar("KpT0"), ar("KpT1")]
    r_krp = ar("krTpast")
    r_PT = [ar("PT%d" % i) for i in range(4)]
    r_recip = ar("recip")
    r_Lacc = [ar("Lacc0"), ar("Lacc1")]
    r_gmy = [ar("gmy%d" % c) for c in range(8)]
    r_Csb = [ar("Csb0"), ar("Csb1")]
    r_up = [ar("up0"), ar("up1")]
    r_acc = [ar("acc0"), ar("acc1")]
    r_ycin = [ar("ycin%d" % c) for c in range(8)]
    r_gate = [ar("gate0"), ar("gate1")]
    r_tt = [ar("tt0"), ar("tt1")]
    r_mT = [ar("mT%d" % c) for c in range(8)]

    em.dma(SP, s_const, [(identb[:], identb_d[:, :], {}), (identf[:], identf_d[:, :], {})], writes=[r_const])
    em.op(POOL, _f("memset", ones[:], 1.0), writes=[r_const])
    em.op(POOL, _f("memset", onesf[:], 1.0), writes=[r_const])

    cnt = 0
    for l in range(1):
        for bi, (name, wk, k0, nk, c0, ncols) in enumerate(LBLOCKS):
            i = cnt % NPP
            width = nk * ncols
            src = W[wk][l, k0 * 128:(k0 + nk) * 128, c0:c0 + ncols].rearrange("(k p) c -> p k c", p=128)
            dst = pp_stage[i][:, 0:width].rearrange("p (k c) -> p k c", c=ncols)
            em.dma(SP, s_ppin[i], [(dst, src, {})], writes=[r_ppst[i]])
            if cnt % 2 == 0:
                em.op(ACT, _f("activation", out=pp_out[i][:, 0:width], in_=pp_stage[i][:, 0:width], func=AF.Copy),
                      reads=[r_ppst[i]], writes=[r_ppo[i]])
            else:
                em.op(DVE, _f("tensor_copy", out=pp_out[i][:, 0:width], in_=pp_stage[i][:, 0:width]),
                      reads=[r_ppst[i]], writes=[r_ppo[i]])
            em.dma(POOL, s_ppout[i], [(tape[l, bi, :, 0:width], pp_out[i][:, 0:width], {})], reads=[r_ppo[i]], writes=[r_tape])
            cnt += 1
    em.wait_all(SP, s_ppout)
    r_tape.w = None

    def aview(off, n, dt):
        esz = 4 if dt is F32 else 2
        ap = arena[:, off // 4:(off + n * esz) // 4]
        return ap.bitcast(BF16) if dt is BF16 else ap

    D_OFF = 54272
    dstage = [aview(D_OFF + i * 8192, BLK, F32) for i in range(3)]
    dout_b = [aview(D_OFF + 3 * 8192 + i * 4096, BLK, BF16) for i in range(2)]
    s_din = [em.sem("din%d" % i) for i in range(3)]
    s_dout = [em.sem("dout%d" % i) for i in range(2)]
    r_dst = [Res("dst%d" % i) for i in range(3)]
    r_dout = [Res("dout%d" % i) for i in range(2)]
    r_tapeL = {0: [r_tape]}
    for l in range(1, depth):
        r_tapeL[l] = [Res("tape%d_0" % l), Res("tape%d_1" % l)]
    dsteps = [(l, bi) for l in range(1, depth) for bi in range(NB)]
    dstate = dict(cast=0, primed=False)

    def d_load(b):
        l, bi = dsteps[b]
        _, wk, k0, nk, c0, ncols = LBLOCKS[bi]
        width = nk * ncols
        i = b % 3
        src = W[wk][l, k0 * 128:(k0 + nk) * 128, c0:c0 + ncols].rearrange("(k p) c -> p k c", p=128)
        dst = dstage[i][:, 0:width].rearrange("p (k c) -> p k c", c=ncols)
        em.dma(ACT, s_din[i], [(dst, src, {})], writes=[r_dst[i]])

    def pp_tick(n=1):
        if not dsteps:
            return
        if not dstate["primed"]:
            dstate["primed"] = True
            for b in range(min(3, len(dsteps))):
                d_load(b)
        for _ in range(n):
            b = dstate["cast"]
            if b >= len(dsteps):
                return
            l, bi = dsteps[b]
            _, wk, k0, nk, c0, ncols = LBLOCKS[bi]
            width = nk * ncols
            i, o = b % 3, b % 2
            em.op(ACT, _f("activation", out=dout_b[o][:, 0:width], in_=dstage[i][:, 0:width], func=AF.Copy),
                  reads=[r_dst[i]], writes=[r_dout[o]])
            em.dma(ACT, s_dout[o], [(tape[l, bi, :, 0:width], dout_b[o][:, 0:width], {})], reads=[r_dout[o]], writes=[r_tapeL[l][o]])
            if b + 3 < len(dsteps):
                d_load(b + 3)
            dstate["cast"] = b + 1

    def pp_drain(upto_layer):
        while dstate["cast"] < len(dsteps) and dsteps[dstate["cast"]][0] <= upto_layer:
            pp_tick(1)

    tiles = []
    for b in range(npseq):
        for t in range(ntile):
            tiles.append(dict(kind="p", b=b, t=t, NT=512, NSUB=4, tab=t))
    seq = []
    for _ in tiles:
        for l in range(depth):
            seq += [(l, bi) for bi in range(NB)]
    if sample:
        for l in range(depth):
            seq += [(l, BIDX[("KV", 0)]), (l, BIDX[("KV", 1)])]
        for l in range(depth):
            seq += [(l, bi) for bi in range(NB)]

    class Ring:
        def __init__(self):
            self.loaded = 0
            self.consumed = 0
            self.closed = 0

        def _pump(self):
            while self.loaded < min(len(seq), self.closed + NSLOT):
                k = self.loaded
                l, bi = seq[k]
                _, wk, k0, nk, c0, ncols = LBLOCKS[bi]
                width = nk * ncols
                s = k % NSLOT
                pp_drain(l)
                em.dma(SP, ring_sem[s], [(ring[:, s * BLK:s * BLK + width], tape[l, bi, :, 0:width], {})],
                       reads=r_tapeL[l], writes=[ring_res[s]])
                self.loaded += 1

        def next(self, l, name):
            k = self.consumed
            assert seq[k] == (l, BIDX[name]), (seq[k], l, name)
            self._pump()
            assert k < self.loaded
            self.consumed += 1
            s = k % NSLOT
            return ring[:, s * BLK:(s + 1) * BLK], ring_res[s]

        def close(self, n=1):
            self.closed += n
            self._pump()

    rg = Ring()

    ALPHA = float((2 * 4) ** 0.25)

    def mm(out, lhsT, rhs, start, stop):
        return _f("matmul", out, lhsT, rhs, start=start, stop=stop)

    def tr(out, in_, ident):
        return _f("transpose", out, in_, ident)

    alt = [0]

    def evac_copy(out, in_, reads, writes, scale=None):
        alt[0] += 1
        if alt[0] % 2 == 0:
            if scale is None:
                em.op(ACT, _f("activation", out=out, in_=in_, func=AF.Copy), reads=reads, writes=writes)
            else:
                em.op(ACT, _f("activation", out=out, in_=in_, func=AF.Copy, scale=scale), reads=reads, writes=writes)
        else:
            if scale is None:
                em.op(DVE, _f("tensor_copy", out=out, in_=in_), reads=reads, writes=writes)
            else:
                em.op(DVE, _f("tensor_scalar", out=out, in0=in_, scalar1=scale, scalar2=None, op0=ALU.mult), reads=reads, writes=writes)

    pendT = []

    def cast_x(s):
        par = s % 2
        em.op(ACT, _f("activation", out=xb[par][:, :], in_=x_res[:, s, :], func=AF.Copy), reads=[r_xres[s]], writes=[r_xb[par]])

    def flush_T(keep=0):
        while len(pendT) > keep:
            emit_T(pendT.pop(0))

    def make_xT(s, bank=None):
        cast_x(s)
        emit_T(s, bank)

    def emit_T(s, bank=None):
        par = s % 2
        if bank is None:
            (bk,) = rot(1)
        else:
            bk = bank
        pb = banks[bk][:].bitcast(BF16)
        em.group([tr(pb[:, kc * 128:(kc + 1) * 128], xb[par][:, kc * 128:(kc + 1) * 128], identb[:]) for kc in range(8)],
                 reads=[r_xb[par], r_const], writes=[bres[bk]])
        evac_copy(xT[:, :, s * 128:(s + 1) * 128], pb.rearrange("p (a b) -> p a b", b=128), [bres[bk]], [r_xT[s]])

    def load_lnp(l, i):
        em.dma(SP, s_lnp, [(lnp[:, 0:D], lng[l, i:i + 1, :].partition_broadcast(128), {}),
                           (lnp[:, D:2 * D], lnb[l, i:i + 1, :].partition_broadcast(128), {})], writes=[r_lnp])

    lnst = sb("lnst", [128, 4, 8], F32)
    junk2 = sb("junk2", [128, D], BF16)
    r_lnst = [res("lnst%d" % i) for i in range(4)]
    r_junk2 = res("junk2")

    def ln_A(s):
        st = lnst[:, s, :]
        em.op(ACT, _f("activation", out=junk2[:, :], in_=x_res[:, s, :], func=AF.Square, accum_out=lnst[:, s, 2:3]),
              reads=[r_xres[s]], writes=[r_junk2, r_lnst[s]])
        em.op(ACT, _f("activation", out=lnst[:, s, 3:4], in_=lnst[:, s, 0:1], func=AF.Identity, bias=lnst[:, s, 1:2], scale=1.0),
              reads=[r_lnst[s]], writes=[r_lnst[s]])
        em.op(ACT, _f("activation", out=lnst[:, s, 4:5], in_=lnst[:, s, 3:4], func=AF.Copy, scale=-1.0 / (D * D)),
              reads=[r_lnst[s]], writes=[r_lnst[s]])
        em.op(ACT, _f("activation", out=lnst[:, s, 5:6], in_=lnst[:, s, 3:4], func=AF.Identity, bias=EPS_AP[:, 0:1], scale=lnst[:, s, 4:5]),
              reads=[r_lnst[s], r_const], writes=[r_lnst[s]])
        em.op(ACT, _f("activation", out=lnst[:, s, 6:7], in_=lnst[:, s, 2:3], func=AF.Sqrt, bias=lnst[:, s, 5:6], scale=1.0 / D),
              reads=[r_lnst[s]], writes=[r_lnst[s]])

    def ln_B(s):
        par = s % 2
        em.op(DVE, _f("reciprocal", out=lnst[:, s, 7:8], in_=lnst[:, s, 6:7]), reads=[r_lnst[s]], writes=[r_lnst[s]])
        em.op(DVE, _f("scalar_tensor_tensor", out=lnst[:, s, 4:5], in0=lnst[:, s, 3:4], scalar=-1.0 / D, in1=lnst[:, s, 7:8],
                      op0=ALU.mult, op1=ALU.mult), reads=[r_lnst[s]], writes=[r_lnst[s]])
        em.op(ACT, _f("activation", out=zbuf[par][:, :], in_=x_res[:, s, :], func=AF.Identity, bias=lnst[:, s, 4:5], scale=lnst[:, s, 7:8]),
              reads=[r_xres[s], r_lnst[s]], writes=[r_z[par]])

    def ln_C(s, want_cast):
        par = s % 2
        em.op(DVE, _f("tensor_tensor", out=zbuf[par][:, :], in0=zbuf[par][:, :], in1=lnp[:, 0:D], op=ALU.mult),
              reads=[r_z[par], r_lnp], writes=[r_z[par]])
        if want_cast:
            em.op(DVE, _f("tensor_tensor", out=xb[par][:, :], in0=zbuf[par][:, :], in1=lnp[:, D:2 * D], op=ALU.add),
                  reads=[r_z[par], r_lnp], writes=[r_xb[par]])
        em.op(POOL, _f("tensor_tensor", out=x_res[:, s, :], in0=zbuf[par][:, :], in1=lnp[:, D:2 * D], op=ALU.add),
              reads=[r_z[par], r_lnp], writes=[r_xres[s]])

    def ln_pipeline(NSUB, emit_group, tbank, want_T):
        for step in range(NSUB + 3):
            if step < NSUB:
                emit_group(step)
                ln_A(step)
            if 0 <= step - 1 < NSUB:
                ln_B(step - 1)
            if 0 <= step - 2 < NSUB:
                ln_C(step - 2, want_T)
            if want_T and 0 <= step - 3 < NSUB - 1:
                emit_T(step - 3, tbank(step - 3))
        if want_T:
            pendT.append(NSUB - 1)

    epsb = sb("epsb", [128, 1], F32)
    EPS_AP = epsb
    em.op(POOL, _f("memset", epsb[:], EPS), writes=[r_const])

    def residual_add(s, half, bk):
        em.op(DVE, _f("scalar_tensor_tensor", out=x_res[:, s, half * 512:(half + 1) * 512],
                      in0=x_res[:, s, half * 512:(half + 1) * 512], scalar=ALPHA,
                      in1=banks[bk][:, :], op0=ALU.mult, op1=ALU.add, accum_out=lnst[:, s, half:half + 1]),
              reads=[r_xres[s], bres[bk]], writes=[r_xres[s], r_lnst[s]])

    def ffn(l, w, tile, ln_idx, last):
        NT, NSUB = tile["NT"], tile["NSUB"]
        load_lnp(l, ln_idx)
        rot_pool[0] = 8
        split = (NSUB == 4 and len(pendT) > 0)
        if not split:
            flush_T()
        deferred = []

        def gu_mms(G, U, i, bg_, bu_, t0, t1):
            fns = [mm(banks[bg_][:, t0:t1], G[:, kc * 256 + i * 128:kc * 256 + (i + 1) * 128], xT[:, kc, t0:t1], kc == 0, kc == 7)
                   for kc in range(8)]
            fns += [mm(banks[bu_][:, t0:t1], U[:, kc * 256 + i * 128:kc * 256 + (i + 1) * 128], xT[:, kc, t0:t1], kc == 0, kc == 7)
                    for kc in range(8)]
            return fns

        def gu_evac(c, bg_, bu_):
            pp_tick(2)
            par = c % 2
            em.op(ACT, _f("activation", out=silu_b[par][:, 0:NT], in_=banks[bg_][:, 0:NT], func=AF.Silu),
                  reads=[bres[bg_]], writes=[r_silu[par]])
            em.op(DVE, _f("scalar_tensor_tensor", out=hT[:, c, 0:NT], in0=silu_b[par][:, 0:NT], scalar=0.5,
                          in1=banks[bu_][:, 0:NT], op0=ALU.mult, op1=ALU.mult),
                  reads=[r_silu[par], bres[bu_]], writes=[r_hT[c]])

        for j in range(11):
            G, rG = rg.next(l, ("G", w, j))
            U, rU = rg.next(l, ("U", w, j))
            for i in range(2):
                c = 2 * j + i
                bg_, bu_ = rot(2)
                if split and c < 3:
                    em.group(gu_mms(G, U, i, bg_, bu_, 0, 384), reads=[rG, rU] + r_xT[0:3], writes=[bres[bg_], bres[bu_]])
                    deferred.append((c, G, U, rG, rU, i, bg_, bu_))
                    if c == 2:
                        flush_T()
                        for (c2, G2, U2, rG2, rU2, i2, bg2, bu2) in deferred:
                            em.group(gu_mms(G2, U2, i2, bg2, bu2, 384, 512), reads=[rG2, rU2, r_xT[3]], writes=[bres[bg2], bres[bu2]])
                            gu_evac(c2, bg2, bu2)
                else:
                    em.group(gu_mms(G, U, i, bg_, bu_, 0, NT), reads=[rG, rU] + r_xT[:NSUB], writes=[bres[bg_], bres[bu_]])
                    gu_evac(c, bg_, bu_)
            if split and j == 0:
                pass
            elif split and j == 1:
                rg.close(4)
            else:
                rg.close(2)
        rot_pool[0] = 4
        for k in range(6):
            Dk, rD = rg.next(l, ("D", w, 0, k))
            nk = 4 if k < 5 else 2
            fns = []
            for kk in range(nk):
                kc = 4 * k + kk
                for s in range(NSUB):
                    fns.append(mm(banks[4 + s][:, :], hT[:, kc, s * 128:(s + 1) * 128], Dk[:, kk * 512:(kk + 1) * 512], kc == 0, kc == 21))
            em.group(fns, reads=[rD] + r_hT[4 * k:4 * k + nk], writes=[bres[4 + s] for s in range(NSUB)])
            rg.close(1)
        Dh = [rg.next(l, ("D", w, 1, k)) for k in range(6)]
        def grp(s):
            fns = []
            for kc in range(22):
                Dk = Dh[kc // 4][0]
                kk = kc % 4
                fns.append(mm(banks[s][:, :], hT[:, kc, s * 128:(s + 1) * 128], Dk[:, kk * 512:(kk + 1) * 512], kc == 0, kc == 21))
            em.group(fns, reads=[d[1] for d in Dh] + r_hT, writes=[bres[s]])
            if s == NSUB - 1:
                rg.close(6)
            residual_add(s, 0, 4 + s)
            residual_add(s, 1, s)

        ln_pipeline(NSUB, grp, lambda s2: 4 + s2, want_T=not last)

    def lat_transposes(s, with_q):
        (bk,) = rot(1)
        pb = banks[bk][:].bitcast(BF16)
        fns = []
        reads = [r_ckvb[s], r_krdb[s], r_const]
        if with_q:
            reads.append(r_qnb[s])
            for c in range(3):
                fns.append(tr(pb[:, c * 128:(c + 1) * 128], qn_b[:, s, c * 128:(c + 1) * 128], identb[:]))
        for c in range(2):
            fns.append(tr(pb[:, 384 + c * 128:384 + (c + 1) * 128], ckv_b[:, s, c * 128:(c + 1) * 128], identb[:]))
        fns.append(tr(pb[:, 640:768], krd_b[:, s, :], identb[:]))
        em.group(fns, reads=reads, writes=[bres[bk]])
        if with_q:
            evac_copy(latT[:, :, s * 128:(s + 1) * 128], pb[:, 0:768].rearrange("p (a b) -> p a b", b=128), [bres[bk]], [r_latT[s]])
        else:
            evac_copy(latT[:, 3:6, s * 128:(s + 1) * 128], pb[:, 384:768].rearrange("p (a b) -> p a b", b=128), [bres[bk]], [r_latT[s]])

    def cast_lat(s):
        em.op(ACT, _f("activation", out=ckv_b[:, s, :], in_=ckv_f[:, s, :], func=AF.Copy), reads=[r_ckvf], writes=[r_ckvb[s]])
        em.op(ACT, _f("activation", out=krd_b[:, s, 0:64], in_=kr_f[:, s, :], func=AF.Copy), reads=[r_krf], writes=[r_krdb[s]])
        em.op(ACT, _f("activation", out=krd_b[:, s, 64:128], in_=kr_f[:, s, :], func=AF.Copy), reads=[r_krf], writes=[r_krdb[s]])

    def kv_expand(l, NT, NSUB, vblocks, store):
        KV0, rK0 = rg.next(l, ("KV", 0))
        KV1, rK1 = rg.next(l, ("KV", 1))
        for h in range(NH):
            (bk,) = rot(1)
            em.group([mm(banks[bk][:, 0:NT], KV0[:, kc * 1024 + h * 128:kc * 1024 + (h + 1) * 128], latT[:, 3 + kc, 0:NT], kc == 0, kc == 1)
                      for kc in range(2)], reads=[rK0] + r_latT[:NSUB], writes=[bres[bk]])
            evac_copy(KT_own[:, h, 0:NT], banks[bk][:, 0:NT], [bres[bk]], [r_KT[h]])
        for vb, (c0, nk) in enumerate(vblocks):
            b0, b1 = rot(2)
            fns = []
            for half, bk in enumerate((b0, b1)):
                for kc in range(2):
                    fns.append(mm(banks[bk][0:nk, :], latT[:, 3 + kc, c0:c0 + nk], KV1[:, kc * 1024 + half * 512:kc * 1024 + (half + 1) * 512], kc == 0, kc == 1))
            em.group(fns, reads=[rK1] + r_latT[:NSUB], writes=[bres[b0], bres[b1]])
            for half, bk in enumerate((b0, b1)):
                evac_copy(V_own[0:nk, vb, half * 512:(half + 1) * 512], banks[bk][0:nk, :], [bres[bk]], [r_V[vb]])
        if store is not None:
            slot, t0 = store
            Ks, Vs, Rs, si = scr(slot)
            key = (l, slot)
            rk = r_scrK.setdefault(key, [Res("scrK"), Res("scrV"), Res("scrR")])
            em.dma(POOL, s_stK, [(Ks[l, si, :, :, t0:t0 + NT].rearrange("h d t -> d h t"), KT_own[:, :, 0:NT], {})],
                   reads=r_KT, writes=[rk[0]])
            em.dma(POOL, s_stV, [(Vs[l, si, t0:t0 + NT, :].rearrange("(s p) c -> p s c", p=128), V_own[:, 0:NSUB, :], {})],
                   reads=r_V[:NSUB], writes=[rk[1]])
            em.dma(POOL, s_stR, [(Rs[l, si, :, t0:t0 + NT], latT[:, 5, 0:NT], {})], reads=r_latT[:NSUB], writes=[rk[2]])

    hcount = [0]

    def load_past(l, slot, h, n_past, buf):
        Ks, Vs, Rs, si = scr(slot)
        rk = r_scrK[(l, slot)]
        nkb = n_past // 128
        em.dma(SP, s_kvp[buf],
               [(KpT[buf][:, 0:n_past], Ks[l, si, h, :, 0:n_past], {}),
                (Vp[buf][:, 0:nkb, :], Vs[l, si, 0:n_past, h * 128:(h + 1) * 128].rearrange("(k p) v -> p k v", p=128), {})],
               reads=[rk[0], rk[1]], writes=[r_KpT[buf]])

    def attention_prefetch(l, jobs):
        pairs = [(ji, h) for ji in range(len(jobs)) for h in range(NH)]

        def prefetch(idx):
            if idx < len(pairs):
                ji, h = pairs[idx]
                jb = jobs[ji]
                if jb["n_past"] > 0:
                    if idx == 0:
                        load_krp(jb)
                    load_past(l, jb["slot"], h, jb["n_past"], idx % 2)

        def load_krp(jb):
            Ks, Vs, Rs, si = scr(jb["slot"])
            em.dma(SP, s_krp, [(krT_past[:, 0:jb["n_past"]], Rs[l, si, :, 0:jb["n_past"]], {})],
                   reads=[r_scrK[(l, jb["slot"])][2]], writes=[r_krp])

        prefetch(0)
        prefetch(1)
        return pairs, prefetch, load_krp

    def attention(l, jobs, pre):
        pairs, prefetch, load_krp = pre
        for idx, (ji, h) in enumerate(pairs):
            jb = jobs[ji]
            q0, N, n_past = jb["q0"], jb["N"], jb["n_past"]
            if h == 0 and ji > 0 and n_past > 0:
                load_krp(jb)
            buf = idx % 2
            hp = h % 2
            rows = slice(hp * 64, hp * 64 + 64)
            blocks = []
            for kb in range(n_past // 128):
                blocks.append(dict(K=KpT[buf][:, kb * 128:(kb + 1) * 128], R=krT_past[:, kb * 128:(kb + 1) * 128],
                                   V=Vp[buf][:, kb, :], nk=128, q0=q0, N=N, masked=False,
                                   reads=[r_KpT[buf], r_krp]))
            for (kc0, nk, vb, qb0, masked) in jb["own"]:
                blocks.append(dict(K=KT_own[:, h, kc0:kc0 + nk], R=latT[:, 5, kc0:kc0 + nk],
                                   V=V_own[0:nk, vb, h * 128:(h + 1) * 128], nk=nk, q0=qb0, N=q0 + N - qb0, masked=masked,
                                   reads=[r_KT[h], r_V[vb]] + r_latT))
            nb = len(blocks)
            Ob = 4 + (hcount[0] % 2)
            Lb = 6 + (hcount[0] % 2)
            hcount[0] += 1
            sbank = {}

            def QK(i):
                bl = blocks[i]
                (bk,) = rot(1)
                sbank[i] = bk
                nk, bq0, bN = bl["nk"], bl["q0"], bl["N"]
                em.group([mm(banks[bk][0:nk, 0:bN], bl["K"], QnT[:, h, bq0:bq0 + bN], True, False),
                          mm(banks[bk][0:nk, 0:bN], bl["R"], QrT[:, h, bq0:bq0 + bN], False, True)],
                         reads=bl["reads"] + [r_QnT[h], r_QrT[h]], writes=[bres[bk]])

            def EXP(i):
                bl = blocks[i]
                bk = sbank[i]
                p = i % 4
                nk, bN = bl["nk"], bl["N"]
                if not bl["masked"]:
                    em.op(ACT, _f("activation", out=PT[p][0:nk, 0:bN], in_=banks[bk][0:nk, 0:bN], func=AF.Exp),
                          reads=[bres[bk]], writes=[r_PT[p]])
                else:
                    em.op(ACT, _f("activation", out=PT[p][0:128, 64:bN], in_=banks[bk][0:128, 64:bN], func=AF.Exp),
                          reads=[bres[bk]], writes=[r_PT[p]])
                    em.op(ACT, _f("activation", out=PT[p][0:64, 0:64], in_=banks[bk][0:64, 0:64], func=AF.Exp),
                          reads=[bres[bk]], writes=[r_PT[p]])
                    em.op(POOL, _f("memset", PT[p][64:128, 0:64], 0.0), reads=[r_PT[p]], writes=[r_PT[p]])

            def PV(i):
                bl = blocks[i]
                p = i % 4
                nk, bq0, bN = bl["nk"], bl["q0"], bl["N"]
                em.group([mm(banks[Ob][:, bq0:bq0 + bN], bl["V"], PT[p][0:nk, 0:bN], i == 0, i == nb - 1)],
                         reads=bl["reads"] + [r_PT[p]], writes=[bres[Ob]])
                a = 1 if i % 4 == 3 else 0
                em.op(POOL if a else DVE, _f("tensor_tensor", out=Lacc[a][0:nk, bq0:bq0 + bN], in0=Lacc[a][0:nk, bq0:bq0 + bN],
                                             in1=PT[p][0:nk, 0:bN], op=ALU.add), reads=[r_PT[p], r_Lacc[a]], writes=[r_Lacc[a]])

            for a in range(2):
                em.op(POOL, _f("memset", Lacc[a][:, :], 0.0), writes=[r_Lacc[a]])
            QK(0)
            if nb > 1:
                QK(1)
            for i in range(nb):
                if i + 2 < nb:
                    QK(i + 2)
                EXP(i)
                PV(i)
            prefetch(idx + 2)
            em.group([mm(banks[Lb][:, q0:q0 + N], onesf[:, :], Lacc[0][:, q0:q0 + N], True, False),
                      mm(banks[Lb][:, q0:q0 + N], onesf[:, :], Lacc[1][:, q0:q0 + N], False, True)],
                     reads=r_Lacc + [r_const], writes=[bres[Lb]])
            em.op(DVE, _f("reciprocal", out=recip[:, q0:q0 + N], in_=banks[Lb][:, q0:q0 + N]), reads=[bres[Lb]], writes=[r_recip])
            em.op(DVE, _f("tensor_tensor", out=OT[:, h, q0:q0 + N], in0=banks[Ob][:, q0:q0 + N], in1=recip[:, q0:q0 + N], op=ALU.mult),
                  reads=[bres[Ob], r_recip], writes=[r_OT[h]])

    def mixer(l, tile):
        NT, NSUB, kind = tile["NT"], tile["NSUB"], tile["kind"]
        em.set_phase()
        rot_pool[0] = 8
        load_lnp(l, 1)
        if kind == "p":
            jobs = [dict(q0=0, N=512, slot=tile["b"], n_past=tile["t"] * 512,
                         own=[(j * 128, 128, j, j * 128, True) for j in range(4)])]
        else:
            jobs = [dict(q0=i * 64, N=64, slot=npseq + i, n_past=PAST, own=[(i * 64, 64, i, i * 64, False)]) for i in range(2)]
        pre = attention_prefetch(l, jobs)
        Tb = [rg.next(l, ("T", i)) for i in range(3)]
        latb = {}

        def lat_group(s):
            if s in pendT:
                flush_T()
            bA, bB = rot(2)
            latb[s] = (bA, bB)
            fns = []
            for reg, (bk, c0) in enumerate(((bA, 0), (bA, 256), (bB, 0))):
                for kc in range(8):
                    fns.append(mm(banks[bk][:, c0:c0 + 256], xT[:, kc, s * 128:(s + 1) * 128], Tb[reg][0][:, kc * 256:(kc + 1) * 256], kc == 0, kc == 7))
            em.group(fns, reads=[t[1] for t in Tb] + [r_xT[s]], writes=[bres[bA], bres[bB]])

        for s in range(min(3, NSUB)):
            lat_group(s)
        for s in range(NSUB):
            if s + 3 < NSUB:
                pass
            bA, bB = latb[s]
            par = s % 2
            em.op(ACT, _f("activation", out=junk[:, 0:QL], in_=banks[bA][:, 0:QL], func=AF.Square, scale=float(QL ** -0.5),
                                                              accum_out=ssq[:, par, 0:1]), reads=[bres[bA]], writes=[r_junk, r_ssq[par]])
            em.op(ACT, _f("activation", out=junk[:, 0:KVL], in_=banks[bB][:, 0:KVL], func=AF.Square, scale=float(KVL ** -0.5),
                                                              accum_out=ssq[:, par, 1:2]), reads=[bres[bB]], writes=[r_junk, r_ssq[par]])
            em.op(ACT, _f("activation", out=ssq[:, par, 2:4], in_=ssq[:, par, 0:2], func=AF.Sqrt, bias=EPS_AP[:, 0:1], scale=1.0),
                  reads=[r_ssq[par], r_const], writes=[r_ssq[par]])
            em.op(DVE, _f("reciprocal", out=ssq[:, par, 0:2], in_=ssq[:, par, 2:4]), reads=[r_ssq[par]], writes=[r_ssq[par]])
            em.op(DVE, _f("scalar_tensor_tensor", out=qn_b[:, s, :], in0=banks[bA][:, 0:QL], scalar=ssq[:, par, 0:1],
                                                                             in1=qg[:, :], op0=ALU.mult, op1=ALU.mult),
                  reads=[bres[bA], r_ssq[par], r_par], writes=[r_qnb[s]])
            em.op(DVE, _f("scalar_tensor_tensor", out=ckv_f[:, s, :], in0=banks[bB][:, 0:KVL], scalar=ssq[:, par, 1:2],
                                                                             in1=kvg[:, :], op0=ALU.mult, op1=ALU.mult),
                  reads=[bres[bB], r_ssq[par], r_par], writes=[r_ckvf])
            em.op(DVE, _f("tensor_tensor", out=rtmp[:, par, 0, :], in0=banks[bA][:, 384:448], in1=cosT[:, s, :], op=ALU.mult),
                  reads=[bres[bA], r_tbl], writes=[r_rtmp[par]])
            em.op(DVE, _f("tensor_tensor", out=rtmp[:, par, 1, :], in0=banks[bA][:, 448:512], in1=sinT[:, s, :], op=ALU.mult),
                  reads=[bres[bA], r_tbl, r_rtmp[par]], writes=[r_rtmp[par]])
            em.op(POOL, _f("tensor_tensor", out=kr_f[:, s, :], in0=rtmp[:, par, 0, :], in1=rtmp[:, par, 1, :], op=ALU.add),
                  reads=[r_rtmp[par]], writes=[r_krf])
            cast_lat(s)
            lat_transposes(s, with_q=True)
            if s + 3 < NSUB:
                lat_group(s + 3)
        flush_T()
        rg.close(3)
        if kind == "p":
            b, t0 = tile["b"], tile["t"] * 512
            em.dma(POOL, s_stlat, [(plat[l, b, t0:t0 + NT, :].rearrange("(s p) c -> p s c", p=128), ckv_f[:, 0:NSUB, :], {})],
                   reads=[r_ckvf])
            em.dma(POOL, s_stkr, [(pkr[l, b, t0:t0 + NT, :].rearrange("(s p) c -> p s c", p=128), kr_f[:, 0:NSUB, :], {})],
                   reads=[r_krf])
        else:
            em.dma(POOL, s_stlat, [(slat[l, :, :], ckv_f[:, 0, :], {})], reads=[r_ckvf])
            em.dma(POOL, s_stkr, [(skr[l, :, :], kr_f[:, 0, :], {})], reads=[r_krf])
        Qb = [rg.next(l, ("Q", i)) for i in range(4)]
        for h in range(NH):
            (bk,) = rot(1)
            Qw, rQ = Qb[h // 4]
            em.group([mm(banks[bk][:, 0:NT], Qw[:, kc * 512 + (h % 4) * 128:kc * 512 + (h % 4 + 1) * 128], latT[:, kc, 0:NT], kc == 0, kc == 2)
                      for kc in range(3)], reads=[rQ] + r_latT[:NSUB], writes=[bres[bk]])
            evac_copy(QnT[:, h, 0:NT], banks[bk][:, 0:NT], [bres[bk]], [r_QnT[h]], scale=ATTN_SCALE)
        for pr in range(4):
            bA, bB = rot(2)
            em.group([mm(banks[bA][:, 0:NT], Qb[2][0][:, kc * 512 + pr * 128:kc * 512 + (pr + 1) * 128], latT[:, kc, 0:NT], kc == 0, kc == 2) for kc in range(3)]
                     + [mm(banks[bB][:, 0:NT], Qb[3][0][:, kc * 512 + pr * 128:kc * 512 + (pr + 1) * 128], latT[:, kc, 0:NT], kc == 0, kc == 2) for kc in range(3)],
                     reads=[Qb[2][1], Qb[3][1]] + r_latT[:NSUB], writes=[bres[bA], bres[bB]])
            em.op(DVE, _f("tensor_tensor", out=ropet[0][:, 0:NT], in0=banks[bA][:, 0:NT], in1=cosF[:, 0:NT], op=ALU.mult),
                  reads=[bres[bA], r_tbl], writes=[r_ropet[0]])
            em.op(DVE, _f("tensor_tensor", out=ropet[1][:, 0:NT], in0=banks[bB][:, 0:NT], in1=sinF[:, 0:NT], op=ALU.mult),
                  reads=[bres[bB], r_tbl], writes=[r_ropet[1]])
            for hh in range(2):
                hq = 2 * pr + hh
                lo, zo = hh * 64, (1 - hh) * 64
                em.op(POOL, _f("tensor_tensor", out=QrT[lo:lo + 64, hq, 0:NT], in0=ropet[0][lo:lo + 64, 0:NT], in1=ropet[1][lo:lo + 64, 0:NT], op=ALU.add),
                      reads=r_ropet, writes=[r_QrT[hq]])
                em.op(POOL, _f("memset", QrT[zo:zo + 64, hq, 0:NT], 0.0), writes=[r_QrT[hq]])
        rg.close(4)
        if kind == "p":
            vblocks = [(j * 128, 128) for j in range(4)]
            kv_expand(l, NT, NSUB, vblocks, store=(tile["b"], tile["t"] * 512))
        else:
            vblocks = [(0, 64), (64, 64)]
            kv_expand(l, NT, NSUB, vblocks, store=None)
        rg.close(2)
        rot_pool[0] = 4
        attention(l, jobs, pre)

        em.set_phase()
        rot_pool[0] = 8
        if kind == "p":
            segs = [(0, NT)]
        else:
            segs = [(0, 64), (64, 64)]
        for j in range(4):
            CC, rCC = rg.next(l, ("CC", j))
            CU, rCU = rg.next(l, ("CU", j))
            CB, rCB = rg.next(l, ("CB", j))
            for i in range(2):
                c = 2 * j + i
                bC, bU = rot(2)
                (bB,) = rot(1)
                em.group([mm(banks[bC][:, 0:NT], CC[:, kc * 256 + i * 128:kc * 256 + (i + 1) * 128], xT[:, kc, 0:NT], kc == 0, kc == 7) for kc in range(8)]
                         + [mm(banks[bU][:, 0:NT], CU[:, kc * 256 + i * 128:kc * 256 + (i + 1) * 128], xT[:, kc, 0:NT], kc == 0, kc == 7) for kc in range(8)]
                         + [mm(banks[bB][:, 0:NT], CB[:, kc * 256 + i * 128:kc * 256 + (i + 1) * 128], xT[:, kc, 0:NT], kc == 0, kc == 7) for kc in range(8)],
                         reads=[rCC, rCU, rCB] + r_xT[:NSUB], writes=[bres[bC], bres[bU], bres[bB]])
                par = c % 2
                pp_tick(1)
                em.op(ACT, _f("activation", out=C_sb[par][:, 0:NT], in_=banks[bC][:, 0:NT], func=AF.Copy),
                      reads=[bres[bC]], writes=[r_Csb[par]])
                up = upb[par]
                for si, (c0, L) in enumerate(segs):
                    o = c0 + 2 * si
                    em.op(DVE, _f("tensor_tensor", out=up[:, o + 2:o + 2 + L], in0=banks[bU][:, c0:c0 + L],
                                                                                                in1=C_sb[par][:, c0:c0 + L], op=ALU.mult),
                          reads=[bres[bU], r_Csb[par]], writes=[r_up[par]])
                    if kind == "p":
                        em.op(POOL, _f("tensor_copy", out=up[:, o:o + 2], in_=carry[:, l, :, c]),
                              reads=[r_carry[l], r_up[par]], writes=[r_up[par]])
                    else:
                        em.op(POOL, _f("tensor_copy", out=up[:, o:o + 2], in_=scarry[:, l, si, :, c]),
                              reads=[r_scarry[l], r_up[par]], writes=[r_up[par]])
                    acc = accb[par]
                    em.op(ACT, _f("activation", out=acc[:, c0:c0 + L], in_=up[:, o + 2:o + 2 + L], func=AF.Copy, scale=cwt[:, 16 + c:17 + c]),
                          reads=[r_up[par], r_par], writes=[r_acc[par]])
                    for k in (1, 0):
                        em.op(DVE, _f("scalar_tensor_tensor",
                            out=acc[:, c0:c0 + L], in0=up[:, o + k:o + k + L], scalar=cwt[:, 8 * k + c:8 * k + c + 1], in1=acc[:, c0:c0 + L],
                            op0=ALU.mult, op1=ALU.add), reads=[r_up[par], r_par, r_acc[par]], writes=[r_acc[par]])
                    if kind == "p":
                        em.op(POOL, _f("tensor_copy", out=carry[:, l, :, c], in_=up[:, o + L:o + L + 2]),
                              reads=[r_up[par], r_carry[l]], writes=[r_carry[l]])
                    else:
                        em.op(POOL, _f("tensor_copy", out=scarry[:, l, si, :, c], in_=up[:, o + L:o + L + 2]),
                              reads=[r_up[par], r_scarry[l]], writes=[r_scarry[l]])
                em.op(DVE, _f("tensor_tensor", out=yc_in[:, c, 0:NT], in0=banks[bB][:, 0:NT], in1=accb[par][:, 0:NT], op=ALU.mult),
                      reads=[bres[bB], r_acc[par]], writes=[r_ycin[c]])
            rg.close(3)
        for j in range(4):
            MO, rMO = rg.next(l, ("MO", j))
            GM, rGM = rg.next(l, ("GM", j))
            for i in range(2):
                c = 2 * j + i
                by, bg_ = rot(2)
                em.group([mm(banks[by][:, 0:NT], MO[:, kc * 256 + i * 128:kc * 256 + (i + 1) * 128], OT[:, kc, 0:NT], kc == 0, kc == 7) for kc in range(8)]
                         + [mm(banks[bg_][:, 0:NT], GM[:, kc * 256 + i * 128:kc * 256 + (i + 1) * 128], xT[:, kc, 0:NT], kc == 0, kc == 7) for kc in range(8)],
                         reads=[rMO, rGM] + r_OT + r_xT[:NSUB], writes=[bres[by], bres[bg_]])
                par = c % 2
                em.op(ACT, _f("activation", out=gate[par][:, 0:NT], in_=banks[bg_][:, 0:NT], func=AF.Sigmoid,
                                                                         bias=bgt[:, 8 + c:9 + c], scale=1.0),
                      reads=[bres[bg_], r_par], writes=[r_gate[par]])
                em.op(DVE, _f("tensor_tensor", out=gmy[:, c, 0:NT], in0=banks[by][:, 0:NT], in1=gate[par][:, 0:NT], op=ALU.mult),
                      reads=[bres[by], r_gate[par]], writes=[r_gmy[c]])
            rg.close(2)
        conv_out = []
        if kind == "p" and tile["t"] == ntile - 1:
            conv_out.append((carry[:, l, :, :].rearrange("p r c -> p (r c)"), pconv[l, tile["b"]], r_carry[l]))
        if kind == "s":
            for i in range(2):
                conv_out.append((scarry[:, l, i, :, :].rearrange("p r c -> p (r c)"), sconv_o[l, i], r_scarry[l]))
        for src, dst, rr in conv_out:
            (bk,) = rot(1)
            em.group([tr(banks[bk][0:16, 0:128], src, identf[:])], reads=[rr, r_const], writes=[bres[bk]])
            em.op(ACT, _f("activation", out=cst[:, :], in_=banks[bk][0:16, 0:128], func=AF.Copy), reads=[bres[bk]], writes=[r_cst])
            em.dma(POOL, s_stc, [(dst.rearrange("r (c p) -> (r c) p", p=128), cst[:, :], {})], reads=[r_cst])
        for j in range(4):
            CO, rCO = rg.next(l, ("CO", j))
            GC, rGC = rg.next(l, ("GC", j))
            for i in range(2):
                c = 2 * j + i
                by, bg_ = rot(2)
                em.group([mm(banks[by][:, 0:NT], CO[:, kc * 256 + i * 128:kc * 256 + (i + 1) * 128], yc_in[:, kc, 0:NT], kc == 0, kc == 7) for kc in range(8)]
                         + [mm(banks[bg_][:, 0:NT], GC[:, kc * 256 + i * 128:kc * 256 + (i + 1) * 128], xT[:, kc, 0:NT], kc == 0, kc == 7) for kc in range(8)],
                         reads=[rCO, rGC] + r_ycin + r_xT[:NSUB], writes=[bres[by], bres[bg_]])
                par = c % 2
                em.op(ACT, _f("activation", out=gate[par][:, 0:NT], in_=banks[bg_][:, 0:NT], func=AF.Sigmoid,
                                                                         bias=bgt[:, c:c + 1], scale=1.0),
                      reads=[bres[bg_], r_par], writes=[r_gate[par]])
                em.op(DVE, _f("tensor_tensor", out=ttb[par][:, 0:NT], in0=banks[by][:, 0:NT], in1=gate[par][:, 0:NT], op=ALU.mult),
                      reads=[bres[by], r_gate[par]], writes=[r_tt[par]])
                em.op(POOL, _f("tensor_tensor", out=mergedT[:, c, 0:NT], in0=ttb[par][:, 0:NT], in1=gmy[:, c, 0:NT], op=ALU.add),
                      reads=[r_tt[par], r_gmy[c]], writes=[r_mT[c]])
            rg.close(2)
        MX = [rg.next(l, ("MX", k)) for k in range(4)]
        mxb = {}

        def grp(s):
            b0, b1 = rot(2)
            mxb[s] = b0
            fns = []
            for half, bk in enumerate((b0, b1)):
                for kc in range(8):
                    fns.append(mm(banks[bk][:, :], mergedT[:, kc, s * 128:(s + 1) * 128], MX[kc // 2][0][:, (kc % 2) * 1024 + half * 512:(kc % 2) * 1024 + (half + 1) * 512],
                                  kc == 0, kc == 7))
            em.group(fns, reads=[m[1] for m in MX] + r_mT, writes=[bres[b0], bres[b1]])
            if s == NSUB - 1:
                rg.close(4)
            residual_add(s, 0, b0)
            residual_add(s, 1, b1)

        ln_pipeline(NSUB, grp, lambda s2: mxb[s2], want_T=True)
        em.set_phase()

    def load_params(l):
        em.dma(SP, s_par, [(qg[:, :], qg_d[l:l + 1, :].partition_broadcast(128), {}),
                           (kvg[:, :], kvg_d[l:l + 1, :].partition_broadcast(128), {}),
                           (bgt[:, :], bg_d[l, :, :], {}),
                           (cwt[:, :], cw_d[l, :, :], {})], writes=[r_par])

    def run_tile(tile):
        NT, NSUB = tile["NT"], tile["NSUB"]
        tab = tile["tab"]
        if tile["kind"] == "p":
            src = xp[tile["b"], tile["t"] * 512:(tile["t"] + 1) * 512, :].rearrange("(s p) c -> p s c", p=128)
            em.dma(SP, s_xld, [(x_res[:, :, :], src, {})], writes=r_xres)
        else:
            em.dma(SP, s_xld, [(x_res[:, 0, :], xs[:, :], {})], writes=[r_xres[0]])
        em.dma(SP, s_tbl, [(cosT[:, :, :], cosT_d[tab].rearrange("p (s c) -> p s c", c=64), {}),
                           (sinT[:, :, :], sinT_d[tab].rearrange("p (s c) -> p s c", c=64), {}),
                           (cosF[:, :], cosF_d[tab], {}), (sinF[:, :], sinF_d[tab], {})], writes=[r_tbl])
        if tile["kind"] == "p" and tile["t"] == 0:
            em.op(POOL, _f("memset", carry[:], 0.0), writes=r_carry)
        em.set_phase()
        for s in range(NSUB):
            make_xT(s)
        for l in range(depth):
            load_params(l)
            ffn(l, 1, tile, 0, last=False)
            mixer(l, tile)
            ffn(l, 2, tile, 2, last=(l == depth - 1))
        if tile["kind"] == "p":
            dst = yp[tile["b"], tile["t"] * 512:(tile["t"] + 1) * 512, :].rearrange("(s p) c -> p s c", p=128)
            em.dma(POOL, s_sty, [(dst, x_res[:, :, :], {})], reads=r_xres)
        else:
            em.dma(POOL, s_sty, [(ys[:, :], x_res[:, 0, :], {})], reads=[r_xres[0]])

    em.set_phase()
    for ti, tile in enumerate(tiles):
        run_tile(tile)
        if ti == 0:
            pp_drain(depth)
            em.wait_all(SP, s_dout + s_din)

    if sample:
        em.set_phase()
        em.dma(SP, s_scar, [(scarry[:].rearrange("p l i r c -> p l (i r c)"), sconv.rearrange("l p f -> p l f"), {})], writes=r_scarry)
        for l in range(depth):
            first = True
            for i in range(2):
                for pt in range(2):
                    t0 = pt * 512
                    em.dma(SP, s_cache[0], [(ckv_f[:, :, :], clat[l, i, t0:t0 + 512, :].rearrange("(s p) c -> p s c", p=128), {})], writes=[r_ckvf])
                    em.dma(SP, s_cache[1], [(kr_f[:, :, :], ckr[l, i, t0:t0 + 512, :].rearrange("(s p) c -> p s c", p=128), {})], writes=[r_krf])
                    for s in range(4):
                        cast_lat(s)
                        lat_transposes(s, with_q=False)
                    if not first:
                        rg.consumed -= 2
                    kv_expand(l, 512, 4, [(j * 128, 128) for j in range(4)], store=(npseq + i, t0))
                    first = False
            rg.close(2)
        run_tile(dict(kind="s", NT=128, NSUB=1, tab=ntile))

    em.wait_all(SP, [s_sty, s_stlat, s_stkr, s_stc, s_stK, s_stV, s_stR])

    for s_ in em.sems:
        s_.h = es.enter_context(nc.semaphore(s_.name))
    with nc.Block() as block:
        @block.sync
        def _(e):
            em.replay(SP, e)

        @block.tensor
        def _(e):
            em.replay(PE, e)

        @block.scalar
        def _(e):
            em.replay(ACT, e)

        @block.vector
        def _(e):
            em.replay(DVE, e)

        @block.gpsimd
        def _(e):
            em.replay(POOL, e)
    es.close()
    return nc


def _rope_tables(ntile, with_sample=True):
    half = RD // 2
    inv = (np.float32(10000.0) ** (-np.arange(half, dtype=np.float32) / np.float32(half))).astype(np.float32)
    ntab = ntile + 1
    cosT = np.zeros((ntab, 128, 4, 64), np.float32)
    sinT = np.zeros((ntab, 128, 4, 64), np.float32)
    cosF = np.zeros((ntab, 128, 512), np.float32)
    sinF = np.zeros((ntab, 128, 512), np.float32)
    k = np.arange(64)
    sign = np.where(k < 32, -1.0, 1.0)
    for t in range(ntab):
        if t < ntile:
            pos = t * 512 + np.arange(512)
        else:
            pos = np.concatenate([PAST + np.arange(64), PAST + np.arange(64), np.zeros(384)])
        ang = (pos.astype(np.float32)[:, None] * inv[None, :]).astype(np.float32).astype(np.float64)
        c = np.cos(ang)[:, k % 32]
        s = np.sin(ang)[:, k % 32] * sign[None, :]
        cosT[t] = c.reshape(4, 128, 64).transpose(1, 0, 2)
        sinT[t] = s.reshape(4, 128, 64).transpose(1, 0, 2)
        cosF[t] = np.concatenate([c.T, c.T], axis=0) * ATTN_SCALE
        sinF[t] = np.concatenate([s.T, s.T], axis=0) * ATTN_SCALE
    return cosT.reshape(ntab, 128, 256), sinT.reshape(ntab, 128, 256), cosF, sinF


def _prep_weights(inp, depth):
    w_in = np.ascontiguousarray(inp["w_in"][:depth])
    kr = w_in[:, :, 640:704]
    kr_sw = np.concatenate([kr[:, :, 32:64], kr[:, :, 0:32]], axis=2)
    wtok = np.ascontiguousarray(np.concatenate([w_in[:, :, 0:384], kr, kr_sw, w_in[:, :, 384:640]], axis=2))
    w_uq = inp["w_uq"][:depth]
    uq = np.zeros((depth, QL, 2048), np.float32)
    swap = (np.arange(64) + 32) % 64
    for h in range(NH):
        uq[:, :, h * 128:(h + 1) * 128] = w_uq[:, :, h * 192:h * 192 + 128]
        pr, hh = h // 2, h % 2
        r = w_uq[:, :, h * 192 + 128:h * 192 + 192]
        uq[:, :, 1024 + pr * 128 + hh * 64:1024 + pr * 128 + hh * 64 + 64] = r
        uq[:, :, 1536 + pr * 128 + hh * 64:1536 + pr * 128 + hh * 64 + 64] = r[:, :, swap]
    w_ukv = inp["w_ukv"][:depth]
    ukv = np.zeros((depth, KVL, 2048), np.float32)
    for h in range(NH):
        ukv[:, :, h * 128:(h + 1) * 128] = w_ukv[:, :, h * 256:h * 256 + 128]
        ukv[:, :, 1024 + h * 128:1024 + (h + 1) * 128] = w_ukv[:, :, h * 256 + 128:h * 256 + 256]
    bg = np.ascontiguousarray(inp["b_gate"][:depth].reshape(depth, 2, 8, 128).transpose(0, 3, 1, 2).reshape(depth, 128, 16))
    cw = np.ascontiguousarray(inp["conv_w"][:depth].reshape(depth, 3, 8, 128).transpose(0, 3, 1, 2).reshape(depth, 128, 24))
    c = np.ascontiguousarray
    return {
        "f1gu": c(inp["ffn1_w_gate_up"][:depth]), "f1d": c(inp["ffn1_w_down"][:depth]),
        "f2gu": c(inp["ffn2_w_gate_up"][:depth]), "f2d": c(inp["ffn2_w_down"][:depth]),
        "win": w_in, "wtok": wtok, "uq": uq, "ukv": ukv,
        "mo": c(inp["w_mla_out"][:depth]), "co": c(inp["w_conv_out"][:depth]), "mx": c(inp["w_mix_out"][:depth]),
        "lng": c(inp["ln_gain"][:depth]), "lnb": c(inp["ln_bias"][:depth]),
        "qg": c(inp["q_norm_gain"][:depth]), "kvg": c(inp["kv_norm_gain"][:depth]), "bg": bg, "cw": cw,
    }


def run(inp, depth=4, npseq=2, ntile=8, sample=True, n_cores=8, trace=False):
    import ml_dtypes
    S = ntile * 512
    nc = build_program(depth, npseq, ntile, sample)
    shared = _prep_weights(inp, depth)
    cT, sT, cF, sF = _rope_tables(ntile)
    shared.update(cosT=cT, sinT=sT, cosF=cF, sinF=sF,
                  identb=np.eye(128, dtype=np.float32).astype(ml_dtypes.bfloat16), identf=np.eye(128, dtype=np.float32))
    in_maps = []
    for c in range(n_cores):
        m = dict(shared)
        m["xp"] = np.ascontiguousarray(inp["x_prompt"][c * npseq:(c + 1) * npseq, :S])
        m["xs"] = np.ascontiguousarray(inp["x_sample"][2 * c:2 * c + 2]).reshape(128, D)
        m["clat"] = np.ascontiguousarray(inp["cache_kv_latent"][:depth, 2 * c:2 * c + 2])
        m["ckr"] = np.ascontiguousarray(inp["cache_k_rope"][:depth, 2 * c:2 * c + 2])
        sc = inp["state_conv"][:depth, 2 * c:2 * c + 2]
        m["sconv"] = np.ascontiguousarray(sc.reshape(depth, 2, 2, 8, 128).transpose(0, 4, 1, 2, 3).reshape(depth, 128, 32))
        in_maps.append(m)
    res = run_bass_kernel_spmd(nc, in_maps, core_ids=list(range(n_cores)), **({"trace": True} if trace else {}))
    R = res.results
    y_p = np.concatenate([r["yp"] for r in R], axis=0)
    y_s = np.concatenate([r["ys"].reshape(2, 64, D) for r in R], axis=0)
    p_lat = np.concatenate([r["plat"] for r in R], axis=1)
    p_kr = np.concatenate([r["pkr"] for r in R], axis=1)
    p_conv = np.concatenate([r["pconv"] for r in R], axis=1)
    s_lat = np.concatenate([r["slat"].reshape(depth, 2, 64, KVL) for r in R], axis=1)
    s_kr = np.concatenate([r["skr"].reshape(depth, 2, 64, RD) for r in R], axis=1)
    s_conv = np.concatenate([r["sconv_o"] for r in R], axis=1)
    return (y_p, y_s, p_lat, p_kr, p_conv, s_lat, s_kr, s_conv), res


def kernel(**inputs):
    inp = {k: np.asarray(v) for k, v in inputs.items()}
    outs, _ = run(inp)
    return tuple(np.ascontiguousarray(o, dtype=np.float32) for o in outs)
```
